# Optimizing a Trainium2 kernel written in Bass

```python
import math
import jax, jax.numpy as jnp
from jax import lax
import numpy as np

D_MODEL = 2048
BATCH = 4
SEQ = 4096
DEPTH = 4

GMLP_WIDTH = D_MODEL // 2
GMLP_GROUPS = 8
GMLP_GROUP_DIM = GMLP_WIDTH // GMLP_GROUPS
CHUNK = 128

ATTN_HEADS = 8
ATTN_WIDTH = D_MODEL // 2
ATTN_V_DIM = ATTN_WIDTH // ATTN_HEADS
ATTN_QK_DIM = ATTN_V_DIM // 2
Q_BLOCK = 128
ROPE_THETA = 10000.0

N_BRANCH = 2
QK_WIDTH = ATTN_HEADS * 2 * ATTN_QK_DIM
SPLITS = (GMLP_WIDTH, 2 * GMLP_WIDTH, 2 * GMLP_WIDTH + QK_WIDTH, 2 * GMLP_WIDTH + 2 * QK_WIDTH,
          2 * GMLP_WIDTH + 2 * QK_WIDTH + ATTN_WIDTH)
D_IN = 2 * GMLP_WIDTH + 2 * QK_WIDTH + ATTN_WIDTH + N_BRANCH * D_MODEL

FFN_HIDDEN = -(-8 * D_MODEL // (3 * 256)) * 256
EPS = 1e-6

kernel_name = "hybrid_gmlp_diffattn_adaln_encoder"


def rms_norm(x, g):
    xf = x.astype(jnp.float32)
    y = xf * lax.rsqrt(jnp.mean(xf * xf, axis=-1, keepdims=True) + EPS)
    return (y * g.astype(jnp.float32)).astype(x.dtype)


def layer_norm(x, g, b):
    xf = x.astype(jnp.float32)
    mu = jnp.mean(xf, axis=-1, keepdims=True)
    var = jnp.mean(jnp.square(xf - mu), axis=-1, keepdims=True)
    y = (xf - mu) * lax.rsqrt(var + EPS)
    return (y * g.astype(jnp.float32) + b.astype(jnp.float32)).astype(x.dtype)


def rope_tables(seq_len, dim):
    pos = jnp.arange(seq_len, dtype=jnp.float32)
    inv_freq = ROPE_THETA ** (-jnp.arange(0, dim, 2, dtype=jnp.float32) / dim)
    ang = pos[:, None] * inv_freq[None, :]
    return jnp.cos(ang), jnp.sin(ang)


def apply_rope(x, cos, sin):
    cos = cos.astype(x.dtype)
    sin = sin.astype(x.dtype)
    x1, x2 = jnp.split(x, 2, axis=-1)
    return jnp.concatenate([x1 * cos - x2 * sin, x2 * cos + x1 * sin], axis=-1)


def gmlp_spatial_gate(u, v, ln_g, ln_b, w_s, b_s):
    bsz, seq = v.shape[0], v.shape[1]
    v = layer_norm(v, ln_g, ln_b)
    vr = v.reshape(bsz, seq // CHUNK, CHUNK, GMLP_GROUPS, GMLP_GROUP_DIM)
    mixed = jnp.einsum('gpq,bnqgd->bnpgd', w_s, vr) + b_s.T[None, None, :, :, None]
    return u * mixed.reshape(bsz, seq, GMLP_WIDTH)


def differential_attention(q, k, v, lam, sub_g, lambda_init):
    bsz, seq = q.shape[0], q.shape[1]
    cos, sin = rope_tables(seq, ATTN_QK_DIM)
    q = apply_rope(q.transpose(0, 2, 3, 1, 4), cos, sin)
    k = apply_rope(k.transpose(0, 2, 3, 1, 4), cos, sin)
    v = v.transpose(0, 2, 1, 3)
    scale = ATTN_QK_DIM ** -0.5
    n_blk = seq // Q_BLOCK
    qb = q.reshape(bsz, ATTN_HEADS, 2, n_blk, Q_BLOCK, ATTN_QK_DIM).transpose(3, 0, 1, 2, 4, 5)

    def block(q_blk):
        s = jnp.einsum('bhmqd,bhmkd->bhmqk', q_blk, k).astype(jnp.float32) * scale
        p = jax.nn.softmax(s, axis=-1)
        p_diff = p[:, :, 0] - lam * p[:, :, 1]
        return jnp.einsum('bhqk,bhkd->bhqd', p_diff.astype(v.dtype), v)

    o = lax.map(block, qb)
    o = o.transpose(1, 0, 3, 2, 4).reshape(bsz, seq, ATTN_HEADS, ATTN_V_DIM)
    o = rms_norm(o, sub_g) * (1.0 - lambda_init)
    return o.reshape(bsz, seq, ATTN_WIDTH)


def setup_inputs(seed: int = 0) -> dict:
    key = jax.random.key(seed)
    ks = jax.random.split(key, 23)
    f32 = jnp.float32
    L, D = DEPTH, D_MODEL

    def nrm(k, shape, scale):
        return jax.random.normal(k, shape, f32) * scale

    return {
        "x": nrm(ks[0], (BATCH, SEQ, D), 1.0),
        "c": nrm(ks[1], (BATCH, D), 1.0),
        "ada_w": nrm(ks[2], (L, D, 6 * D), 0.5 * D ** -0.5),
        "ada_b": nrm(ks[3], (L, 6 * D), 0.02),
        "norm1_g": 1.0 + nrm(ks[4], (L, D), 0.1),
        "norm2_g": 1.0 + nrm(ks[5], (L, D), 0.1),
        "w_in": nrm(ks[6], (L, D, D_IN), D ** -0.5),
        "gmlp_ln_g": 1.0 + nrm(ks[7], (L, GMLP_WIDTH), 0.1),
        "gmlp_ln_b": nrm(ks[8], (L, GMLP_WIDTH), 0.02),
        "w_s": nrm(ks[9], (L, GMLP_GROUPS, CHUNK, CHUNK), CHUNK ** -0.5),
        "b_s": 1.0 + nrm(ks[10], (L, GMLP_GROUPS, CHUNK), 0.1),
        "lambda_q1": nrm(ks[11], (L, ATTN_QK_DIM), 0.1),
        "lambda_k1": nrm(ks[12], (L, ATTN_QK_DIM), 0.1),
        "lambda_q2": nrm(ks[13], (L, ATTN_QK_DIM), 0.1),
        "lambda_k2": nrm(ks[14], (L, ATTN_QK_DIM), 0.1),
        "subln_g": 1.0 + nrm(ks[15], (L, ATTN_V_DIM), 0.1),
        "w_up_gmlp": nrm(ks[16], (L, GMLP_WIDTH, D), GMLP_WIDTH ** -0.5),
        "w_up_attn": nrm(ks[17], (L, ATTN_WIDTH, D), ATTN_WIDTH ** -0.5),
        "w_o": nrm(ks[18], (L, D, D), D ** -0.5),
        "ffn_w1": nrm(ks[19], (L, D, FFN_HIDDEN), D ** -0.5),
        "ffn_w3": nrm(ks[20], (L, D, FFN_HIDDEN), D ** -0.5),
        "ffn_w2": nrm(ks[21], (L, FFN_HIDDEN, D), FFN_HIDDEN ** -0.5),
        "final_g": 1.0 + nrm(ks[22], (D,), 0.1),
    }


def reference(x, c, ada_w, ada_b, norm1_g, norm2_g, w_in, gmlp_ln_g, gmlp_ln_b, w_s, b_s,
              lambda_q1, lambda_k1, lambda_q2, lambda_k2, subln_g, w_up_gmlp, w_up_attn, w_o,
              ffn_w1, ffn_w3, ffn_w2, final_g):
    bsz, seq = x.shape[0], x.shape[1]
    c_act = jax.nn.silu(c)
    for l in range(DEPTH):
        mod = jnp.dot(c_act, ada_w[l]) + ada_b[l]
        sh1, sc1, gt1, sh2, sc2, gt2 = [m[:, None, :] for m in jnp.split(mod, 6, axis=-1)]

        h = rms_norm(x, norm1_g[l]) * (1.0 + sc1) + sh1
        proj = jnp.dot(h, w_in[l])
        u, v, q, k, va, gate_logits = jnp.split(proj, SPLITS, axis=-1)

        branch_a = gmlp_spatial_gate(jax.nn.gelu(u), jax.nn.gelu(v),
                                     gmlp_ln_g[l], gmlp_ln_b[l], w_s[l], b_s[l])

        lambda_init = 0.8 - 0.6 * math.exp(-0.3 * l)
        lam = (jnp.exp(jnp.sum(lambda_q1[l].astype(jnp.float32) * lambda_k1[l].astype(jnp.float32)))
               - jnp.exp(jnp.sum(lambda_q2[l].astype(jnp.float32) * lambda_k2[l].astype(jnp.float32)))
               + lambda_init)
        branch_b = differential_attention(
            q.reshape(bsz, seq, ATTN_HEADS, 2, ATTN_QK_DIM),
            k.reshape(bsz, seq, ATTN_HEADS, 2, ATTN_QK_DIM),
            va.reshape(bsz, seq, ATTN_HEADS, ATTN_V_DIM),
            lam, subln_g[l], lambda_init)

        gates = jax.nn.sigmoid(gate_logits).reshape(bsz, seq, N_BRANCH, D_MODEL)
        merged = (gates[:, :, 0] * jnp.dot(branch_a, w_up_gmlp[l])
                  + gates[:, :, 1] * jnp.dot(branch_b, w_up_attn[l]))
        x = x + gt1 * jnp.dot(merged, w_o[l])

        h = rms_norm(x, norm2_g[l]) * (1.0 + sc2) + sh2
        f = jnp.dot(jax.nn.silu(jnp.dot(h, ffn_w1[l])) * jnp.dot(h, ffn_w3[l]), ffn_w2[l])
        x = x + gt2 * f
    return rms_norm(x, final_g)
```

```python
import math
import numpy as np
import concourse.bass as bass
import concourse.mybir as mybir
from concourse.bass_utils import run_bass_kernel_spmd

F32 = mybir.dt.float32
BF16 = mybir.dt.bfloat16
AF = mybir.ActivationFunctionType
ALU = mybir.AluOpType
AX = mybir.AxisListType.X

D = 2048
S = 4096
SL = 2048
DEPTH = 4
DIN = 9216
FH = 5632
NCORES = 8
RG = [[0, 1], [2, 3], [4, 5], [6, 7]]
EPS = 1e-6
ENG = ["pe", "act", "dve", "pool", "sp"]


class Prog:
    def __init__(self, nc):
        self.nc = nc
        self.streams = {e: [] for e in ENG}
        self.psem = {e: nc.alloc_semaphore(name="prog_" + e) for e in ENG}
        self.cnt = {e: 0 for e in ENG}
        self.known = {e: {} for e in ENG}
        self.dsems = {}

    def _wait(self, eng, tok):
        if tok is None:
            return
        key, handle, val = tok
        if self.known[eng].get(key, 0) >= val:
            return
        self.known[eng][key] = val
        self.streams[eng].append(lambda e, h=handle, v=val: e.wait_ge(h, v))

    def op(self, eng, fn, deps=(), signal=True):
        for d in deps:
            self._wait(eng, d)
        if signal:
            self.cnt[eng] += 1
            h = self.psem[eng]
            self.streams[eng].append(lambda e, fn=fn, h=h: fn(e).then_inc(h, 1))
            return (eng, h, self.cnt[eng])
        self.streams[eng].append(lambda e, fn=fn: fn(e))
        return None

    def dma(self, queue, out, in_, deps=(), sem="d"):
        for d in deps:
            self._wait(queue, d)
        if sem not in self.dsems:
            self.dsems[sem] = [self.nc.alloc_semaphore(name="dma_" + sem), 0]
        s = self.dsems[sem]
        s[1] += 16
        self.streams[queue].append(lambda e, o=out, i=in_, h=s[0]: e.dma_start(out=o, in_=i).then_inc(h, 16))
        return ("dma_" + sem, s[0], s[1])

    def coll(self, in_t, out_t, deps=(), sem="cc"):
        for d in deps:
            self._wait("pool", d)
        if sem not in self.dsems:
            self.dsems[sem] = [self.nc.alloc_semaphore(name="dma_" + sem), 0]
        s = self.dsems[sem]
        s[1] += 1
        self.streams["pool"].append(lambda e, i=in_t, o=out_t, h=s[0]: e.collective_compute(
            "AllGather", ALU.bypass, replica_groups=RG, ins=[i.ap().opt()], outs=[o.ap().opt()]).then_inc(h))
        return ("dma_" + sem, s[0], s[1])

    def dtoks(self, names):
        return [("dma_" + k, self.dsems[k][0], self.dsems[k][1]) for k in names if k in self.dsems and self.dsems[k][1] > 0]

    def barrier(self):
        toks = [(e, self.psem[e], self.cnt[e]) for e in ENG if self.cnt[e] > 0]
        toks += [("dma_" + k, s[0], s[1]) for k, s in self.dsems.items() if s[1] > 0]
        for e in ENG:
            for t in toks:
                self._wait(e, t)

    def mm(self, out, lhsT, rhs, start, stop, deps=(), signal=False):
        return self.op("pe", lambda e: e.matmul(out, lhsT, rhs, start=start, stop=stop, skip_group_check=True),
                       deps, signal)

    def transpose(self, out, in_, ident, deps=(), signal=True):
        return self.op("pe", lambda e: e.transpose(out, in_, ident), deps, signal)

    def act(self, out, in_, func, deps=(), scale=None, bias=None, accum_out=None):
        kw = {}
        if scale is not None:
            kw["scale"] = scale
        if bias is not None:
            kw["bias"] = bias
        if accum_out is not None:
            kw["accum_out"] = accum_out
        return self.op("act", lambda e: e.activation(out=out, in_=in_, func=func, **kw), deps)

    def tt(self, eng, out, in0, in1, op, deps=()):
        return self.op(eng, lambda e: e.tensor_tensor(out=out, in0=in0, in1=in1, op=op), deps)

    def ts(self, eng, out, in0, s1, s2, op0, op1=None, deps=()):
        if op1 is None:
            return self.op(eng, lambda e: e.tensor_scalar(out=out, in0=in0, scalar1=s1, scalar2=None, op0=op0), deps)
        return self.op(eng, lambda e: e.tensor_scalar(out=out, in0=in0, scalar1=s1, scalar2=s2, op0=op0, op1=op1), deps)

    def stt(self, out, in0, scalar, in1, op0, op1, deps=()):
        return self.op("dve", lambda e: e.scalar_tensor_tensor(out=out, in0=in0, scalar=scalar, in1=in1, op0=op0, op1=op1), deps)

    def recip(self, out, in_, deps=()):
        return self.op("dve", lambda e: e.reciprocal(out=out, in_=in_), deps)

    def copy(self, eng, out, in_, deps=()):
        return self.op(eng, lambda e: e.tensor_copy(out=out, in_=in_), deps)

    def memset(self, ap, val, deps=()):
        return self.op("pool", lambda e: e.memset(ap, val), deps)


class Rot:
    def __init__(self, bufs):
        self.bufs = bufs
        self.free = [[] for _ in bufs]
        self.i = 0

    def next(self):
        i = self.i
        self.i = (i + 1) % len(self.bufs)
        return i, self.bufs[i], list(self.free[i])


def build_program(depth=DEPTH, debug=False):
    nc = bass.Bass("TRN2", target_bir_lowering=False)

    def din(name, shape, dt=F32):
        return nc.dram_tensor(name, list(shape), dt, kind="ExternalInput").ap()

    def dint(name, shape, dt):
        kind = "ExternalOutput" if debug else "Internal"
        return nc.dram_tensor(name, list(shape), dt, kind=kind).ap()

    xT = din("xT", [D, SL])
    ccol = din("ccol", [128, 16])
    ada_w = din("ada_w", [DEPTH, D, 6 * D])
    ada_b = din("ada_b", [128, DEPTH, 96])
    n1g = din("n1g", [128, DEPTH, 16])
    n2g = din("n2g", [128, DEPTH, 16])
    fg = din("fg", [128, 16])
    w_in = din("w_in", [DEPTH, D, DIN])
    lng = din("lng", [DEPTH, 128, 1024])
    lnb = din("lnb", [DEPTH, 128, 1024])
    wsT = din("wsT", [DEPTH, 128, 8, 128])
    bsb = din("bsb", [DEPTH, 128, 1024])
    lamv = din("lamv", [DEPTH, 128, 4, 64])
    subg = din("subg", [DEPTH, 128, 128])
    wupg = din("wupg", [DEPTH, 1024, D])
    wupa = din("wupa", [DEPTH, 1024, D])
    w_o = din("w_o", [DEPTH, D, D])
    w1 = din("w1", [DEPTH, D, FH])
    w3 = din("w3", [DEPTH, D, FH])
    w2 = din("w2", [DEPTH, FH, D])
    cident = din("cident", [128, 128])
    crt = din("crt", [128, 128])
    ccos = din("ccos", [128, SL])
    csin = din("csin", [128, SL])

    yT = nc.dram_tensor("yT", [D, SL], F32, kind="ExternalOutput").ap()
    XS = dint("XS", [D, SL], F32)
    QTd = dint("QTd", [8, 128, SL], BF16)
    KL = [[nc.dram_tensor("KL%d_%d" % (p_, c_), [256, SL], BF16) for c_ in range(4)] for p_ in range(2)]
    KA = [[nc.dram_tensor("KA%d_%d" % (p_, c_), [512, SL], BF16) for c_ in range(4)] for p_ in range(2)]
    VL = [[nc.dram_tensor("VL%d_%d" % (p_, c_), [SL, 256], BF16) for c_ in range(4)] for p_ in range(2)]
    VA = [[nc.dram_tensor("VA%d_%d" % (p_, c_), [2 * SL, 256], BF16) for c_ in range(4)] for p_ in range(2)]
    SAd = dint("SAd", [16, 128, SL], BF16)
    SBd = dint("SBd", [16, 128, SL], BF16)

    P = Prog(nc)

    ARENA = 106400
    arena = nc.alloc_sbuf_tensor("arena", [128, ARENA], BF16)
    psum = nc.alloc_psum_tensor("psum", [128, 4096], F32)

    def carve(off, nel, dt=BF16):
        a = arena[:, off:off + nel]
        if dt == F32:
            a = a.bitcast(F32)
        return a

    A0 = 0
    B0 = 32768
    W0 = B0 + 45056
    M0 = W0 + 16384

    bigA = carve(A0, 32768).rearrange("p (c t) -> p c t", t=2048)
    bigB = carve(B0, 45056).rearrange("p (c t) -> p c t", t=2048)

    def bslot(s0, ns, dt=BF16):
        return carve(B0 + s0 * 2048, ns * 2048, dt)

    wbuf = [carve(W0 + i * 8192, 8192) for i in range(2)]

    mo = [M0]

    def misc(nel_bf16, dt=BF16):
        a = carve(mo[0], nel_bf16, dt)
        mo[0] += nel_bf16
        assert mo[0] <= ARENA, mo[0]
        return a

    modAll = misc(4 * 96 * 2, F32).rearrange("p (l j) -> p l j", j=96)
    geff = misc(4 * 32 * 2, F32).rearrange("p (l j) -> p l j", j=32)
    ident = misc(128)
    RT = misc(128)
    ones = misc(128)
    cact = misc(16)
    small = misc(64 * 2, F32)
    lamt = misc(4 * 64 * 2, F32).rearrange("p (a b) -> p a b", b=64)
    lng_t = misc(1024)
    lnb_t = misc(1024)
    bs_t = misc(1024 * 2, F32)
    ws_t = misc(1024).rearrange("p (g q) -> p g q", q=128)
    subg_t = misc(128 * 2, F32)
    rstd_t = misc(512 * 2, F32)
    XT = [misc(512 * 2, F32) for _ in range(2)]
    sil = [misc(512) for _ in range(2)]
    par_f = misc(128 * 2, F32)

    ps = [psum[:, b * 512:(b + 1) * 512] for b in range(8)]

    neglam = small[:, 0:1]
    rcp = small[:, 1:2]
    ssum = small[:, 2:3]
    rr = small[:, 3:4]
    e1 = small[:, 4:5]
    e2 = small[:, 5:6]
    mv = small[:, 8:10]
    rs = small[:, 10:11]
    stats = small[:, 16:28].rearrange("p (a b) -> p a b", b=6)
    fgt = small[:, 32:48]

    P.dma("pool", ident, cident, sem="c0")
    P.dma("pool", RT, crt, sem="c0")
    P.dma("sp", par_f[:, 0:16], ccol, sem="c1")
    P.dma("sp", fgt, fg, sem="c1")
    P.dma("sp", modAll, ada_b, sem="c1")
    n1t = bslot(0, 1, F32)[:, 0:64].rearrange("p (l j) -> p l j", j=16)
    n2t = bslot(1, 1, F32)[:, 0:64].rearrange("p (l j) -> p l j", j=16)
    P.dma("sp", n1t, n1g, sem="c1")
    P.dma("sp", n2t, n2g, sem="c1")
    P.memset(ones, 1.0)
    P.barrier()
    P.act(cact, par_f[:, 0:16], AF.Silu)
    P.barrier()

    for l in range(depth):
        wfree = [[], []]
        for ci in range(24):
            slot = ci % 2
            wv = wbuf[slot].rearrange("p (k n) -> p k n", n=512)
            t_w = P.dma("pool", wv, ada_w[l][:, ci * 512:(ci + 1) * 512].rearrange("(k p) n -> p k n", p=128),
                        deps=wfree[slot], sem="w%d" % slot)
            for sc in range(4):
                j = ci * 4 + sc
                for k in range(16):
                    t = P.mm(ps[0][:, j:j + 1], wv[:, k, sc * 128:(sc + 1) * 128], cact[:, k:k + 1],
                             start=(j == 0 and k == 0), stop=(k == 15), deps=[t_w], signal=(k == 15))
            wfree[slot] = [t]
        t = P.tt("dve", modAll[:, l, :], ps[0][:, 0:96], modAll[:, l, :], ALU.add, deps=[t])
        P.stt(geff[:, l, 0:16], modAll[:, l, 16:32], 1.0, n1t[:, l, :], ALU.add, ALU.mult, deps=[t])
        P.stt(geff[:, l, 16:32], modAll[:, l, 64:80], 1.0, n2t[:, l, :], ALU.add, ALU.mult, deps=[t])
        P.barrier()

    class WStream:
        def __init__(self):
            self.free = [[], []]
            self.n = 0
            self.fifo = []

        def _issue(self, key, fn):
            slot = self.n % 2
            self.n += 1
            t = None
            for i, (dst, src) in enumerate(fn(wbuf[slot])):
                t = P.dma("pool", dst, src, deps=self.free[slot] if i == 0 else (), sem="w%d" % slot)
            return (key, slot, t)

        def prefetch(self, key, fn):
            self.fifo.append(self._issue(key, fn))

        def take(self, key, fn):
            if self.fifo:
                k, slot, t = self.fifo.pop(0)
                assert k == key, (k, key)
                return slot, t
            _, slot, t = self._issue(key, fn)
            return slot, t

        def release(self, slot, tok):
            self.free[slot] = [tok]

    W = WStream()

    def kv(src):
        return src.rearrange("(k p) n -> p k n", p=128)

    def spec_in(l, ci):
        return ("in", l, ci), lambda wb: [(wb.rearrange("p (k n) -> p k n", n=512), kv(w_in[l][:, ci * 512:(ci + 1) * 512]))]

    def spec_mg(l, ci):
        def fn(wb):
            wv = wb.rearrange("p (a k n) -> p a k n", a=2, n=512)
            return [(wv[:, 0], kv(wupg[l][:, ci * 512:(ci + 1) * 512])), (wv[:, 1], kv(wupa[l][:, ci * 512:(ci + 1) * 512]))]
        return ("mg", l, ci), fn

    def spec_wo(l, ci):
        return ("wo", l, ci), lambda wb: [(wb.rearrange("p (k n) -> p k n", n=512), kv(w_o[l][:, ci * 512:(ci + 1) * 512]))]

    def spec_w13(l, hs, jc):
        def fn(wb):
            col0 = (hs * 22 + jc * 2) * 128
            wv = wb.rearrange("p (a k n) -> p a k n", a=2, n=256)
            return [(wv[:, 0], kv(w1[l][:, col0:col0 + 256])), (wv[:, 1], kv(w3[l][:, col0:col0 + 256]))]
        return ("w13", l, hs, jc), fn

    def spec_w2(l, hs, ci):
        return ("w2", l, hs, ci), lambda wb: [(wb[:, 0:22 * 256].rearrange("p (k n) -> p k n", n=256),
                                               kv(w2[l][hs * 2816:(hs + 1) * 2816, ci * 256:(ci + 1) * 256]))]

    def prefetch2(spec_fn, *args):
        for i in range(2):
            W.prefetch(*spec_fn(*args, i))

    def norm_phase(xsrc, half, g_ap, sh_ap, final=False):
        xg_rot = Rot([bslot(12, 8, F32).rearrange("p (c t) -> p c t", t=512),
                      bslot(0, 8, F32).rearrange("p (c t) -> p c t", t=512)])
        sq_rot = Rot([bslot(20, 1)[:, i * 512:(i + 1) * 512] for i in range(4)])
        tmp_rot = Rot([bslot(21, 1, F32)[:, i * 512:(i + 1) * 512] for i in range(2)])
        out_rot = Rot(XT)
        bank_rot = Rot([ps[0], ps[1]])
        loads = {}

        def load_x(tg_):
            ixg_, xg_, xfree_ = xg_rot.next()
            t0_ = half * 2048 + tg_ * 512
            loads[tg_] = (ixg_, xg_, P.dma("sp", xg_, xsrc[:, t0_:t0_ + 512].rearrange("(c p) t -> p c t", p=128),
                                           deps=xfree_, sem="xg%d" % ixg_))

        load_x(0)
        for tg in range(4):
            tok0 = half * 2048 + tg * 512
            if tg + 1 < 4:
                load_x(tg + 1)
            ixg, xg, t_ld = loads.pop(tg)
            _, bank, bfree = bank_rot.next()
            for c in range(16):
                i, sqb, fr = sq_rot.next()
                t_sq = P.act(sqb, xg[:, c, :], AF.Square, deps=[t_ld] + fr)
                t_mm = P.mm(bank, ones, sqb, start=(c == 0), stop=(c == 15),
                            deps=[t_sq] + (bfree if c == 0 else []), signal=True)
                sq_rot.free[i] = [t_mm]
            t1 = P.ts("dve", rstd_t, bank, 1.0 / D, EPS, ALU.mult, ALU.add, deps=[t_mm])
            bank_rot.free[(bank_rot.i - 1) % 2] = [t1]
            t2 = P.act(rstd_t, rstd_t, AF.Sqrt, deps=[t1])
            t3 = P.recip(rstd_t, rstd_t, deps=[t2])
            last = []
            for c in range(16):
                i, tb, fr = tmp_rot.next()
                t_a = P.tt("dve", tb, xg[:, c, :], rstd_t, ALU.mult, deps=[t3] + fr)
                if not final:
                    t_b = P.act(bigA[:, c, tg * 512:(tg + 1) * 512], tb, AF.Identity, deps=[t_a],
                                scale=g_ap[:, c:c + 1], bias=sh_ap[:, c:c + 1])
                    tmp_rot.free[i] = [t_b]
                    last = [t_a, t_b]
                else:
                    io, ob, fro = out_rot.next()
                    t_b = P.act(ob, tb, AF.Identity, deps=[t_a] + fro, scale=g_ap[:, c:c + 1])
                    tmp_rot.free[i] = [t_b]
                    t_s = P.dma("sp", yT[c * 128:(c + 1) * 128, tok0:tok0 + 512], ob, deps=[t_b], sem="xo%d" % io)
                    out_rot.free[io] = [t_s]
                    last = [t_a, t_b]
            xg_rot.free[ixg] = last
        P.barrier()

    def resid_update(bank, t_mm, c, tok0, xsrc, gt_ap, xrot, bank_rot_free_cb):
        io, xt, fro = xrot.next()
        t_x = P.dma("sp", xt, xsrc[c * 128:(c + 1) * 128, tok0:tok0 + 512], deps=fro, sem="xi%d" % io)
        t_u = P.stt(xt, bank, gt_ap[:, c:c + 1], xt, ALU.mult, ALU.add, deps=[t_mm, t_x])
        bank_rot_free_cb([t_u])
        t_s = P.dma("sp", XS[c * 128:(c + 1) * 128, tok0:tok0 + 512], xt, deps=[t_u], sem="xi%d" % io)
        xrot.free[io] = [t_s]

    for l in range(depth):
        lambda_init = 0.8 - 0.6 * math.exp(-0.3 * l)
        xsrc = xT if l == 0 else XS
        sh1 = modAll[:, l, 0:16]
        gt1 = modAll[:, l, 32:48]
        sh2 = modAll[:, l, 48:64]
        gt2 = modAll[:, l, 80:96]
        g1 = geff[:, l, 0:16]
        g2 = geff[:, l, 16:32]

        par = l % 2
        if l == 0:
            prefetch2(spec_in, l)
        P.dma("pool", lng_t, lng[l], sem="c0")
        P.dma("pool", lnb_t, lnb[l], sem="c0")
        P.dma("pool", ws_t, wsT[l], sem="c0")
        P.dma("sp", bs_t, bsb[l], sem="c1")
        P.dma("sp", subg_t, subg[l], sem="c1")
        P.dma("sp", lamt, lamv[l], sem="c1")
        P.barrier()
        t = P.tt("dve", lamt[:, 0, :], lamt[:, 0, :], lamt[:, 1, :], ALU.mult)
        t = P.op("dve", lambda e: e.reduce_sum(out=e1, in_=lamt[:, 0, :], axis=AX), deps=[t])
        t = P.act(e1, e1, AF.Exp, deps=[t])
        t2 = P.tt("dve", lamt[:, 2, :], lamt[:, 2, :], lamt[:, 3, :], ALU.mult)
        t2 = P.op("dve", lambda e: e.reduce_sum(out=e2, in_=lamt[:, 2, :], axis=AX), deps=[t2])
        t2 = P.act(e2, e2, AF.Exp, deps=[t2])
        t = P.tt("dve", neglam, e2, e1, ALU.subtract, deps=[t, t2])
        t = P.ts("dve", neglam, neglam, -lambda_init, None, ALU.add, deps=[t])
        P.ts("dve", subg_t, subg_t, 1.0 - lambda_init, None, ALU.mult)
        P.barrier()

        for half in range(1):
            hb = half * 2048
            norm_phase(xsrc, half, g1, sh1)
            hT = bigA
            uT = bigB[:, 0:8, :]
            cosb = bslot(8, 1)
            sinb = bslot(10, 1)
            P.dma("pool", cosb[:, 0:SL], ccos, sem="c0")
            P.dma("pool", sinb[:, 0:SL], csin, sem="c0")
            vg = bslot(12, 1, F32)
            tmp2 = bslot(13, 1, F32)
            vn = bslot(14, 1)[:, 0:1024]
            vst_rot = Rot([bslot(14, 1)[:, 1024:2048], bslot(15, 1)[:, 0:1024]])
            stage_rot = Rot([bslot(16, 1)[:, i * 512:(i + 1) * 512] for i in range(4)])
            qb_rot = Rot([bslot(17, 1)[:, i * 512:(i + 1) * 512] for i in range(2)])
            f32_rot = Rot([bslot(18 + i // 2, 1, F32)[:, (i % 2) * 512:(i % 2 + 1) * 512] for i in range(4)])
            P.barrier()
            bank_rot = Rot([ps[0], ps[1], ps[2], ps[3]])
            pair_rot = Rot([psum[:, 2048:3072], psum[:, 3072:4096]])
            t_w = {}
            w_slot = {}

            def load_w(ci):
                w_slot[ci], t_w[ci] = W.take(*spec_in(l, ci))
                return wbuf[w_slot[ci]].rearrange("p (k n) -> p k n", n=512)

            def fm_piece(ci, kind, jbase):
                wv = load_w(ci)
                slot = w_slot[ci]
                rope_pend = []
                for sc in range(4):
                    j = ci * 4 + sc - jbase
                    for tg in range(4):
                        ib, bank, bfree = bank_rot.next()
                        tsl = slice(tg * 512, (tg + 1) * 512)
                        gsl = slice(hb + tg * 512, hb + (tg + 1) * 512)
                        for k in range(16):
                            t_mm = P.mm(bank, wv[:, k, sc * 128:(sc + 1) * 128], hT[:, k, tsl], start=(k == 0),
                                        stop=(k == 15), deps=([t_w[ci]] + bfree) if k == 0 else (), signal=(k == 15))
                        if kind == "u":
                            t = P.act(uT[:, j, tsl], bank, AF.Gelu_apprx_tanh, deps=[t_mm])
                            bank_rot.free[ib] = [t]
                        elif kind in ("ga", "gb"):
                            i4, st, frs = stage_rot.next()
                            t = P.act(st, bank, AF.Sigmoid, deps=[t_mm] + frs)
                            bank_rot.free[ib] = [t]
                            dst = SAd if kind == "ga" else SBd
                            t_s = P.dma("sp", dst[j][:, gsl], st, deps=[t], sem="st%d" % i4)
                            stage_rot.free[i4] = [t_s]
                        else:
                            iq, qb_, frq = qb_rot.next()
                            t_c = P.act(qb_, bank, AF.Copy, deps=[t_mm] + frq)
                            bank_rot.free[ib] = [t_c]

                            def rope_part(iq=iq, qb_=qb_, t_c=t_c, j=j, gsl=gsl):
                                ib2, bank2, bfree2 = bank_rot.next()
                                t_r = P.mm(bank2, RT, qb_, start=True, stop=True, deps=[t_c] + bfree2, signal=True)
                                ia, ta, fra = f32_rot.next()
                                t_a = P.tt("dve", ta, qb_, cosb[:, gsl], ALU.mult, deps=[t_c] + fra)
                                ibb, tb_, frb = f32_rot.next()
                                t_b = P.tt("dve", tb_, bank2, sinb[:, gsl], ALU.mult, deps=[t_r] + frb)
                                bank_rot.free[ib2] = [t_b]
                                qb_rot.free[iq] = [t_r, t_a]
                                i4, st, frs = stage_rot.next()
                                t_o = P.tt("dve", st, ta, tb_, ALU.add, deps=[t_a, t_b] + frs)
                                f32_rot.free[ia] = [t_o]
                                f32_rot.free[ibb] = [t_o]
                                if kind == "q":
                                    dst_ap = QTd[j][:, gsl]
                                else:
                                    dst_ap = KL[par][j // 2][(j % 2) * 128:(j % 2 + 1) * 128, gsl]
                                t_s = P.dma("sp", dst_ap, st, deps=[t_o], sem="st%d" % i4)
                                stage_rot.free[i4] = [t_s]

                            while rope_pend:
                                rope_pend.pop(0)()
                            rope_pend.append(rope_part)
                while rope_pend:
                    rope_pend.pop(0)()
                W.release(slot, t_mm)

            def tm_v(ci0):
                wvs = [load_w(ci0), load_w(ci0 + 1)]
                sets = [(bslot(12, 1, F32), bslot(13, 1, F32), bslot(14, 1)[:, 0:1024]),
                        (bslot(9, 1, F32), bslot(11, 1, F32), bslot(20, 1)[:, 0:1024])]
                set_free = [[], []]
                mainb, spb = pair_rot.bufs[0], pair_rot.bufs[1]
                fr = {"main": list(pair_rot.free[0]), "sp": list(pair_rot.free[1])}
                t_mm = None

                def sp_part(tt_, tmp2_, vn_, t_n_):
                    tsl_ = slice(tt_ * 128, (tt_ + 1) * 128)
                    for g in range(8):
                        t_sp = P.mm(spb[:, g * 128:(g + 1) * 128], vn_[:, g * 128:(g + 1) * 128], ws_t[:, g, :],
                                    start=True, stop=True, deps=([t_n_] + fr["sp"]) if g == 0 else (), signal=(g == 7))
                    t_m = P.tt("dve", tmp2_, spb, bs_t, ALU.add, deps=[t_sp])
                    fr["sp"] = [t_m]
                    t_u = P.tt("dve", uT[:, :, tsl_], tmp2_.rearrange("p (g q) -> p g q", q=128), uT[:, :, tsl_],
                               ALU.mult, deps=[t_m])
                    set_free[tt_ % 2] = [t_u]

                pend = None
                for tt_ in range(16):
                    vg_, tmp2_, vn_ = sets[tt_ % 2]
                    tsl = slice(tt_ * 128, (tt_ + 1) * 128)
                    for cc in range(2):
                        for k in range(16):
                            first = (cc == 0 and k == 0)
                            t_mm = P.mm(mainb[:, cc * 512:(cc + 1) * 512], hT[:, k, tsl], wvs[cc][:, k, :],
                                        start=(k == 0), stop=(k == 15),
                                        deps=([t_w[ci0], t_w[ci0 + 1]] + fr["main"]) if first else (),
                                        signal=(cc == 1 and k == 15))
                    t_g = P.act(vg_, mainb, AF.Gelu_apprx_tanh, deps=[t_mm] + set_free[tt_ % 2])
                    fr["main"] = [t_g]
                    ta_ = P.op("dve", lambda e, v=vg_: e.bn_stats(out=stats[:, 0, :], in_=v[:, 0:512]), deps=[t_g])
                    tb2 = P.op("dve", lambda e, v=vg_: e.bn_stats(out=stats[:, 1, :], in_=v[:, 512:1024]), deps=[t_g])
                    t_ag = P.op("dve", lambda e: e.bn_aggr(out=mv, in_=small[:, 16:28]), deps=[ta_, tb2])
                    t_r1 = P.ts("dve", rs, mv[:, 1:2], EPS, None, ALU.add, deps=[t_ag])
                    t_r2 = P.act(rs, rs, AF.Sqrt, deps=[t_r1])
                    t_r3 = P.recip(rs, rs, deps=[t_r2])
                    t_n = P.ts("dve", vg_, vg_, mv[:, 0:1], rs, ALU.subtract, ALU.mult, deps=[t_r3])
                    t_n = P.tt("dve", vg_, vg_, lng_t, ALU.mult, deps=[t_n])
                    t_n = P.tt("dve", vn_, vg_, lnb_t, ALU.add, deps=[t_n])
                    if pend is not None:
                        sp_part(*pend)
                    pend = (tt_, tmp2_, vn_, t_n)
                sp_part(*pend)
                pair_rot.free[0] = fr["main"]
                pair_rot.free[1] = fr["sp"]
                pair_rot.i = 0
                W.release(w_slot[ci0], t_mm)
                W.release(w_slot[ci0 + 1], t_mm)

            def tm_piece(ci0, kind):
                if kind == "v":
                    return tm_v(ci0)
                wv0 = load_w(ci0)
                wv1 = load_w(ci0 + 1)
                wvs = [wv0, wv1]
                t_u_last = None
                for tt_ in range(16):
                    ip, pair, pfree = pair_rot.next()
                    tsl = slice(tt_ * 128, (tt_ + 1) * 128)
                    for cc in range(2):
                        for k in range(16):
                            first = (cc == 0 and k == 0)
                            t_mm = P.mm(pair[:, cc * 512:(cc + 1) * 512], hT[:, k, tsl], wvs[cc][:, k, :],
                                        start=(k == 0), stop=(k == 15),
                                        deps=([t_w[ci0], t_w[ci0 + 1]] + pfree) if first else (),
                                        signal=(cc == 1 and k == 15))
                    if kind == "va":
                        iv, vs, frv = vst_rot.next()
                        t = P.act(vs, pair, AF.Copy, deps=[t_mm] + frv)
                        pair_rot.free[ip] = [t]
                        for c_ in range(4):
                            t_s = P.dma("sp", VL[par][c_][tt_ * 128:(tt_ + 1) * 128, :], vs[:, c_ * 256:(c_ + 1) * 256],
                                        deps=[t], sem="vs%d" % iv)
                        vst_rot.free[iv] = [t_s]
                    else:
                        t_g = P.act(vg, pair, AF.Gelu_apprx_tanh, deps=[t_mm] + ([t_u_last] if t_u_last else []))
                        pair_rot.free[ip] = [t_g]
                        ta_ = P.op("dve", lambda e: e.bn_stats(out=stats[:, 0, :], in_=vg[:, 0:512]), deps=[t_g])
                        tb2 = P.op("dve", lambda e: e.bn_stats(out=stats[:, 1, :], in_=vg[:, 512:1024]), deps=[t_g])
                        t_ag = P.op("dve", lambda e: e.bn_aggr(out=mv, in_=small[:, 16:28]), deps=[ta_, tb2])
                        t_r1 = P.ts("dve", rs, mv[:, 1:2], EPS, None, ALU.add, deps=[t_ag])
                        t_r2 = P.act(rs, rs, AF.Sqrt, deps=[t_r1])
                        t_r3 = P.recip(rs, rs, deps=[t_r2])
                        t_n = P.ts("dve", vg, vg, mv[:, 0:1], rs, ALU.subtract, ALU.mult, deps=[t_r3])
                        t_n = P.tt("dve", vg, vg, lng_t, ALU.mult, deps=[t_n])
                        t_n = P.tt("dve", vn, vg, lnb_t, ALU.add, deps=[t_n] + ([t_u_last] if t_u_last else []))
                        ip2, pair2, pfree2 = pair_rot.next()
                        for g in range(8):
                            t_sp = P.mm(pair2[:, g * 128:(g + 1) * 128], vn[:, g * 128:(g + 1) * 128], ws_t[:, g, :],
                                        start=True, stop=True, deps=([t_n] + pfree2) if g == 0 else (), signal=(g == 7))
                        t_m = P.tt("dve", tmp2, pair2, bs_t, ALU.add, deps=[t_sp])
                        pair_rot.free[ip2] = [t_m]
                        t_u_last = P.tt("dve", uT[:, :, tsl], tmp2.rearrange("p (g q) -> p g q", q=128), uT[:, :, tsl],
                                        ALU.mult, deps=[t_m])
                W.release(w_slot[ci0], t_mm)
                W.release(w_slot[ci0 + 1], t_mm)

            fm_piece(0, "u", 0)
            fm_piece(1, "u", 0)
            tm_piece(2, "v")
            fm_piece(4, "q", 16)
            fm_piece(5, "q", 16)
            fm_piece(6, "k", 24)
            fm_piece(7, "k", 24)
            tm_piece(8, "va")
            for ci in range(10, 14):
                if ci == 10:
                    kdeps = P.dtoks(["st0", "st1", "st2", "st3"])
                    for c_ in range(4):
                        P.coll(KL[par][c_], KA[par][c_], deps=kdeps, sem="cc")
                if ci == 12:
                    vdeps = P.dtoks(["vs0", "vs1"])
                    for c_ in range(4):
                        P.coll(VL[par][c_], VA[par][c_], deps=vdeps, sem="cc")
                fm_piece(ci, "ga", 40)
            for ci in range(14, 18):
                fm_piece(ci, "gb", 56)
            P.barrier()

        def aslot(s0, ns, dt=BF16):
            return carve(A0 + s0 * 2048, ns * 2048, dt)

        KTa = [aslot(0, 2), aslot(2, 2)]
        KTb = [aslot(4, 2), aslot(6, 2)]
        QTb = [aslot(8, 1), aslot(9, 1)]
        Vau = [aslot(10, 3)[:, 0:32 * 129].rearrange("p (k d) -> p k d", d=129),
               aslot(13, 3)[:, 0:32 * 129].rearrange("p (k d) -> p k d", d=129)]
        PT_rot = Rot([bslot(18, 1)[:, i * 512:(i + 1) * 512] for i in range(4)])
        onorm = [[bslot(19, 1, F32)[:, (m * 4 + j) * 128:(m * 4 + j + 1) * 128] for j in range(4)] for m in range(2)]
        ocomb = bslot(20, 1, F32)[:, 0:128]
        junk = bslot(20, 1, F32)[:, 128:256]
        obf = bslot(20, 1)[:, 512:640]
        brB_w = bigB[:, 8:16, :]
        prefetch2(spec_mg, l)
        for b in range(2):
            P.memset(KTa[b][64:128, :], 0.0)
            P.memset(KTb[b][0:64, :], 0.0)
            P.memset(Vau[b][:, :, 128:129], 1.0)
        P.barrier()
        S_rot = Rot([ps[0], ps[1], ps[2]])
        O_rot = Rot([psum[:, 1536:2560], psum[:, 2560:3584]])
        psT = ps[7].bitcast(BF16)
        psT_free = []
        hfree = [[], []]
        on_free = [[[] for _ in range(4)] for _ in range(2)]
        for h in range(8):
            buf = h % 2
            hc, hl = h // 2, h % 2
            tl = []
            for r_ in range(2):
                r0 = r_ * 256 + hl * 128
                tl.append(P.dma("sp", KTa[buf][0:64, r_ * SL:(r_ + 1) * SL], KA[par][hc][r0:r0 + 64, :],
                                deps=hfree[buf] if r_ == 0 else (), sem="hk%d" % buf))
                tl.append(P.dma("sp", KTb[buf][64:128, r_ * SL:(r_ + 1) * SL], KA[par][hc][r0 + 64:r0 + 128, :],
                                sem="hk%d" % buf))
            tl.append(P.dma("sp", QTb[buf], QTd[h], sem="hk%d" % buf))
            for r_ in range(2):
                tl.append(P.dma("sp", Vau[buf][:, r_ * 16:(r_ + 1) * 16, 0:128],
                                VA[par][hc][r_ * SL:(r_ + 1) * SL, hl * 128:(hl + 1) * 128].rearrange(
                                    "(k p) d -> p k d", p=128), sem="hk%d" % buf))
            t_ld = tl[-1]
            last_pe = None
            for qg in range(SL // 512):
                qsl = slice(qg * 512, (qg + 1) * 512)
                for m in range(2):
                    KTm = KTa[buf] if m == 0 else KTb[buf]
                    io, Ob, ofree = O_rot.next()
                    Oj = [Ob[:, (j // 2) * 512 + (j % 2) * 129:(j // 2) * 512 + (j % 2) * 129 + 129] for j in range(4)]
                    sinfo = {}

                    def issue_S(kc):
                        ib, bank, bfree = S_rot.next()
                        t = P.mm(bank, KTm[:, kc * 128:(kc + 1) * 128], QTb[buf][:, qsl], start=True, stop=True,
                                 deps=[t_ld] + bfree, signal=True)
                        sinfo[kc] = (ib, bank, t)

                    issue_S(0)
                    issue_S(1)
                    for kc in range(32):
                        ib, bank, t_S = sinfo.pop(kc)
                        ipt, pt, frp = PT_rot.next()
                        t_e = P.act(pt, bank, AF.Exp, deps=[t_S] + frp, scale=0.125)
                        S_rot.free[ib] = [t_e]
                        if kc + 2 < 32:
                            issue_S(kc + 2)
                        for j in range(4):
                            t_pv = P.mm(Oj[j], pt[:, j * 128:(j + 1) * 128], Vau[buf][:, kc, :],
                                        start=(kc == 0 and j % 2 == 0), stop=(kc == 31),
                                        deps=([t_e] + (ofree if kc == 0 else [])) if j == 0 else (), signal=(j == 3))
                        PT_rot.free[ipt] = [t_pv]
                    last_pe = t_pv
                    t_on = None
                    for j in range(4):
                        t_r = P.recip(rcp, Oj[j][:, 128:129], deps=[t_pv] + on_free[m][j])
                        t_on = P.ts("dve", onorm[m][j], Oj[j][:, 0:128], rcp, None, ALU.mult, deps=[t_r])
                    O_rot.free[io] = [t_on]
                    if m == 1:
                        for j in range(4):
                            t_c = P.stt(ocomb, onorm[1][j], neglam, onorm[0][j], ALU.mult, ALU.add, deps=[t_on])
                            on_free[0][j] = [t_c]
                            on_free[1][j] = [t_c]
                            t_q = P.tt("dve", junk, ocomb, ocomb, ALU.mult, deps=[t_c])
                            t_q = P.op("dve", lambda e: e.reduce_sum(out=ssum, in_=junk, axis=AX), deps=[t_q])
                            t_q = P.ts("dve", rr, ssum, 1.0 / 128, EPS, ALU.mult, ALU.add, deps=[t_q])
                            t_q = P.act(rr, rr, AF.Ln, deps=[t_q])
                            t_q = P.act(rr, rr, AF.Exp, deps=[t_q], scale=-0.5)
                            t_ob = P.stt(obf, ocomb, rr, subg_t, ALU.mult, ALU.mult, deps=[t_q])
                            t_tr = P.transpose(psT[:, j * 128:(j + 1) * 128], obf, ident,
                                               deps=[t_ob] + (psT_free if j == 0 else []))
                            t_on = t_tr
                        t_cp = P.copy("dve", brB_w[:, h, qsl], psT[:, 0:512], deps=[t_tr])
                        psT_free = [t_cp]
            hfree[buf] = [last_pe]
        P.barrier()

        for half in range(1):
            hb = half * 2048
            brA = bigB[:, 0:8, :]
            brB = bigB[:, 8:16, :]
            merged = bigA
            sa_rot = Rot([bslot(16, 1), bslot(17, 1)])
            sb_rot = Rot([bslot(18, 1), bslot(19, 1)])
            t1_rot = Rot([bslot(20, 1, F32)[:, i * 512:(i + 1) * 512] for i in range(2)])
            t2_rot = Rot([bslot(21, 1, F32)[:, i * 512:(i + 1) * 512] for i in range(2)])
            bank_rot = Rot([ps[0], ps[1], ps[2], ps[3], ps[4], ps[5]])
            for ci in range(4):
                slot, t_w = W.take(*spec_mg(l, ci))
                wv = wbuf[slot].rearrange("p (a k n) -> p a k n", a=2, n=512)
                for sc in range(4):
                    c = ci * 4 + sc
                    isa, sat, frsa = sa_rot.next()
                    isb, sbt, frsb = sb_rot.next()
                    t_sa = P.dma("sp", sat, SAd[c][:, hb:hb + 2048], deps=frsa, sem="sa%d" % isa)
                    t_sb = P.dma("sp", sbt, SBd[c][:, hb:hb + 2048], deps=frsb, sem="sb%d" % isb)
                    for tg in range(4):
                        tsl = slice(tg * 512, (tg + 1) * 512)
                        ia_, bka, fa = bank_rot.next()
                        for k in range(8):
                            t_ma = P.mm(bka, wv[:, 0, k, sc * 128:(sc + 1) * 128], brA[:, k, tsl], start=(k == 0),
                                        stop=(k == 7), deps=([t_w] + fa) if k == 0 else (), signal=(k == 7))
                        ib_, bkb, fb = bank_rot.next()
                        for k in range(8):
                            t_mb = P.mm(bkb, wv[:, 1, k, sc * 128:(sc + 1) * 128], brB[:, k, tsl], start=(k == 0),
                                        stop=(k == 7), deps=fb if k == 0 else (), signal=(k == 7))
                        i1, t1b, f1 = t1_rot.next()
                        i2, t2b, f2 = t2_rot.next()
                        ta_ = P.tt("dve", t1b, bka, sat[:, tsl], ALU.mult, deps=[t_ma, t_sa] + f1)
                        tb2 = P.tt("dve", t2b, bkb, sbt[:, tsl], ALU.mult, deps=[t_mb, t_sb] + f2)
                        bank_rot.free[ia_] = [ta_]
                        bank_rot.free[ib_] = [tb2]
                        t_m = P.tt("pool", merged[:, c, tsl], t1b, t2b, ALU.add, deps=[ta_, tb2])
                        t1_rot.free[i1] = [t_m]
                        t2_rot.free[i2] = [t_m]
                    sa_rot.free[isa] = [ta_]
                    sb_rot.free[isb] = [tb2]
                W.release(slot, t_mb)
            prefetch2(spec_wo, l)
            P.barrier()
            xrot = Rot(XT)
            for ci in range(4):
                slot, t_w = W.take(*spec_wo(l, ci))
                wv = wbuf[slot].rearrange("p (k n) -> p k n", n=512)
                for sc in range(4):
                    c = ci * 4 + sc
                    for tg in range(4):
                        tsl = slice(tg * 512, (tg + 1) * 512)
                        ib, bank, bfree = bank_rot.next()
                        for k in range(16):
                            t_mm = P.mm(bank, wv[:, k, sc * 128:(sc + 1) * 128], merged[:, k, tsl], start=(k == 0),
                                        stop=(k == 15), deps=([t_w] + bfree) if k == 0 else (), signal=(k == 15))

                        def cb(toks, ib=ib):
                            bank_rot.free[ib] = toks
                        resid_update(bank, t_mm, c, hb + tg * 512, xsrc, gt1, xrot, cb)
                W.release(slot, t_mm)
            prefetch2(spec_w13, l, 0)
            P.barrier()

        for half in range(1):
            hb = half * 2048
            norm_phase(XS, half, g2, sh2)
            hT = bigA
            hid = bigB
            bank_rot = Rot([ps[0], ps[1], ps[2], ps[3], ps[4], ps[5]])
            sil_rot = Rot(sil)
            xrot = Rot(XT)
            for hs in range(2):
                for jc in range(11):
                    slot, t_w = W.take(*spec_w13(l, hs, jc))
                    wv = wbuf[slot].rearrange("p (a k n) -> p a k n", a=2, n=256)
                    for sc in range(2):
                        jj = jc * 2 + sc
                        for tg in range(4):
                            tsl = slice(tg * 512, (tg + 1) * 512)
                            i1, b1, f1 = bank_rot.next()
                            for k in range(16):
                                t_m1 = P.mm(b1, wv[:, 0, k, sc * 128:(sc + 1) * 128], hT[:, k, tsl], start=(k == 0),
                                            stop=(k == 15), deps=([t_w] + f1) if k == 0 else (), signal=(k == 15))
                            i3, b3, f3 = bank_rot.next()
                            for k in range(16):
                                t_m3 = P.mm(b3, wv[:, 1, k, sc * 128:(sc + 1) * 128], hT[:, k, tsl], start=(k == 0),
                                            stop=(k == 15), deps=f3 if k == 0 else (), signal=(k == 15))
                            isl, sb_, fs = sil_rot.next()
                            t_s = P.act(sb_, b1, AF.Silu, deps=[t_m1] + fs)
                            bank_rot.free[i1] = [t_s]
                            t_h = P.tt("dve", hid[:, jj, tsl], b3, sb_, ALU.mult, deps=[t_m3, t_s])
                            bank_rot.free[i3] = [t_h]
                            sil_rot.free[isl] = [t_h]
                    W.release(slot, t_m3)
                prefetch2(spec_w2, l, hs)
                P.barrier()
                for ci in range(8):
                    slot, t_w = W.take(*spec_w2(l, hs, ci))
                    wv = wbuf[slot][:, 0:22 * 256].rearrange("p (k n) -> p k n", n=256)
                    for sc in range(2):
                        c = ci * 2 + sc
                        for tg in range(4):
                            tsl = slice(tg * 512, (tg + 1) * 512)
                            ib, bank, bfree = bank_rot.next()
                            for k in range(22):
                                t_mm = P.mm(bank, wv[:, k, sc * 128:(sc + 1) * 128], hid[:, k, tsl], start=(k == 0),
                                            stop=(k == 21), deps=([t_w] + bfree) if k == 0 else (), signal=(k == 21))

                            def cb(toks, ib=ib):
                                bank_rot.free[ib] = toks
                            resid_update(bank, t_mm, c, hb + tg * 512, XS, gt2, xrot, cb)
                    W.release(slot, t_mm)
                if hs == 0:
                    prefetch2(spec_w13, l, 1)
                elif l + 1 < depth:
                    prefetch2(spec_in, l + 1)
                P.barrier()

    for half in range(1):
        norm_phase(XS, half, fgt, None, final=True)
    P.barrier()

    with nc.Block() as block:
        @block.tensor
        def _(e):
            for f in P.streams["pe"]:
                f(e)

        @block.scalar
        def _(e):
            for f in P.streams["act"]:
                f(e)

        @block.vector
        def _(e):
            for f in P.streams["dve"]:
                f(e)

        @block.gpsimd
        def _(e):
            for f in P.streams["pool"]:
                f(e)

        @block.sync
        def _(e):
            for f in P.streams["sp"]:
                f(e)
    return nc


def rope_consts():
    j = np.arange(32, dtype=np.float32)
    inv_freq = (10000.0 ** (-(2.0 * j) / 64.0)).astype(np.float32)
    pos = np.arange(S, dtype=np.float32)
    ang = pos[None, :] * inv_freq[:, None]
    cos = np.cos(ang).astype(np.float32)
    sin = np.sin(ang).astype(np.float32)
    cosT = np.tile(cos, (4, 1))
    sinT = np.tile(sin, (4, 1))
    rt = np.zeros((128, 128), np.float32)
    for m in range(2):
        for jj in range(32):
            i0 = m * 64 + jj
            i1 = m * 64 + 32 + jj
            rt[i1, i0] = -1.0
            rt[i0, i1] = 1.0
    return cosT, sinT, rt, np.eye(128, dtype=np.float32)


def make_in_maps(x, c, ada_w, ada_b, norm1_g, norm2_g, w_in, gmlp_ln_g, gmlp_ln_b, w_s, b_s,
                 lambda_q1, lambda_k1, lambda_q2, lambda_k2, subln_g, w_up_gmlp, w_up_attn, w_o,
                 ffn_w1, ffn_w3, ffn_w2, final_g):
    f = lambda a: np.ascontiguousarray(np.asarray(a, dtype=np.float32))
    cosT, sinT, rt, ident = rope_consts()
    L = DEPTH
    shared = {
        "ada_w": f(ada_w),
        "ada_b": f(np.asarray(ada_b).reshape(L, 96, 128).transpose(2, 0, 1)),
        "n1g": f(np.asarray(norm1_g).reshape(L, 16, 128).transpose(2, 0, 1)),
        "n2g": f(np.asarray(norm2_g).reshape(L, 16, 128).transpose(2, 0, 1)),
        "fg": f(np.asarray(final_g).reshape(16, 128).T),
        "w_in": f(w_in),
        "lng": f(np.broadcast_to(np.asarray(gmlp_ln_g)[:, None, :], (L, 128, 1024))),
        "lnb": f(np.broadcast_to(np.asarray(gmlp_ln_b)[:, None, :], (L, 128, 1024))),
        "wsT": f(np.asarray(w_s).transpose(0, 3, 1, 2)),
        "bsb": f(np.broadcast_to(np.asarray(b_s).reshape(L, 1, 1024), (L, 128, 1024))),
        "lamv": f(np.broadcast_to(np.stack([np.asarray(lambda_q1), np.asarray(lambda_k1), np.asarray(lambda_q2),
                                            np.asarray(lambda_k2)], axis=1)[:, None], (L, 128, 4, 64))),
        "subg": f(np.broadcast_to(np.asarray(subln_g)[:, None, :], (L, 128, 128))),
        "wupg": f(w_up_gmlp), "wupa": f(w_up_attn), "w_o": f(w_o),
        "w1": f(ffn_w1), "w3": f(ffn_w3), "w2": f(ffn_w2),
        "cident": ident, "crt": rt,
    }
    x = np.asarray(x, dtype=np.float32)
    c = np.asarray(c, dtype=np.float32)
    in_maps = []
    for core in range(NCORES):
        b, r = core // 2, core % 2
        m = dict(shared)
        m["xT"] = np.ascontiguousarray(x[b, r * SL:(r + 1) * SL, :].T)
        m["ccol"] = np.ascontiguousarray(c[b].reshape(16, 128).T)
        m["ccos"] = np.ascontiguousarray(cosT[:, r * SL:(r + 1) * SL])
        m["csin"] = np.ascontiguousarray(sinT[:, r * SL:(r + 1) * SL])
        in_maps.append(m)
    return in_maps


def kernel(**inputs):
    in_maps = make_in_maps(**inputs)
    nc = build_program()
    res = run_bass_kernel_spmd(nc, in_maps, core_ids=list(range(NCORES)))
    out = np.empty((NCORES // 2, S, D), np.float32)
    for core in range(NCORES):
        b, r = core // 2, core % 2
        out[b, r * SL:(r + 1) * SL, :] = np.asarray(res.results[core]["yT"]).T
    return out
```

```python
import math
import numpy as np
import concourse.bass as bass
import concourse.mybir as mybir
from concourse.bass_utils import run_bass_kernel_spmd

F32 = mybir.dt.float32
BF16 = mybir.dt.bfloat16
AF = mybir.ActivationFunctionType
ALU = mybir.AluOpType
AX = mybir.AxisListType.X

D = 2048
S = 4096
SL = 2048
DEPTH = 4
DIN = 9216
FH = 5632
NCORES = 8
RG = [[0, 1], [2, 3], [4, 5], [6, 7]]
EPS = 1e-6
ENG = ["pe", "act", "dve", "pool", "sp"]


class Prog:
    def __init__(self, nc):
        self.nc = nc
        self.streams = {e: [] for e in ENG}
        self.psem = {e: nc.alloc_semaphore(name="prog_" + e) for e in ENG}
        self.cnt = {e: 0 for e in ENG}
        self.known = {e: {} for e in ENG}
        self.dsems = {}

    def _wait(self, eng, tok):
        if tok is None:
            return
        key, handle, val = tok
        if self.known[eng].get(key, 0) >= val:
            return
        self.known[eng][key] = val
        self.streams[eng].append(lambda e, h=handle, v=val: e.wait_ge(h, v))

    def op(self, eng, fn, deps=(), signal=True):
        for d in deps:
            self._wait(eng, d)
        if signal:
            self.cnt[eng] += 1
            h = self.psem[eng]
            self.streams[eng].append(lambda e, fn=fn, h=h: fn(e).then_inc(h, 1))
            return (eng, h, self.cnt[eng])
        self.streams[eng].append(lambda e, fn=fn: fn(e))
        return None

    def dma(self, queue, out, in_, deps=(), sem="d"):
        for d in deps:
            self._wait(queue, d)
        if sem not in self.dsems:
            self.dsems[sem] = [self.nc.alloc_semaphore(name="dma_" + sem), 0]
        s = self.dsems[sem]
        s[1] += 16
        self.streams[queue].append(lambda e, o=out, i=in_, h=s[0]: e.dma_start(out=o, in_=i).then_inc(h, 16))
        return ("dma_" + sem, s[0], s[1])

    def coll(self, in_t, out_t, deps=(), sem="cc"):
        for d in deps:
            self._wait("pool", d)
        if sem not in self.dsems:
            self.dsems[sem] = [self.nc.alloc_semaphore(name="dma_" + sem), 0]
        s = self.dsems[sem]
        s[1] += 1
        self.streams["pool"].append(lambda e, i=in_t, o=out_t, h=s[0]: e.collective_compute(
            "AllGather", ALU.bypass, replica_groups=RG, ins=[i.ap().opt()], outs=[o.ap().opt()]).then_inc(h))
        return ("dma_" + sem, s[0], s[1])

    def dtoks(self, names):
        return [("dma_" + k, self.dsems[k][0], self.dsems[k][1]) for k in names if k in self.dsems and self.dsems[k][1] > 0]

    def barrier(self):
        toks = [(e, self.psem[e], self.cnt[e]) for e in ENG if self.cnt[e] > 0]
        toks += [("dma_" + k, s[0], s[1]) for k, s in self.dsems.items() if s[1] > 0]
        for e in ENG:
            for t in toks:
                self._wait(e, t)

    def mm(self, out, lhsT, rhs, start, stop, deps=(), signal=False):
        return self.op("pe", lambda e: e.matmul(out, lhsT, rhs, start=start, stop=stop, skip_group_check=True),
                       deps, signal)

    def transpose(self, out, in_, ident, deps=(), signal=True):
        return self.op("pe", lambda e: e.transpose(out, in_, ident), deps, signal)

    def act(self, out, in_, func, deps=(), scale=None, bias=None, accum_out=None):
        kw = {}
        if scale is not None:
            kw["scale"] = scale
        if bias is not None:
            kw["bias"] = bias
        if accum_out is not None:
            kw["accum_out"] = accum_out
        return self.op("act", lambda e: e.activation(out=out, in_=in_, func=func, **kw), deps)

    def tt(self, eng, out, in0, in1, op, deps=()):
        return self.op(eng, lambda e: e.tensor_tensor(out=out, in0=in0, in1=in1, op=op), deps)

    def ts(self, eng, out, in0, s1, s2, op0, op1=None, deps=()):
        if op1 is None:
            return self.op(eng, lambda e: e.tensor_scalar(out=out, in0=in0, scalar1=s1, scalar2=None, op0=op0), deps)
        return self.op(eng, lambda e: e.tensor_scalar(out=out, in0=in0, scalar1=s1, scalar2=s2, op0=op0, op1=op1), deps)

    def stt(self, out, in0, scalar, in1, op0, op1, deps=()):
        return self.op("dve", lambda e: e.scalar_tensor_tensor(out=out, in0=in0, scalar=scalar, in1=in1, op0=op0, op1=op1), deps)

    def recip(self, out, in_, deps=()):
        return self.op("dve", lambda e: e.reciprocal(out=out, in_=in_), deps)

    def copy(self, eng, out, in_, deps=()):
        return self.op(eng, lambda e: e.tensor_copy(out=out, in_=in_), deps)

    def memset(self, ap, val, deps=()):
        return self.op("pool", lambda e: e.memset(ap, val), deps)


class Rot:
    def __init__(self, bufs):
        self.bufs = bufs
        self.free = [[] for _ in bufs]
        self.i = 0

    def next(self):
        i = self.i
        self.i = (i + 1) % len(self.bufs)
        return i, self.bufs[i], list(self.free[i])


def build_program(depth=DEPTH, debug=False):
    nc = bass.Bass("TRN2", target_bir_lowering=False)

    def din(name, shape, dt=F32):
        return nc.dram_tensor(name, list(shape), dt, kind="ExternalInput").ap()

    def dint(name, shape, dt):
        kind = "ExternalOutput" if debug else "Internal"
        return nc.dram_tensor(name, list(shape), dt, kind=kind).ap()

    xT = din("xT", [D, SL])
    ccol = din("ccol", [128, 16])
    ada_w = din("ada_w", [DEPTH, D, 6 * D])
    ada_b = din("ada_b", [128, DEPTH, 96])
    n1g = din("n1g", [128, DEPTH, 16])
    n2g = din("n2g", [128, DEPTH, 16])
    fg = din("fg", [128, 16])
    w_in = din("w_in", [DEPTH, D, DIN])
    lng = din("lng", [DEPTH, 128, 1024])
    lnb = din("lnb", [DEPTH, 128, 1024])
    wsT = din("wsT", [DEPTH, 128, 8, 128])
    bsb = din("bsb", [DEPTH, 128, 1024])
    lamv = din("lamv", [DEPTH, 128, 4, 64])
    subg = din("subg", [DEPTH, 128, 128])
    wupg = din("wupg", [DEPTH, 1024, D])
    wupa = din("wupa", [DEPTH, 1024, D])
    w_o = din("w_o", [DEPTH, D, D])
    w1 = din("w1", [DEPTH, D, FH])
    w3 = din("w3", [DEPTH, D, FH])
    w2 = din("w2", [DEPTH, FH, D])
    cident = din("cident", [128, 128])
    crt = din("crt", [128, 128])
    ccos = din("ccos", [128, SL])
    csin = din("csin", [128, SL])

    yT = nc.dram_tensor("yT", [D, SL], F32, kind="ExternalOutput").ap()
    XS = dint("XS", [D, SL], F32)
    QTd = dint("QTd", [8, 128, SL], BF16)
    KL = [[nc.dram_tensor("KL%d_%d" % (p_, c_), [256, SL], BF16) for c_ in range(4)] for p_ in range(2)]
    KA = [[nc.dram_tensor("KA%d_%d" % (p_, c_), [512, SL], BF16) for c_ in range(4)] for p_ in range(2)]
    VL = [[nc.dram_tensor("VL%d_%d" % (p_, c_), [SL, 256], BF16) for c_ in range(4)] for p_ in range(2)]
    VA = [[nc.dram_tensor("VA%d_%d" % (p_, c_), [2 * SL, 256], BF16) for c_ in range(4)] for p_ in range(2)]
    SAd = dint("SAd", [16, 128, SL], BF16)
    SBd = dint("SBd", [16, 128, SL], BF16)

    P = Prog(nc)

    ARENA = 106400
    arena = nc.alloc_sbuf_tensor("arena", [128, ARENA], BF16)
    psum = nc.alloc_psum_tensor("psum", [128, 4096], F32)

    def carve(off, nel, dt=BF16):
        a = arena[:, off:off + nel]
        if dt == F32:
            a = a.bitcast(F32)
        return a

    A0 = 0
    B0 = 32768
    W0 = B0 + 45056
    M0 = W0 + 16384

    bigA = carve(A0, 32768).rearrange("p (c t) -> p c t", t=2048)
    bigB = carve(B0, 45056).rearrange("p (c t) -> p c t", t=2048)

    def bslot(s0, ns, dt=BF16):
        return carve(B0 + s0 * 2048, ns * 2048, dt)

    wbuf = [carve(W0 + i * 8192, 8192) for i in range(2)]

    mo = [M0]

    def misc(nel_bf16, dt=BF16):
        a = carve(mo[0], nel_bf16, dt)
        mo[0] += nel_bf16
        assert mo[0] <= ARENA, mo[0]
        return a

    modAll = misc(4 * 96 * 2, F32).rearrange("p (l j) -> p l j", j=96)
    geff = misc(4 * 32 * 2, F32).rearrange("p (l j) -> p l j", j=32)
    ident = misc(128)
    RT = misc(128)
    ones = misc(128)
    cact = misc(16)
    small = misc(64 * 2, F32)
    lamt = misc(4 * 64 * 2, F32).rearrange("p (a b) -> p a b", b=64)
    lng_t = misc(1024)
    lnb_t = misc(1024)
    bs_t = misc(1024 * 2, F32)
    ws_t = misc(1024).rearrange("p (g q) -> p g q", q=128)
    subg_t = misc(128 * 2, F32)
    rstd_t = misc(512 * 2, F32)
    XT = [misc(512 * 2, F32) for _ in range(2)]
    sil = [misc(512) for _ in range(2)]
    par_f = misc(128 * 2, F32)

    ps = [psum[:, b * 512:(b + 1) * 512] for b in range(8)]

    neglam = small[:, 0:1]
    rcp = small[:, 1:2]
    ssum = small[:, 2:3]
    rr = small[:, 3:4]
    e1 = small[:, 4:5]
    e2 = small[:, 5:6]
    mv = small[:, 8:10]
    rs = small[:, 10:11]
    stats = small[:, 16:28].rearrange("p (a b) -> p a b", b=6)
    fgt = small[:, 32:48]

    P.dma("pool", ident, cident, sem="c0")
    P.dma("pool", RT, crt, sem="c0")
    P.dma("sp", par_f[:, 0:16], ccol, sem="c1")
    P.dma("sp", fgt, fg, sem="c1")
    P.dma("sp", modAll, ada_b, sem="c1")
    n1t = bslot(0, 1, F32)[:, 0:64].rearrange("p (l j) -> p l j", j=16)
    n2t = bslot(1, 1, F32)[:, 0:64].rearrange("p (l j) -> p l j", j=16)
    P.dma("sp", n1t, n1g, sem="c1")
    P.dma("sp", n2t, n2g, sem="c1")
    P.memset(ones, 1.0)
    P.barrier()
    P.act(cact, par_f[:, 0:16], AF.Silu)
    P.barrier()

    for l in range(depth):
        wfree = [[], []]
        for ci in range(24):
            slot = ci % 2
            wv = wbuf[slot].rearrange("p (k n) -> p k n", n=512)
            t_w = P.dma("pool", wv, ada_w[l][:, ci * 512:(ci + 1) * 512].rearrange("(k p) n -> p k n", p=128),
                        deps=wfree[slot], sem="w%d" % slot)
            for sc in range(4):
                j = ci * 4 + sc
                for k in range(16):
                    t = P.mm(ps[0][:, j:j + 1], wv[:, k, sc * 128:(sc + 1) * 128], cact[:, k:k + 1],
                             start=(j == 0 and k == 0), stop=(k == 15), deps=[t_w], signal=(k == 15))
            wfree[slot] = [t]
        t = P.tt("dve", modAll[:, l, :], ps[0][:, 0:96], modAll[:, l, :], ALU.add, deps=[t])
        P.stt(geff[:, l, 0:16], modAll[:, l, 16:32], 1.0, n1t[:, l, :], ALU.add, ALU.mult, deps=[t])
        P.stt(geff[:, l, 16:32], modAll[:, l, 64:80], 1.0, n2t[:, l, :], ALU.add, ALU.mult, deps=[t])
        P.barrier()

    class WStream:
        def __init__(self):
            self.free = [[], []]
            self.n = 0
            self.fifo = []

        def _issue(self, key, fn):
            slot = self.n % 2
            self.n += 1
            t = None
            for i, (dst, src) in enumerate(fn(wbuf[slot])):
                t = P.dma("pool", dst, src, deps=self.free[slot] if i == 0 else (), sem="w%d" % slot)
            return (key, slot, t)

        def prefetch(self, key, fn):
            self.fifo.append(self._issue(key, fn))

        def take(self, key, fn):
            if self.fifo:
                k, slot, t = self.fifo.pop(0)
                assert k == key, (k, key)
                return slot, t
            _, slot, t = self._issue(key, fn)
            return slot, t

        def release(self, slot, tok):
            self.free[slot] = [tok]

    W = WStream()

    def kv(src):
        return src.rearrange("(k p) n -> p k n", p=128)

    def spec_in(l, ci):
        return ("in", l, ci), lambda wb: [(wb.rearrange("p (k n) -> p k n", n=512), kv(w_in[l][:, ci * 512:(ci + 1) * 512]))]

    def spec_mg(l, ci):
        def fn(wb):
            wv = wb.rearrange("p (a k n) -> p a k n", a=2, n=512)
            return [(wv[:, 0], kv(wupg[l][:, ci * 512:(ci + 1) * 512])), (wv[:, 1], kv(wupa[l][:, ci * 512:(ci + 1) * 512]))]
        return ("mg", l, ci), fn

    def spec_wo(l, ci):
        return ("wo", l, ci), lambda wb: [(wb.rearrange("p (k n) -> p k n", n=512), kv(w_o[l][:, ci * 512:(ci + 1) * 512]))]

    def spec_w13(l, hs, jc):
        def fn(wb):
            col0 = (hs * 22 + jc * 2) * 128
            wv = wb.rearrange("p (a k n) -> p a k n", a=2, n=256)
            return [(wv[:, 0], kv(w1[l][:, col0:col0 + 256])), (wv[:, 1], kv(w3[l][:, col0:col0 + 256]))]
        return ("w13", l, hs, jc), fn

    def spec_w2(l, hs, ci):
        return ("w2", l, hs, ci), lambda wb: [(wb[:, 0:22 * 256].rearrange("p (k n) -> p k n", n=256),
                                               kv(w2[l][hs * 2816:(hs + 1) * 2816, ci * 256:(ci + 1) * 256]))]

    def prefetch2(spec_fn, *args):
        for i in range(2):
            W.prefetch(*spec_fn(*args, i))

    def norm_phase(xsrc, half, g_ap, sh_ap, final=False):
        xg_rot = Rot([bslot(12, 8, F32).rearrange("p (c t) -> p c t", t=512),
                      bslot(0, 8, F32).rearrange("p (c t) -> p c t", t=512)])
        sq_rot = Rot([bslot(20, 1)[:, i * 512:(i + 1) * 512] for i in range(4)])
        tmp_rot = Rot([bslot(21, 1, F32)[:, i * 512:(i + 1) * 512] for i in range(2)])
        out_rot = Rot(XT)
        bank_rot = Rot([ps[0], ps[1]])
        loads = {}

        def load_x(tg_):
            ixg_, xg_, xfree_ = xg_rot.next()
            t0_ = half * 2048 + tg_ * 512
            loads[tg_] = (ixg_, xg_, P.dma("sp", xg_, xsrc[:, t0_:t0_ + 512].rearrange("(c p) t -> p c t", p=128),
                                           deps=xfree_, sem="xg%d" % ixg_))

        load_x(0)
        for tg in range(4):
            tok0 = half * 2048 + tg * 512
            if tg + 1 < 4:
                load_x(tg + 1)
            ixg, xg, t_ld = loads.pop(tg)
            _, bank, bfree = bank_rot.next()
            for c in range(16):
                i, sqb, fr = sq_rot.next()
                t_sq = P.act(sqb, xg[:, c, :], AF.Square, deps=[t_ld] + fr)
                t_mm = P.mm(bank, ones, sqb, start=(c == 0), stop=(c == 15),
                            deps=[t_sq] + (bfree if c == 0 else []), signal=True)
                sq_rot.free[i] = [t_mm]
            t1 = P.ts("dve", rstd_t, bank, 1.0 / D, EPS, ALU.mult, ALU.add, deps=[t_mm])
            bank_rot.free[(bank_rot.i - 1) % 2] = [t1]
            t2 = P.act(rstd_t, rstd_t, AF.Sqrt, deps=[t1])
            t3 = P.recip(rstd_t, rstd_t, deps=[t2])
            last = []
            for c in range(16):
                i, tb, fr = tmp_rot.next()
                t_a = P.tt("dve", tb, xg[:, c, :], rstd_t, ALU.mult, deps=[t3] + fr)
                if not final:
                    t_b = P.act(bigA[:, c, tg * 512:(tg + 1) * 512], tb, AF.Identity, deps=[t_a],
                                scale=g_ap[:, c:c + 1], bias=sh_ap[:, c:c + 1])
                    tmp_rot.free[i] = [t_b]
                    last = [t_a, t_b]
                else:
                    io, ob, fro = out_rot.next()
                    t_b = P.act(ob, tb, AF.Identity, deps=[t_a] + fro, scale=g_ap[:, c:c + 1])
                    tmp_rot.free[i] = [t_b]
                    t_s = P.dma("sp", yT[c * 128:(c + 1) * 128, tok0:tok0 + 512], ob, deps=[t_b], sem="xo%d" % io)
                    out_rot.free[io] = [t_s]
                    last = [t_a, t_b]
            xg_rot.free[ixg] = last
        P.barrier()

    def resid_update(bank, t_mm, c, tok0, xsrc, gt_ap, xrot, bank_rot_free_cb):
        io, xt, fro = xrot.next()
        t_x = P.dma("sp", xt, xsrc[c * 128:(c + 1) * 128, tok0:tok0 + 512], deps=fro, sem="xi%d" % io)
        t_u = P.stt(xt, bank, gt_ap[:, c:c + 1], xt, ALU.mult, ALU.add, deps=[t_mm, t_x])
        bank_rot_free_cb([t_u])
        t_s = P.dma("sp", XS[c * 128:(c + 1) * 128, tok0:tok0 + 512], xt, deps=[t_u], sem="xi%d" % io)
        xrot.free[io] = [t_s]

    for l in range(depth):
        lambda_init = 0.8 - 0.6 * math.exp(-0.3 * l)
        xsrc = xT if l == 0 else XS
        sh1 = modAll[:, l, 0:16]
        gt1 = modAll[:, l, 32:48]
        sh2 = modAll[:, l, 48:64]
        gt2 = modAll[:, l, 80:96]
        g1 = geff[:, l, 0:16]
        g2 = geff[:, l, 16:32]

        par = l % 2
        if l == 0:
            prefetch2(spec_in, l)
        P.dma("pool", lng_t, lng[l], sem="c0")
        P.dma("pool", lnb_t, lnb[l], sem="c0")
        P.dma("pool", ws_t, wsT[l], sem="c0")
        P.dma("sp", bs_t, bsb[l], sem="c1")
        P.dma("sp", subg_t, subg[l], sem="c1")
        P.dma("sp", lamt, lamv[l], sem="c1")
        P.barrier()
        t = P.tt("dve", lamt[:, 0, :], lamt[:, 0, :], lamt[:, 1, :], ALU.mult)
        t = P.op("dve", lambda e: e.reduce_sum(out=e1, in_=lamt[:, 0, :], axis=AX), deps=[t])
        t = P.act(e1, e1, AF.Exp, deps=[t])
        t2 = P.tt("dve", lamt[:, 2, :], lamt[:, 2, :], lamt[:, 3, :], ALU.mult)
        t2 = P.op("dve", lambda e: e.reduce_sum(out=e2, in_=lamt[:, 2, :], axis=AX), deps=[t2])
        t2 = P.act(e2, e2, AF.Exp, deps=[t2])
        t = P.tt("dve", neglam, e2, e1, ALU.subtract, deps=[t, t2])
        t = P.ts("dve", neglam, neglam, -lambda_init, None, ALU.add, deps=[t])
        P.ts("dve", subg_t, subg_t, 1.0 - lambda_init, None, ALU.mult)
        P.barrier()

        for half in range(1):
            hb = half * 2048
            norm_phase(xsrc, half, g1, sh1)
            hT = bigA
            uT = bigB[:, 0:8, :]
            cosb = bslot(8, 1)
            sinb = bslot(10, 1)
            P.dma("pool", cosb[:, 0:SL], ccos, sem="c0")
            P.dma("pool", sinb[:, 0:SL], csin, sem="c0")
            vg = bslot(12, 1, F32)
            tmp2 = bslot(13, 1, F32)
            vn = bslot(14, 1)[:, 0:1024]
            vst_rot = Rot([bslot(14, 1)[:, 1024:2048], bslot(15, 1)[:, 0:1024]])
            stage_rot = Rot([bslot(16, 1)[:, i * 512:(i + 1) * 512] for i in range(4)])
            qb_rot = Rot([bslot(17, 1)[:, i * 512:(i + 1) * 512] for i in range(2)])
            f32_rot = Rot([bslot(18 + i // 2, 1, F32)[:, (i % 2) * 512:(i % 2 + 1) * 512] for i in range(4)])
            P.barrier()
            bank_rot = Rot([ps[0], ps[1], ps[2], ps[3]])
            pair_rot = Rot([psum[:, 2048:3072], psum[:, 3072:4096]])
            t_w = {}
            w_slot = {}

            def load_w(ci):
                w_slot[ci], t_w[ci] = W.take(*spec_in(l, ci))
                return wbuf[w_slot[ci]].rearrange("p (k n) -> p k n", n=512)

            def fm_piece(ci, kind, jbase):
                wv = load_w(ci)
                slot = w_slot[ci]
                rope_pend = []
                for sc in range(4):
                    j = ci * 4 + sc - jbase
                    for tg in range(4):
                        ib, bank, bfree = bank_rot.next()
                        tsl = slice(tg * 512, (tg + 1) * 512)
                        gsl = slice(hb + tg * 512, hb + (tg + 1) * 512)
                        for k in range(16):
                            t_mm = P.mm(bank, wv[:, k, sc * 128:(sc + 1) * 128], hT[:, k, tsl], start=(k == 0),
                                        stop=(k == 15), deps=([t_w[ci]] + bfree) if k == 0 else (), signal=(k == 15))
                        if kind == "u":
                            t = P.act(uT[:, j, tsl], bank, AF.Gelu_apprx_tanh, deps=[t_mm])
                            bank_rot.free[ib] = [t]
                        elif kind in ("ga", "gb"):
                            i4, st, frs = stage_rot.next()
                            t = P.act(st, bank, AF.Sigmoid, deps=[t_mm] + frs)
                            bank_rot.free[ib] = [t]
                            dst = SAd if kind == "ga" else SBd
                            t_s = P.dma("sp", dst[j][:, gsl], st, deps=[t], sem="st%d" % i4)
                            stage_rot.free[i4] = [t_s]
                        else:
                            iq, qb_, frq = qb_rot.next()
                            t_c = P.act(qb_, bank, AF.Copy, deps=[t_mm] + frq)
                            bank_rot.free[ib] = [t_c]

                            def rope_part(iq=iq, qb_=qb_, t_c=t_c, j=j, gsl=gsl):
                                ib2, bank2, bfree2 = bank_rot.next()
                                t_r = P.mm(bank2, RT, qb_, start=True, stop=True, deps=[t_c] + bfree2, signal=True)
                                ia, ta, fra = f32_rot.next()
                                t_a = P.tt("dve", ta, qb_, cosb[:, gsl], ALU.mult, deps=[t_c] + fra)
                                ibb, tb_, frb = f32_rot.next()
                                t_b = P.tt("dve", tb_, bank2, sinb[:, gsl], ALU.mult, deps=[t_r] + frb)
                                bank_rot.free[ib2] = [t_b]
                                qb_rot.free[iq] = [t_r, t_a]
                                i4, st, frs = stage_rot.next()
                                t_o = P.tt("dve", st, ta, tb_, ALU.add, deps=[t_a, t_b] + frs)
                                f32_rot.free[ia] = [t_o]
                                f32_rot.free[ibb] = [t_o]
                                if kind == "q":
                                    dst_ap = QTd[j][:, gsl]
                                else:
                                    dst_ap = KL[par][j // 2][(j % 2) * 128:(j % 2 + 1) * 128, gsl]
                                t_s = P.dma("sp", dst_ap, st, deps=[t_o], sem="st%d" % i4)
                                stage_rot.free[i4] = [t_s]

                            while rope_pend:
                                rope_pend.pop(0)()
                            rope_pend.append(rope_part)
                while rope_pend:
                    rope_pend.pop(0)()
                W.release(slot, t_mm)

            def tm_v(ci0):
                wvs = [load_w(ci0), load_w(ci0 + 1)]
                sets = [(bslot(12, 1, F32), bslot(13, 1, F32), bslot(14, 1)[:, 0:1024]),
                        (bslot(9, 1, F32), bslot(11, 1, F32), bslot(20, 1)[:, 0:1024])]
                set_free = [[], []]
                mainb, spb = pair_rot.bufs[0], pair_rot.bufs[1]
                fr = {"main": list(pair_rot.free[0]), "sp": list(pair_rot.free[1])}
                t_mm = None

                def sp_part(tt_, tmp2_, vn_, t_n_):
                    tsl_ = slice(tt_ * 128, (tt_ + 1) * 128)
                    for g in range(8):
                        t_sp = P.mm(spb[:, g * 128:(g + 1) * 128], vn_[:, g * 128:(g + 1) * 128], ws_t[:, g, :],
                                    start=True, stop=True, deps=([t_n_] + fr["sp"]) if g == 0 else (), signal=(g == 7))
                    t_m = P.tt("dve", tmp2_, spb, bs_t, ALU.add, deps=[t_sp])
                    fr["sp"] = [t_m]
                    t_u = P.tt("dve", uT[:, :, tsl_], tmp2_.rearrange("p (g q) -> p g q", q=128), uT[:, :, tsl_],
                               ALU.mult, deps=[t_m])
                    set_free[tt_ % 2] = [t_u]

                pend = None
                for tt_ in range(16):
                    vg_, tmp2_, vn_ = sets[tt_ % 2]
                    tsl = slice(tt_ * 128, (tt_ + 1) * 128)
                    for cc in range(2):
                        for k in range(16):
                            first = (cc == 0 and k == 0)
                            t_mm = P.mm(mainb[:, cc * 512:(cc + 1) * 512], hT[:, k, tsl], wvs[cc][:, k, :],
                                        start=(k == 0), stop=(k == 15),
                                        deps=([t_w[ci0], t_w[ci0 + 1]] + fr["main"]) if first else (),
                                        signal=(cc == 1 and k == 15))
                    t_g = P.act(vg_, mainb, AF.Gelu_apprx_tanh, deps=[t_mm] + set_free[tt_ % 2])
                    fr["main"] = [t_g]
                    ta_ = P.op("dve", lambda e, v=vg_: e.bn_stats(out=stats[:, 0, :], in_=v[:, 0:512]), deps=[t_g])
                    tb2 = P.op("dve", lambda e, v=vg_: e.bn_stats(out=stats[:, 1, :], in_=v[:, 512:1024]), deps=[t_g])
                    t_ag = P.op("dve", lambda e: e.bn_aggr(out=mv, in_=small[:, 16:28]), deps=[ta_, tb2])
                    t_r1 = P.ts("dve", rs, mv[:, 1:2], EPS, None, ALU.add, deps=[t_ag])
                    t_r2 = P.act(rs, rs, AF.Sqrt, deps=[t_r1])
                    t_r3 = P.recip(rs, rs, deps=[t_r2])
                    t_n = P.ts("dve", vg_, vg_, mv[:, 0:1], rs, ALU.subtract, ALU.mult, deps=[t_r3])
                    t_n = P.tt("dve", vg_, vg_, lng_t, ALU.mult, deps=[t_n])
                    t_n = P.tt("dve", vn_, vg_, lnb_t, ALU.add, deps=[t_n])
                    if pend is not None:
                        sp_part(*pend)
                    pend = (tt_, tmp2_, vn_, t_n)
                sp_part(*pend)
                pair_rot.free[0] = fr["main"]
                pair_rot.free[1] = fr["sp"]
                pair_rot.i = 0
                W.release(w_slot[ci0], t_mm)
                W.release(w_slot[ci0 + 1], t_mm)

            def tm_piece(ci0, kind):
                if kind == "v":
                    return tm_v(ci0)
                wv0 = load_w(ci0)
                wv1 = load_w(ci0 + 1)
                wvs = [wv0, wv1]
                t_u_last = None
                for tt_ in range(16):
                    ip, pair, pfree = pair_rot.next()
                    tsl = slice(tt_ * 128, (tt_ + 1) * 128)
                    for cc in range(2):
                        for k in range(16):
                            first = (cc == 0 and k == 0)
                            t_mm = P.mm(pair[:, cc * 512:(cc + 1) * 512], hT[:, k, tsl], wvs[cc][:, k, :],
                                        start=(k == 0), stop=(k == 15),
                                        deps=([t_w[ci0], t_w[ci0 + 1]] + pfree) if first else (),
                                        signal=(cc == 1 and k == 15))
                    if kind == "va":
                        iv, vs, frv = vst_rot.next()
                        t = P.act(vs, pair, AF.Copy, deps=[t_mm] + frv)
                        pair_rot.free[ip] = [t]
                        for c_ in range(4):
                            t_s = P.dma("sp", VL[par][c_][tt_ * 128:(tt_ + 1) * 128, :], vs[:, c_ * 256:(c_ + 1) * 256],
                                        deps=[t], sem="vs%d" % iv)
                        vst_rot.free[iv] = [t_s]
                    else:
                        t_g = P.act(vg, pair, AF.Gelu_apprx_tanh, deps=[t_mm] + ([t_u_last] if t_u_last else []))
                        pair_rot.free[ip] = [t_g]
                        ta_ = P.op("dve", lambda e: e.bn_stats(out=stats[:, 0, :], in_=vg[:, 0:512]), deps=[t_g])
                        tb2 = P.op("dve", lambda e: e.bn_stats(out=stats[:, 1, :], in_=vg[:, 512:1024]), deps=[t_g])
                        t_ag = P.op("dve", lambda e: e.bn_aggr(out=mv, in_=small[:, 16:28]), deps=[ta_, tb2])
                        t_r1 = P.ts("dve", rs, mv[:, 1:2], EPS, None, ALU.add, deps=[t_ag])
                        t_r2 = P.act(rs, rs, AF.Sqrt, deps=[t_r1])
                        t_r3 = P.recip(rs, rs, deps=[t_r2])
                        t_n = P.ts("dve", vg, vg, mv[:, 0:1], rs, ALU.subtract, ALU.mult, deps=[t_r3])
                        t_n = P.tt("dve", vg, vg, lng_t, ALU.mult, deps=[t_n])
                        t_n = P.tt("dve", vn, vg, lnb_t, ALU.add, deps=[t_n] + ([t_u_last] if t_u_last else []))
                        ip2, pair2, pfree2 = pair_rot.next()
                        for g in range(8):
                            t_sp = P.mm(pair2[:, g * 128:(g + 1) * 128], vn[:, g * 128:(g + 1) * 128], ws_t[:, g, :],
                                        start=True, stop=True, deps=([t_n] + pfree2) if g == 0 else (), signal=(g == 7))
                        t_m = P.tt("dve", tmp2, pair2, bs_t, ALU.add, deps=[t_sp])
                        pair_rot.free[ip2] = [t_m]
                        t_u_last = P.tt("dve", uT[:, :, tsl], tmp2.rearrange("p (g q) -> p g q", q=128), uT[:, :, tsl],
                                        ALU.mult, deps=[t_m])
                W.release(w_slot[ci0], t_mm)
                W.release(w_slot[ci0 + 1], t_mm)

            fm_piece(0, "u", 0)
            fm_piece(1, "u", 0)
            tm_piece(2, "v")
            fm_piece(4, "q", 16)
            fm_piece(5, "q", 16)
            fm_piece(6, "k", 24)
            fm_piece(7, "k", 24)
            tm_piece(8, "va")
            for ci in range(10, 14):
                if ci == 10:
                    kdeps = P.dtoks(["st0", "st1", "st2", "st3"])
                    for c_ in range(4):
                        P.coll(KL[par][c_], KA[par][c_], deps=kdeps, sem="cc")
                if ci == 12:
                    vdeps = P.dtoks(["vs0", "vs1"])
                    for c_ in range(4):
                        P.coll(VL[par][c_], VA[par][c_], deps=vdeps, sem="cc")
                fm_piece(ci, "ga", 40)
            for ci in range(14, 18):
                fm_piece(ci, "gb", 56)
            P.barrier()

        def aslot(s0, ns, dt=BF16):
            return carve(A0 + s0 * 2048, ns * 2048, dt)

        KTa = [aslot(0, 2), aslot(2, 2)]
        KTb = [aslot(4, 2), aslot(6, 2)]
        QTb = [aslot(8, 1), aslot(9, 1)]
        Vau = [aslot(10, 3)[:, 0:32 * 129].rearrange("p (k d) -> p k d", d=129),
               aslot(13, 3)[:, 0:32 * 129].rearrange("p (k d) -> p k d", d=129)]
        PT_rot = Rot([bslot(18, 1)[:, i * 512:(i + 1) * 512] for i in range(4)])
        onorm = [[bslot(19, 1, F32)[:, (m * 4 + j) * 128:(m * 4 + j + 1) * 128] for j in range(4)] for m in range(2)]
        ocomb = bslot(20, 1, F32)[:, 0:128]
        junk = bslot(20, 1, F32)[:, 128:256]
        obf = bslot(20, 1)[:, 512:640]
        brB_w = bigB[:, 8:16, :]
        prefetch2(spec_mg, l)
        for b in range(2):
            P.memset(KTa[b][64:128, :], 0.0)
            P.memset(KTb[b][0:64, :], 0.0)
            P.memset(Vau[b][:, :, 128:129], 1.0)
        P.barrier()
        S_rot = Rot([ps[0], ps[1], ps[2]])
        O_rot = Rot([psum[:, 1536:2560], psum[:, 2560:3584]])
        psT = ps[7].bitcast(BF16)
        psT_free = []
        hfree = [[], []]
        on_free = [[[] for _ in range(4)] for _ in range(2)]
        for h in range(8):
            buf = h % 2
            hc, hl = h // 2, h % 2
            tl = []
            for r_ in range(2):
                r0 = r_ * 256 + hl * 128
                tl.append(P.dma("sp", KTa[buf][0:64, r_ * SL:(r_ + 1) * SL], KA[par][hc][r0:r0 + 64, :],
                                deps=hfree[buf] if r_ == 0 else (), sem="hk%d" % buf))
                tl.append(P.dma("sp", KTb[buf][64:128, r_ * SL:(r_ + 1) * SL], KA[par][hc][r0 + 64:r0 + 128, :],
                                sem="hk%d" % buf))
            tl.append(P.dma("sp", QTb[buf], QTd[h], sem="hk%d" % buf))
            for r_ in range(2):
                tl.append(P.dma("sp", Vau[buf][:, r_ * 16:(r_ + 1) * 16, 0:128],
                                VA[par][hc][r_ * SL:(r_ + 1) * SL, hl * 128:(hl + 1) * 128].rearrange(
                                    "(k p) d -> p k d", p=128), sem="hk%d" % buf))
            t_ld = tl[-1]
            last_pe = None
            for qg in range(SL // 512):
                qsl = slice(qg * 512, (qg + 1) * 512)
                for m in range(2):
                    KTm = KTa[buf] if m == 0 else KTb[buf]
                    io, Ob, ofree = O_rot.next()
                    Oj = [Ob[:, (j // 2) * 512 + (j % 2) * 129:(j // 2) * 512 + (j % 2) * 129 + 129] for j in range(4)]
                    sinfo = {}

                    def issue_S(kc):
                        ib, bank, bfree = S_rot.next()
                        t = P.mm(bank, KTm[:, kc * 128:(kc + 1) * 128], QTb[buf][:, qsl], start=True, stop=True,
                                 deps=[t_ld] + bfree, signal=True)
                        sinfo[kc] = (ib, bank, t)

                    issue_S(0)
                    issue_S(1)
                    for kc in range(32):
                        ib, bank, t_S = sinfo.pop(kc)
                        ipt, pt, frp = PT_rot.next()
                        t_e = P.act(pt, bank, AF.Exp, deps=[t_S] + frp, scale=0.125)
                        S_rot.free[ib] = [t_e]
                        if kc + 2 < 32:
                            issue_S(kc + 2)
                        for j in range(4):
                            t_pv = P.mm(Oj[j], pt[:, j * 128:(j + 1) * 128], Vau[buf][:, kc, :],
                                        start=(kc == 0 and j % 2 == 0), stop=(kc == 31),
                                        deps=([t_e] + (ofree if kc == 0 else [])) if j == 0 else (), signal=(j == 3))
                        PT_rot.free[ipt] = [t_pv]
                    last_pe = t_pv
                    t_on = None
                    for j in range(4):
                        t_r = P.recip(rcp, Oj[j][:, 128:129], deps=[t_pv] + on_free[m][j])
                        t_on = P.ts("dve", onorm[m][j], Oj[j][:, 0:128], rcp, None, ALU.mult, deps=[t_r])
                    O_rot.free[io] = [t_on]
                    if m == 1:
                        for j in range(4):
                            t_c = P.stt(ocomb, onorm[1][j], neglam, onorm[0][j], ALU.mult, ALU.add, deps=[t_on])
                            on_free[0][j] = [t_c]
                            on_free[1][j] = [t_c]
                            t_q = P.tt("dve", junk, ocomb, ocomb, ALU.mult, deps=[t_c])
                            t_q = P.op("dve", lambda e: e.reduce_sum(out=ssum, in_=junk, axis=AX), deps=[t_q])
                            t_q = P.ts("dve", rr, ssum, 1.0 / 128, EPS, ALU.mult, ALU.add, deps=[t_q])
                            t_q = P.act(rr, rr, AF.Ln, deps=[t_q])
                            t_q = P.act(rr, rr, AF.Exp, deps=[t_q], scale=-0.5)
                            t_ob = P.stt(obf, ocomb, rr, subg_t, ALU.mult, ALU.mult, deps=[t_q])
                            t_tr = P.transpose(psT[:, j * 128:(j + 1) * 128], obf, ident,
                                               deps=[t_ob] + (psT_free if j == 0 else []))
                            t_on = t_tr
                        t_cp = P.copy("dve", brB_w[:, h, qsl], psT[:, 0:512], deps=[t_tr])
                        psT_free = [t_cp]
            hfree[buf] = [last_pe]
        P.barrier()

        for half in range(1):
            hb = half * 2048
            brA = bigB[:, 0:8, :]
            brB = bigB[:, 8:16, :]
            merged = bigA
            sa_rot = Rot([bslot(16, 1), bslot(17, 1)])
            sb_rot = Rot([bslot(18, 1), bslot(19, 1)])
            t1_rot = Rot([bslot(20, 1, F32)[:, i * 512:(i + 1) * 512] for i in range(2)])
            t2_rot = Rot([bslot(21, 1, F32)[:, i * 512:(i + 1) * 512] for i in range(2)])
            bank_rot = Rot([ps[0], ps[1], ps[2], ps[3], ps[4], ps[5]])
            for ci in range(4):
                slot, t_w = W.take(*spec_mg(l, ci))
                wv = wbuf[slot].rearrange("p (a k n) -> p a k n", a=2, n=512)
                for sc in range(4):
                    c = ci * 4 + sc
                    isa, sat, frsa = sa_rot.next()
                    isb, sbt, frsb = sb_rot.next()
                    t_sa = P.dma("sp", sat, SAd[c][:, hb:hb + 2048], deps=frsa, sem="sa%d" % isa)
                    t_sb = P.dma("sp", sbt, SBd[c][:, hb:hb + 2048], deps=frsb, sem="sb%d" % isb)
                    for tg in range(4):
                        tsl = slice(tg * 512, (tg + 1) * 512)
                        ia_, bka, fa = bank_rot.next()
                        for k in range(8):
                            t_ma = P.mm(bka, wv[:, 0, k, sc * 128:(sc + 1) * 128], brA[:, k, tsl], start=(k == 0),
                                        stop=(k == 7), deps=([t_w] + fa) if k == 0 else (), signal=(k == 7))
                        ib_, bkb, fb = bank_rot.next()
                        for k in range(8):
                            t_mb = P.mm(bkb, wv[:, 1, k, sc * 128:(sc + 1) * 128], brB[:, k, tsl], start=(k == 0),
                                        stop=(k == 7), deps=fb if k == 0 else (), signal=(k == 7))
                        i1, t1b, f1 = t1_rot.next()
                        i2, t2b, f2 = t2_rot.next()
                        ta_ = P.tt("dve", t1b, bka, sat[:, tsl], ALU.mult, deps=[t_ma, t_sa] + f1)
                        tb2 = P.tt("dve", t2b, bkb, sbt[:, tsl], ALU.mult, deps=[t_mb, t_sb] + f2)
                        bank_rot.free[ia_] = [ta_]
                        bank_rot.free[ib_] = [tb2]
                        t_m = P.tt("pool", merged[:, c, tsl], t1b, t2b, ALU.add, deps=[ta_, tb2])
                        t1_rot.free[i1] = [t_m]
                        t2_rot.free[i2] = [t_m]
                    sa_rot.free[isa] = [ta_]
                    sb_rot.free[isb] = [tb2]
                W.release(slot, t_mb)
            prefetch2(spec_wo, l)
            P.barrier()
            xrot = Rot(XT + [bslot(i // 2, 1, F32)[:, (i % 2) * 512:(i % 2 + 1) * 512] for i in range(4)])
            for ci in range(4):
                slot, t_w = W.take(*spec_wo(l, ci))
                wv = wbuf[slot].rearrange("p (k n) -> p k n", n=512)
                for sc in range(4):
                    c = ci * 4 + sc
                    for tg in range(4):
                        tsl = slice(tg * 512, (tg + 1) * 512)
                        ib, bank, bfree = bank_rot.next()
                        for k in range(16):
                            t_mm = P.mm(bank, wv[:, k, sc * 128:(sc + 1) * 128], merged[:, k, tsl], start=(k == 0),
                                        stop=(k == 15), deps=([t_w] + bfree) if k == 0 else (), signal=(k == 15))

                        def cb(toks, ib=ib):
                            bank_rot.free[ib] = toks
                        resid_update(bank, t_mm, c, hb + tg * 512, xsrc, gt1, xrot, cb)
                W.release(slot, t_mm)
            prefetch2(spec_w13, l, 0)
            P.barrier()

        for half in range(1):
            hb = half * 2048
            norm_phase(XS, half, g2, sh2)
            hT = bigA
            hid = bigB
            bank_rot = Rot([ps[0], ps[1], ps[2], ps[3], ps[4], ps[5]])
            sil_rot = Rot(sil)
            xrot = Rot(XT)
            for hs in range(2):
                for jc in range(11):
                    slot, t_w = W.take(*spec_w13(l, hs, jc))
                    wv = wbuf[slot].rearrange("p (a k n) -> p a k n", a=2, n=256)
                    for sc in range(2):
                        jj = jc * 2 + sc
                        for tg in range(4):
                            tsl = slice(tg * 512, (tg + 1) * 512)
                            i1, b1, f1 = bank_rot.next()
                            for k in range(16):
                                t_m1 = P.mm(b1, wv[:, 0, k, sc * 128:(sc + 1) * 128], hT[:, k, tsl], start=(k == 0),
                                            stop=(k == 15), deps=([t_w] + f1) if k == 0 else (), signal=(k == 15))
                            i3, b3, f3 = bank_rot.next()
                            for k in range(16):
                                t_m3 = P.mm(b3, wv[:, 1, k, sc * 128:(sc + 1) * 128], hT[:, k, tsl], start=(k == 0),
                                            stop=(k == 15), deps=f3 if k == 0 else (), signal=(k == 15))
                            isl, sb_, fs = sil_rot.next()
                            t_s = P.act(sb_, b1, AF.Silu, deps=[t_m1] + fs)
                            bank_rot.free[i1] = [t_s]
                            t_h = P.tt("dve", hid[:, jj, tsl], b3, sb_, ALU.mult, deps=[t_m3, t_s])
                            bank_rot.free[i3] = [t_h]
                            sil_rot.free[isl] = [t_h]
                    W.release(slot, t_m3)
                prefetch2(spec_w2, l, hs)
                P.barrier()
                for ci in range(8):
                    slot, t_w = W.take(*spec_w2(l, hs, ci))
                    wv = wbuf[slot][:, 0:22 * 256].rearrange("p (k n) -> p k n", n=256)
                    for sc in range(2):
                        c = ci * 2 + sc
                        for tg in range(4):
                            tsl = slice(tg * 512, (tg + 1) * 512)
                            ib, bank, bfree = bank_rot.next()
                            for k in range(22):
                                t_mm = P.mm(bank, wv[:, k, sc * 128:(sc + 1) * 128], hid[:, k, tsl], start=(k == 0),
                                            stop=(k == 21), deps=([t_w] + bfree) if k == 0 else (), signal=(k == 21))

                            def cb(toks, ib=ib):
                                bank_rot.free[ib] = toks
                            resid_update(bank, t_mm, c, hb + tg * 512, XS, gt2, xrot, cb)
                    W.release(slot, t_mm)
                if hs == 0:
                    prefetch2(spec_w13, l, 1)
                elif l + 1 < depth:
                    prefetch2(spec_in, l + 1)
                P.barrier()

    for half in range(1):
        norm_phase(XS, half, fgt, None, final=True)
    P.barrier()

    with nc.Block() as block:
        @block.tensor
        def _(e):
            for f in P.streams["pe"]:
                f(e)

        @block.scalar
        def _(e):
            for f in P.streams["act"]:
                f(e)

        @block.vector
        def _(e):
            for f in P.streams["dve"]:
                f(e)

        @block.gpsimd
        def _(e):
            for f in P.streams["pool"]:
                f(e)

        @block.sync
        def _(e):
            for f in P.streams["sp"]:
                f(e)
    return nc


def rope_consts():
    j = np.arange(32, dtype=np.float32)
    inv_freq = (10000.0 ** (-(2.0 * j) / 64.0)).astype(np.float32)
    pos = np.arange(S, dtype=np.float32)
    ang = pos[None, :] * inv_freq[:, None]
    cos = np.cos(ang).astype(np.float32)
    sin = np.sin(ang).astype(np.float32)
    cosT = np.tile(cos, (4, 1))
    sinT = np.tile(sin, (4, 1))
    rt = np.zeros((128, 128), np.float32)
    for m in range(2):
        for jj in range(32):
            i0 = m * 64 + jj
            i1 = m * 64 + 32 + jj
            rt[i1, i0] = -1.0
            rt[i0, i1] = 1.0
    return cosT, sinT, rt, np.eye(128, dtype=np.float32)


def make_in_maps(x, c, ada_w, ada_b, norm1_g, norm2_g, w_in, gmlp_ln_g, gmlp_ln_b, w_s, b_s,
                 lambda_q1, lambda_k1, lambda_q2, lambda_k2, subln_g, w_up_gmlp, w_up_attn, w_o,
                 ffn_w1, ffn_w3, ffn_w2, final_g):
    f = lambda a: np.ascontiguousarray(np.asarray(a, dtype=np.float32))
    cosT, sinT, rt, ident = rope_consts()
    L = DEPTH
    shared = {
        "ada_w": f(ada_w),
        "ada_b": f(np.asarray(ada_b).reshape(L, 96, 128).transpose(2, 0, 1)),
        "n1g": f(np.asarray(norm1_g).reshape(L, 16, 128).transpose(2, 0, 1)),
        "n2g": f(np.asarray(norm2_g).reshape(L, 16, 128).transpose(2, 0, 1)),
        "fg": f(np.asarray(final_g).reshape(16, 128).T),
        "w_in": f(w_in),
        "lng": f(np.broadcast_to(np.asarray(gmlp_ln_g)[:, None, :], (L, 128, 1024))),
        "lnb": f(np.broadcast_to(np.asarray(gmlp_ln_b)[:, None, :], (L, 128, 1024))),
        "wsT": f(np.asarray(w_s).transpose(0, 3, 1, 2)),
        "bsb": f(np.broadcast_to(np.asarray(b_s).reshape(L, 1, 1024), (L, 128, 1024))),
        "lamv": f(np.broadcast_to(np.stack([np.asarray(lambda_q1), np.asarray(lambda_k1), np.asarray(lambda_q2),
                                            np.asarray(lambda_k2)], axis=1)[:, None], (L, 128, 4, 64))),
        "subg": f(np.broadcast_to(np.asarray(subln_g)[:, None, :], (L, 128, 128))),
        "wupg": f(w_up_gmlp), "wupa": f(w_up_attn), "w_o": f(w_o),
        "w1": f(ffn_w1), "w3": f(ffn_w3), "w2": f(ffn_w2),
        "cident": ident, "crt": rt,
    }
    x = np.asarray(x, dtype=np.float32)
    c = np.asarray(c, dtype=np.float32)
    in_maps = []
    for core in range(NCORES):
        b, r = core // 2, core % 2
        m = dict(shared)
        m["xT"] = np.ascontiguousarray(x[b, r * SL:(r + 1) * SL, :].T)
        m["ccol"] = np.ascontiguousarray(c[b].reshape(16, 128).T)
        m["ccos"] = np.ascontiguousarray(cosT[:, r * SL:(r + 1) * SL])
        m["csin"] = np.ascontiguousarray(sinT[:, r * SL:(r + 1) * SL])
        in_maps.append(m)
    return in_maps


def kernel(**inputs):
    in_maps = make_in_maps(**inputs)
    nc = build_program()
    res = run_bass_kernel_spmd(nc, in_maps, core_ids=list(range(NCORES)))
    out = np.empty((NCORES // 2, S, D), np.float32)
    for core in range(NCORES):
        b, r = core // 2, core % 2
        out[b, r * SL:(r + 1) * SL, :] = np.asarray(res.results[core]["yT"]).T
    return out
```

```python
import math
import numpy as np
import concourse.bass as bass
import concourse.mybir as mybir
from concourse.bass_utils import run_bass_kernel_spmd

F32 = mybir.dt.float32
BF16 = mybir.dt.bfloat16
AF = mybir.ActivationFunctionType
ALU = mybir.AluOpType
AX = mybir.AxisListType.X

D = 2048
S = 4096
SL = 2048
DEPTH = 4
DIN = 9216
FH = 5632
NCORES = 8
RG = [[0, 1], [2, 3], [4, 5], [6, 7]]
EPS = 1e-6
ENG = ["pe", "act", "dve", "pool", "sp"]


class Prog:
    def __init__(self, nc):
        self.nc = nc
        self.streams = {e: [] for e in ENG}
        self.psem = {e: nc.alloc_semaphore(name="prog_" + e) for e in ENG}
        self.cnt = {e: 0 for e in ENG}
        self.known = {e: {} for e in ENG}
        self.dsems = {}

    def _wait(self, eng, tok):
        if tok is None:
            return
        key, handle, val = tok
        if self.known[eng].get(key, 0) >= val:
            return
        self.known[eng][key] = val
        self.streams[eng].append(lambda e, h=handle, v=val: e.wait_ge(h, v))

    def op(self, eng, fn, deps=(), signal=True):
        for d in deps:
            self._wait(eng, d)
        if signal:
            self.cnt[eng] += 1
            h = self.psem[eng]
            self.streams[eng].append(lambda e, fn=fn, h=h: fn(e).then_inc(h, 1))
            return (eng, h, self.cnt[eng])
        self.streams[eng].append(lambda e, fn=fn: fn(e))
        return None

    def dma(self, queue, out, in_, deps=(), sem="d"):
        for d in deps:
            self._wait(queue, d)
        if sem not in self.dsems:
            self.dsems[sem] = [self.nc.alloc_semaphore(name="dma_" + sem), 0]
        s = self.dsems[sem]
        s[1] += 16
        self.streams[queue].append(lambda e, o=out, i=in_, h=s[0]: e.dma_start(out=o, in_=i).then_inc(h, 16))
        return ("dma_" + sem, s[0], s[1])

    def coll(self, in_t, out_t, deps=(), sem="cc"):
        for d in deps:
            self._wait("pool", d)
        if sem not in self.dsems:
            self.dsems[sem] = [self.nc.alloc_semaphore(name="dma_" + sem), 0]
        s = self.dsems[sem]
        s[1] += 1
        self.streams["pool"].append(lambda e, i=in_t, o=out_t, h=s[0]: e.collective_compute(
            "AllGather", ALU.bypass, replica_groups=RG, ins=[i.ap().opt()], outs=[o.ap().opt()]).then_inc(h))
        return ("dma_" + sem, s[0], s[1])

    def dtoks(self, names):
        return [("dma_" + k, self.dsems[k][0], self.dsems[k][1]) for k in names if k in self.dsems and self.dsems[k][1] > 0]

    def barrier(self):
        toks = [(e, self.psem[e], self.cnt[e]) for e in ENG if self.cnt[e] > 0]
        toks += [("dma_" + k, s[0], s[1]) for k, s in self.dsems.items() if s[1] > 0]
        for e in ENG:
            for t in toks:
                self._wait(e, t)

    def mm(self, out, lhsT, rhs, start, stop, deps=(), signal=False):
        return self.op("pe", lambda e: e.matmul(out, lhsT, rhs, start=start, stop=stop, skip_group_check=True),
                       deps, signal)

    def transpose(self, out, in_, ident, deps=(), signal=True):
        return self.op("pe", lambda e: e.transpose(out, in_, ident), deps, signal)

    def act(self, out, in_, func, deps=(), scale=None, bias=None, accum_out=None):
        kw = {}
        if scale is not None:
            kw["scale"] = scale
        if bias is not None:
            kw["bias"] = bias
        if accum_out is not None:
            kw["accum_out"] = accum_out
        return self.op("act", lambda e: e.activation(out=out, in_=in_, func=func, **kw), deps)

    def tt(self, eng, out, in0, in1, op, deps=()):
        return self.op(eng, lambda e: e.tensor_tensor(out=out, in0=in0, in1=in1, op=op), deps)

    def ts(self, eng, out, in0, s1, s2, op0, op1=None, deps=()):
        if op1 is None:
            return self.op(eng, lambda e: e.tensor_scalar(out=out, in0=in0, scalar1=s1, scalar2=None, op0=op0), deps)
        return self.op(eng, lambda e: e.tensor_scalar(out=out, in0=in0, scalar1=s1, scalar2=s2, op0=op0, op1=op1), deps)

    def stt(self, out, in0, scalar, in1, op0, op1, deps=()):
        return self.op("dve", lambda e: e.scalar_tensor_tensor(out=out, in0=in0, scalar=scalar, in1=in1, op0=op0, op1=op1), deps)

    def recip(self, out, in_, deps=()):
        return self.op("dve", lambda e: e.reciprocal(out=out, in_=in_), deps)

    def copy(self, eng, out, in_, deps=()):
        return self.op(eng, lambda e: e.tensor_copy(out=out, in_=in_), deps)

    def memset(self, ap, val, deps=()):
        return self.op("pool", lambda e: e.memset(ap, val), deps)


class Rot:
    def __init__(self, bufs):
        self.bufs = bufs
        self.free = [[] for _ in bufs]
        self.i = 0

    def next(self):
        i = self.i
        self.i = (i + 1) % len(self.bufs)
        return i, self.bufs[i], list(self.free[i])


def build_program(depth=DEPTH, debug=False):
    nc = bass.Bass("TRN2", target_bir_lowering=False)

    def din(name, shape, dt=F32):
        return nc.dram_tensor(name, list(shape), dt, kind="ExternalInput").ap()

    def dint(name, shape, dt):
        kind = "ExternalOutput" if debug else "Internal"
        return nc.dram_tensor(name, list(shape), dt, kind=kind).ap()

    xT = din("xT", [D, SL])
    ccol = din("ccol", [128, 16])
    ada_w = din("ada_w", [DEPTH, D, 6 * D])
    ada_b = din("ada_b", [128, DEPTH, 96])
    n1g = din("n1g", [128, DEPTH, 16])
    n2g = din("n2g", [128, DEPTH, 16])
    fg = din("fg", [128, 16])
    w_in = din("w_in", [DEPTH, D, DIN])
    lng = din("lng", [DEPTH, 128, 1024])
    lnb = din("lnb", [DEPTH, 128, 1024])
    wsT = din("wsT", [DEPTH, 128, 8, 128])
    bsb = din("bsb", [DEPTH, 128, 1024])
    lamv = din("lamv", [DEPTH, 128, 4, 64])
    subg = din("subg", [DEPTH, 128, 128])
    wupg = din("wupg", [DEPTH, 1024, D])
    wupa = din("wupa", [DEPTH, 1024, D])
    w_o = din("w_o", [DEPTH, D, D])
    w1 = din("w1", [DEPTH, D, FH])
    w3 = din("w3", [DEPTH, D, FH])
    w2 = din("w2", [DEPTH, FH, D])
    cident = din("cident", [128, 128])
    crt = din("crt", [128, 128])
    ccos = din("ccos", [128, SL])
    csin = din("csin", [128, SL])

    yT = nc.dram_tensor("yT", [D, SL], F32, kind="ExternalOutput").ap()
    XS = dint("XS", [D, SL], F32)
    QTd = dint("QTd", [8, 128, SL], BF16)
    KL = [[nc.dram_tensor("KL%d_%d" % (p_, c_), [256, SL], BF16) for c_ in range(4)] for p_ in range(2)]
    KA = [[nc.dram_tensor("KA%d_%d" % (p_, c_), [512, SL], BF16) for c_ in range(4)] for p_ in range(2)]
    VL = [[nc.dram_tensor("VL%d_%d" % (p_, c_), [SL, 256], BF16) for c_ in range(4)] for p_ in range(2)]
    VA = [[nc.dram_tensor("VA%d_%d" % (p_, c_), [2 * SL, 256], BF16) for c_ in range(4)] for p_ in range(2)]
    SAd = dint("SAd", [16, 128, SL], BF16)
    SBd = dint("SBd", [16, 128, SL], BF16)

    P = Prog(nc)

    ARENA = 106400
    arena = nc.alloc_sbuf_tensor("arena", [128, ARENA], BF16)
    psum = nc.alloc_psum_tensor("psum", [128, 4096], F32)

    def carve(off, nel, dt=BF16):
        a = arena[:, off:off + nel]
        if dt == F32:
            a = a.bitcast(F32)
        return a

    A0 = 0
    B0 = 32768
    W0 = B0 + 45056
    M0 = W0 + 16384

    bigA = carve(A0, 32768).rearrange("p (c t) -> p c t", t=2048)
    bigB = carve(B0, 45056).rearrange("p (c t) -> p c t", t=2048)

    def bslot(s0, ns, dt=BF16):
        return carve(B0 + s0 * 2048, ns * 2048, dt)

    wbuf = [carve(W0 + i * 8192, 8192) for i in range(2)]

    mo = [M0]

    def misc(nel_bf16, dt=BF16):
        a = carve(mo[0], nel_bf16, dt)
        mo[0] += nel_bf16
        assert mo[0] <= ARENA, mo[0]
        return a

    modAll = misc(4 * 96 * 2, F32).rearrange("p (l j) -> p l j", j=96)
    geff = misc(4 * 32 * 2, F32).rearrange("p (l j) -> p l j", j=32)
    ident = misc(128)
    RT = misc(128)
    ones = misc(128)
    cact = misc(16)
    small = misc(64 * 2, F32)
    lamt = misc(4 * 64 * 2, F32).rearrange("p (a b) -> p a b", b=64)
    lng_t = misc(1024)
    lnb_t = misc(1024)
    bs_t = misc(1024 * 2, F32)
    ws_t = misc(1024).rearrange("p (g q) -> p g q", q=128)
    subg_t = misc(128 * 2, F32)
    rstd_t = misc(512 * 2, F32)
    XT = [misc(512 * 2, F32) for _ in range(2)]
    sil = [misc(512) for _ in range(2)]
    par_f = misc(128 * 2, F32)

    ps = [psum[:, b * 512:(b + 1) * 512] for b in range(8)]

    neglam = small[:, 0:1]
    rcp = small[:, 1:2]
    ssum = small[:, 2:3]
    rr = small[:, 3:4]
    e1 = small[:, 4:5]
    e2 = small[:, 5:6]
    mv = small[:, 8:10]
    rs = small[:, 10:11]
    stats = small[:, 16:28].rearrange("p (a b) -> p a b", b=6)
    fgt = small[:, 32:48]

    P.dma("pool", ident, cident, sem="c0")
    P.dma("pool", RT, crt, sem="c0")
    P.dma("sp", par_f[:, 0:16], ccol, sem="c1")
    P.dma("sp", fgt, fg, sem="c1")
    P.dma("sp", modAll, ada_b, sem="c1")
    n1t = bslot(0, 1, F32)[:, 0:64].rearrange("p (l j) -> p l j", j=16)
    n2t = bslot(1, 1, F32)[:, 0:64].rearrange("p (l j) -> p l j", j=16)
    P.dma("sp", n1t, n1g, sem="c1")
    P.dma("sp", n2t, n2g, sem="c1")
    P.memset(ones, 1.0)
    P.barrier()
    P.act(cact, par_f[:, 0:16], AF.Silu)
    P.barrier()

    for l in range(depth):
        wfree = [[], []]
        for ci in range(24):
            slot = ci % 2
            wv = wbuf[slot].rearrange("p (k n) -> p k n", n=512)
            t_w = P.dma("pool", wv, ada_w[l][:, ci * 512:(ci + 1) * 512].rearrange("(k p) n -> p k n", p=128),
                        deps=wfree[slot], sem="w%d" % slot)
            for sc in range(4):
                j = ci * 4 + sc
                for k in range(16):
                    t = P.mm(ps[0][:, j:j + 1], wv[:, k, sc * 128:(sc + 1) * 128], cact[:, k:k + 1],
                             start=(j == 0 and k == 0), stop=(k == 15), deps=[t_w], signal=(k == 15))
            wfree[slot] = [t]
        t = P.tt("dve", modAll[:, l, :], ps[0][:, 0:96], modAll[:, l, :], ALU.add, deps=[t])
        P.stt(geff[:, l, 0:16], modAll[:, l, 16:32], 1.0, n1t[:, l, :], ALU.add, ALU.mult, deps=[t])
        P.stt(geff[:, l, 16:32], modAll[:, l, 64:80], 1.0, n2t[:, l, :], ALU.add, ALU.mult, deps=[t])
        P.barrier()

    class WStream:
        def __init__(self):
            self.free = [[], []]
            self.n = 0
            self.fifo = []

        def _issue(self, key, fn):
            slot = self.n % 2
            self.n += 1
            t = None
            for i, (dst, src) in enumerate(fn(wbuf[slot])):
                t = P.dma("pool", dst, src, deps=self.free[slot] if i == 0 else (), sem="w%d" % slot)
            return (key, slot, t)

        def prefetch(self, key, fn):
            self.fifo.append(self._issue(key, fn))

        def take(self, key, fn):
            if self.fifo:
                k, slot, t = self.fifo.pop(0)
                assert k == key, (k, key)
                return slot, t
            _, slot, t = self._issue(key, fn)
            return slot, t

        def release(self, slot, tok):
            self.free[slot] = [tok]

    W = WStream()

    def kv(src):
        return src.rearrange("(k p) n -> p k n", p=128)

    def spec_in(l, ci):
        return ("in", l, ci), lambda wb: [(wb.rearrange("p (k n) -> p k n", n=512), kv(w_in[l][:, ci * 512:(ci + 1) * 512]))]

    def spec_mg(l, ci):
        def fn(wb):
            wv = wb.rearrange("p (a k n) -> p a k n", a=2, n=512)
            return [(wv[:, 0], kv(wupg[l][:, ci * 512:(ci + 1) * 512])), (wv[:, 1], kv(wupa[l][:, ci * 512:(ci + 1) * 512]))]
        return ("mg", l, ci), fn

    def spec_wo(l, ci):
        return ("wo", l, ci), lambda wb: [(wb.rearrange("p (k n) -> p k n", n=512), kv(w_o[l][:, ci * 512:(ci + 1) * 512]))]

    def spec_w13(l, hs, jc):
        def fn(wb):
            col0 = (hs * 22 + jc * 2) * 128
            wv = wb.rearrange("p (a k n) -> p a k n", a=2, n=256)
            return [(wv[:, 0], kv(w1[l][:, col0:col0 + 256])), (wv[:, 1], kv(w3[l][:, col0:col0 + 256]))]
        return ("w13", l, hs, jc), fn

    def spec_w2(l, hs, ci):
        return ("w2", l, hs, ci), lambda wb: [(wb[:, 0:22 * 256].rearrange("p (k n) -> p k n", n=256),
                                               kv(w2[l][hs * 2816:(hs + 1) * 2816, ci * 256:(ci + 1) * 256]))]

    def prefetch2(spec_fn, *args):
        for i in range(2):
            W.prefetch(*spec_fn(*args, i))

    def norm_phase(xsrc, half, g_ap, sh_ap, final=False):
        xg_rot = Rot([bslot(12, 8, F32).rearrange("p (c t) -> p c t", t=512),
                      bslot(0, 8, F32).rearrange("p (c t) -> p c t", t=512)])
        sq_rot = Rot([bslot(20, 1)[:, i * 512:(i + 1) * 512] for i in range(4)])
        tmp_rot = Rot([bslot(21, 1, F32)[:, i * 512:(i + 1) * 512] for i in range(2)])
        out_rot = Rot(XT)
        bank_rot = Rot([ps[0], ps[1]])
        loads = {}

        def load_x(tg_):
            ixg_, xg_, xfree_ = xg_rot.next()
            t0_ = half * 2048 + tg_ * 512
            loads[tg_] = (ixg_, xg_, P.dma("sp", xg_, xsrc[:, t0_:t0_ + 512].rearrange("(c p) t -> p c t", p=128),
                                           deps=xfree_, sem="xg%d" % ixg_))

        load_x(0)
        for tg in range(4):
            tok0 = half * 2048 + tg * 512
            if tg + 1 < 4:
                load_x(tg + 1)
            ixg, xg, t_ld = loads.pop(tg)
            _, bank, bfree = bank_rot.next()
            for c in range(16):
                i, sqb, fr = sq_rot.next()
                t_sq = P.act(sqb, xg[:, c, :], AF.Square, deps=[t_ld] + fr)
                t_mm = P.mm(bank, ones, sqb, start=(c == 0), stop=(c == 15),
                            deps=[t_sq] + (bfree if c == 0 else []), signal=True)
                sq_rot.free[i] = [t_mm]
            t1 = P.ts("dve", rstd_t, bank, 1.0 / D, EPS, ALU.mult, ALU.add, deps=[t_mm])
            bank_rot.free[(bank_rot.i - 1) % 2] = [t1]
            t2 = P.act(rstd_t, rstd_t, AF.Sqrt, deps=[t1])
            t3 = P.recip(rstd_t, rstd_t, deps=[t2])
            last = []
            for c in range(16):
                i, tb, fr = tmp_rot.next()
                t_a = P.tt("dve", tb, xg[:, c, :], rstd_t, ALU.mult, deps=[t3] + fr)
                if not final:
                    t_b = P.act(bigA[:, c, tg * 512:(tg + 1) * 512], tb, AF.Identity, deps=[t_a],
                                scale=g_ap[:, c:c + 1], bias=sh_ap[:, c:c + 1])
                    tmp_rot.free[i] = [t_b]
                    last = [t_a, t_b]
                else:
                    io, ob, fro = out_rot.next()
                    t_b = P.act(ob, tb, AF.Identity, deps=[t_a] + fro, scale=g_ap[:, c:c + 1])
                    tmp_rot.free[i] = [t_b]
                    t_s = P.dma("sp", yT[c * 128:(c + 1) * 128, tok0:tok0 + 512], ob, deps=[t_b], sem="xo%d" % io)
                    out_rot.free[io] = [t_s]
                    last = [t_a, t_b]
            xg_rot.free[ixg] = last
        P.barrier()

    def resid_update(bank, t_mm, c, tok0, xsrc, gt_ap, xrot, bank_rot_free_cb):
        io, xt, fro = xrot.next()
        t_x = P.dma("sp", xt, xsrc[c * 128:(c + 1) * 128, tok0:tok0 + 512], deps=fro, sem="xi%d" % io)
        t_u = P.stt(xt, bank, gt_ap[:, c:c + 1], xt, ALU.mult, ALU.add, deps=[t_mm, t_x])
        bank_rot_free_cb([t_u])
        t_s = P.dma("sp", XS[c * 128:(c + 1) * 128, tok0:tok0 + 512], xt, deps=[t_u], sem="xi%d" % io)
        xrot.free[io] = [t_s]

    for l in range(depth):
        lambda_init = 0.8 - 0.6 * math.exp(-0.3 * l)
        xsrc = xT if l == 0 else XS
        sh1 = modAll[:, l, 0:16]
        gt1 = modAll[:, l, 32:48]
        sh2 = modAll[:, l, 48:64]
        gt2 = modAll[:, l, 80:96]
        g1 = geff[:, l, 0:16]
        g2 = geff[:, l, 16:32]

        par = l % 2
        if l == 0:
            prefetch2(spec_in, l)
        P.dma("pool", lng_t, lng[l], sem="c0")
        P.dma("pool", lnb_t, lnb[l], sem="c0")
        P.dma("pool", ws_t, wsT[l], sem="c0")
        P.dma("sp", bs_t, bsb[l], sem="c1")
        P.dma("sp", subg_t, subg[l], sem="c1")
        P.dma("sp", lamt, lamv[l], sem="c1")
        P.barrier()
        t = P.tt("dve", lamt[:, 0, :], lamt[:, 0, :], lamt[:, 1, :], ALU.mult)
        t = P.op("dve", lambda e: e.reduce_sum(out=e1, in_=lamt[:, 0, :], axis=AX), deps=[t])
        t = P.act(e1, e1, AF.Exp, deps=[t])
        t2 = P.tt("dve", lamt[:, 2, :], lamt[:, 2, :], lamt[:, 3, :], ALU.mult)
        t2 = P.op("dve", lambda e: e.reduce_sum(out=e2, in_=lamt[:, 2, :], axis=AX), deps=[t2])
        t2 = P.act(e2, e2, AF.Exp, deps=[t2])
        t = P.tt("dve", neglam, e2, e1, ALU.subtract, deps=[t, t2])
        t = P.ts("dve", neglam, neglam, -lambda_init, None, ALU.add, deps=[t])
        P.ts("dve", subg_t, subg_t, 1.0 - lambda_init, None, ALU.mult)
        P.barrier()

        for half in range(1):
            hb = half * 2048
            norm_phase(xsrc, half, g1, sh1)
            hT = bigA
            uT = bigB[:, 0:8, :]
            cosb = bslot(8, 1)
            sinb = bslot(10, 1)
            P.dma("pool", cosb[:, 0:SL], ccos, sem="c0")
            P.dma("pool", sinb[:, 0:SL], csin, sem="c0")
            vg = bslot(12, 1, F32)
            tmp2 = bslot(13, 1, F32)
            vn = bslot(14, 1)[:, 0:1024]
            vst_rot = Rot([bslot(14, 1)[:, 1024:2048], bslot(15, 1)[:, 0:1024]])
            stage_rot = Rot([bslot(16, 1)[:, i * 512:(i + 1) * 512] for i in range(4)])
            qb_rot = Rot([bslot(17, 1)[:, i * 512:(i + 1) * 512] for i in range(2)])
            f32_rot = Rot([bslot(18 + i // 2, 1, F32)[:, (i % 2) * 512:(i % 2 + 1) * 512] for i in range(4)])
            P.barrier()
            bank_rot = Rot([ps[0], ps[1], ps[2], ps[3]])
            pair_rot = Rot([psum[:, 2048:3072], psum[:, 3072:4096]])
            t_w = {}
            w_slot = {}

            def load_w(ci):
                w_slot[ci], t_w[ci] = W.take(*spec_in(l, ci))
                return wbuf[w_slot[ci]].rearrange("p (k n) -> p k n", n=512)

            def fm_piece(ci, kind, jbase):
                wv = load_w(ci)
                slot = w_slot[ci]
                rope_pend = []
                for sc in range(4):
                    j = ci * 4 + sc - jbase
                    for tg in range(4):
                        ib, bank, bfree = bank_rot.next()
                        tsl = slice(tg * 512, (tg + 1) * 512)
                        gsl = slice(hb + tg * 512, hb + (tg + 1) * 512)
                        for k in range(16):
                            t_mm = P.mm(bank, wv[:, k, sc * 128:(sc + 1) * 128], hT[:, k, tsl], start=(k == 0),
                                        stop=(k == 15), deps=([t_w[ci]] + bfree) if k == 0 else (), signal=(k == 15))
                        if kind == "u":
                            t = P.act(uT[:, j, tsl], bank, AF.Gelu_apprx_tanh, deps=[t_mm])
                            bank_rot.free[ib] = [t]
                        elif kind in ("ga", "gb"):
                            i4, st, frs = stage_rot.next()
                            t = P.act(st, bank, AF.Sigmoid, deps=[t_mm] + frs)
                            bank_rot.free[ib] = [t]
                            dst = SAd if kind == "ga" else SBd
                            t_s = P.dma("sp", dst[j][:, gsl], st, deps=[t], sem="st%d" % i4)
                            stage_rot.free[i4] = [t_s]
                        else:
                            iq, qb_, frq = qb_rot.next()
                            t_c = P.act(qb_, bank, AF.Copy, deps=[t_mm] + frq)
                            bank_rot.free[ib] = [t_c]

                            def rope_part(iq=iq, qb_=qb_, t_c=t_c, j=j, gsl=gsl):
                                ib2, bank2, bfree2 = bank_rot.next()
                                t_r = P.mm(bank2, RT, qb_, start=True, stop=True, deps=[t_c] + bfree2, signal=True)
                                ia, ta, fra = f32_rot.next()
                                t_a = P.tt("dve", ta, qb_, cosb[:, gsl], ALU.mult, deps=[t_c] + fra)
                                ibb, tb_, frb = f32_rot.next()
                                t_b = P.tt("dve", tb_, bank2, sinb[:, gsl], ALU.mult, deps=[t_r] + frb)
                                bank_rot.free[ib2] = [t_b]
                                qb_rot.free[iq] = [t_r, t_a]
                                i4, st, frs = stage_rot.next()
                                t_o = P.tt("dve", st, ta, tb_, ALU.add, deps=[t_a, t_b] + frs)
                                f32_rot.free[ia] = [t_o]
                                f32_rot.free[ibb] = [t_o]
                                if kind == "q":
                                    dst_ap = QTd[j][:, gsl]
                                else:
                                    dst_ap = KL[par][j // 2][(j % 2) * 128:(j % 2 + 1) * 128, gsl]
                                t_s = P.dma("sp", dst_ap, st, deps=[t_o], sem="st%d" % i4)
                                stage_rot.free[i4] = [t_s]

                            while rope_pend:
                                rope_pend.pop(0)()
                            rope_pend.append(rope_part)
                while rope_pend:
                    rope_pend.pop(0)()
                W.release(slot, t_mm)

            def tm_v(ci0):
                wvs = [load_w(ci0), load_w(ci0 + 1)]
                sets = [(bslot(12, 1, F32), bslot(13, 1, F32), bslot(14, 1)[:, 0:1024]),
                        (bslot(9, 1, F32), bslot(11, 1, F32), bslot(20, 1)[:, 0:1024])]
                set_free = [[], []]
                mainb, spb = pair_rot.bufs[0], pair_rot.bufs[1]
                fr = {"main": list(pair_rot.free[0]), "sp": list(pair_rot.free[1])}
                t_mm = None

                def sp_part(tt_, tmp2_, vn_, t_n_):
                    tsl_ = slice(tt_ * 128, (tt_ + 1) * 128)
                    for g in range(8):
                        t_sp = P.mm(spb[:, g * 128:(g + 1) * 128], vn_[:, g * 128:(g + 1) * 128], ws_t[:, g, :],
                                    start=True, stop=True, deps=([t_n_] + fr["sp"]) if g == 0 else (), signal=(g == 7))
                    t_m = P.tt("dve", tmp2_, spb, bs_t, ALU.add, deps=[t_sp])
                    fr["sp"] = [t_m]
                    t_u = P.tt("dve", uT[:, :, tsl_], tmp2_.rearrange("p (g q) -> p g q", q=128), uT[:, :, tsl_],
                               ALU.mult, deps=[t_m])
                    set_free[tt_ % 2] = [t_u]

                pend = None
                for tt_ in range(16):
                    vg_, tmp2_, vn_ = sets[tt_ % 2]
                    tsl = slice(tt_ * 128, (tt_ + 1) * 128)
                    for cc in range(2):
                        for k in range(16):
                            first = (cc == 0 and k == 0)
                            t_mm = P.mm(mainb[:, cc * 512:(cc + 1) * 512], hT[:, k, tsl], wvs[cc][:, k, :],
                                        start=(k == 0), stop=(k == 15),
                                        deps=([t_w[ci0], t_w[ci0 + 1]] + fr["main"]) if first else (),
                                        signal=(cc == 1 and k == 15))
                    t_g = P.act(vg_, mainb, AF.Gelu_apprx_tanh, deps=[t_mm] + set_free[tt_ % 2])
                    fr["main"] = [t_g]
                    ta_ = P.op("dve", lambda e, v=vg_: e.bn_stats(out=stats[:, 0, :], in_=v[:, 0:512]), deps=[t_g])
                    tb2 = P.op("dve", lambda e, v=vg_: e.bn_stats(out=stats[:, 1, :], in_=v[:, 512:1024]), deps=[t_g])
                    t_ag = P.op("dve", lambda e: e.bn_aggr(out=mv, in_=small[:, 16:28]), deps=[ta_, tb2])
                    t_r1 = P.ts("dve", rs, mv[:, 1:2], EPS, None, ALU.add, deps=[t_ag])
                    t_r2 = P.act(rs, rs, AF.Sqrt, deps=[t_r1])
                    t_r3 = P.recip(rs, rs, deps=[t_r2])
                    t_n = P.ts("dve", vg_, vg_, mv[:, 0:1], rs, ALU.subtract, ALU.mult, deps=[t_r3])
                    t_n = P.tt("dve", vg_, vg_, lng_t, ALU.mult, deps=[t_n])
                    t_n = P.tt("dve", vn_, vg_, lnb_t, ALU.add, deps=[t_n])
                    if pend is not None:
                        sp_part(*pend)
                    pend = (tt_, tmp2_, vn_, t_n)
                sp_part(*pend)
                pair_rot.free[0] = fr["main"]
                pair_rot.free[1] = fr["sp"]
                pair_rot.i = 0
                W.release(w_slot[ci0], t_mm)
                W.release(w_slot[ci0 + 1], t_mm)

            def tm_piece(ci0, kind):
                if kind == "v":
                    return tm_v(ci0)
                wv0 = load_w(ci0)
                wv1 = load_w(ci0 + 1)
                wvs = [wv0, wv1]
                t_u_last = None
                for tt_ in range(16):
                    ip, pair, pfree = pair_rot.next()
                    tsl = slice(tt_ * 128, (tt_ + 1) * 128)
                    for cc in range(2):
                        for k in range(16):
                            first = (cc == 0 and k == 0)
                            t_mm = P.mm(pair[:, cc * 512:(cc + 1) * 512], hT[:, k, tsl], wvs[cc][:, k, :],
                                        start=(k == 0), stop=(k == 15),
                                        deps=([t_w[ci0], t_w[ci0 + 1]] + pfree) if first else (),
                                        signal=(cc == 1 and k == 15))
                    if kind == "va":
                        iv, vs, frv = vst_rot.next()
                        t = P.act(vs, pair, AF.Copy, deps=[t_mm] + frv)
                        pair_rot.free[ip] = [t]
                        for c_ in range(4):
                            t_s = P.dma("sp", VL[par][c_][tt_ * 128:(tt_ + 1) * 128, :], vs[:, c_ * 256:(c_ + 1) * 256],
                                        deps=[t], sem="vs%d" % iv)
                        vst_rot.free[iv] = [t_s]
                    else:
                        t_g = P.act(vg, pair, AF.Gelu_apprx_tanh, deps=[t_mm] + ([t_u_last] if t_u_last else []))
                        pair_rot.free[ip] = [t_g]
                        ta_ = P.op("dve", lambda e: e.bn_stats(out=stats[:, 0, :], in_=vg[:, 0:512]), deps=[t_g])
                        tb2 = P.op("dve", lambda e: e.bn_stats(out=stats[:, 1, :], in_=vg[:, 512:1024]), deps=[t_g])
                        t_ag = P.op("dve", lambda e: e.bn_aggr(out=mv, in_=small[:, 16:28]), deps=[ta_, tb2])
                        t_r1 = P.ts("dve", rs, mv[:, 1:2], EPS, None, ALU.add, deps=[t_ag])
                        t_r2 = P.act(rs, rs, AF.Sqrt, deps=[t_r1])
                        t_r3 = P.recip(rs, rs, deps=[t_r2])
                        t_n = P.ts("dve", vg, vg, mv[:, 0:1], rs, ALU.subtract, ALU.mult, deps=[t_r3])
                        t_n = P.tt("dve", vg, vg, lng_t, ALU.mult, deps=[t_n])
                        t_n = P.tt("dve", vn, vg, lnb_t, ALU.add, deps=[t_n] + ([t_u_last] if t_u_last else []))
                        ip2, pair2, pfree2 = pair_rot.next()
                        for g in range(8):
                            t_sp = P.mm(pair2[:, g * 128:(g + 1) * 128], vn[:, g * 128:(g + 1) * 128], ws_t[:, g, :],
                                        start=True, stop=True, deps=([t_n] + pfree2) if g == 0 else (), signal=(g == 7))
                        t_m = P.tt("dve", tmp2, pair2, bs_t, ALU.add, deps=[t_sp])
                        pair_rot.free[ip2] = [t_m]
                        t_u_last = P.tt("dve", uT[:, :, tsl], tmp2.rearrange("p (g q) -> p g q", q=128), uT[:, :, tsl],
                                        ALU.mult, deps=[t_m])
                W.release(w_slot[ci0], t_mm)
                W.release(w_slot[ci0 + 1], t_mm)

            fm_piece(0, "u", 0)
            fm_piece(1, "u", 0)
            tm_piece(2, "v")
            fm_piece(4, "q", 16)
            fm_piece(5, "q", 16)
            fm_piece(6, "k", 24)
            fm_piece(7, "k", 24)
            tm_piece(8, "va")
            for ci in range(10, 14):
                if ci == 10:
                    kdeps = P.dtoks(["st0", "st1", "st2", "st3"])
                    for c_ in range(4):
                        P.coll(KL[par][c_], KA[par][c_], deps=kdeps, sem="cc")
                if ci == 12:
                    vdeps = P.dtoks(["vs0", "vs1"])
                    for c_ in range(4):
                        P.coll(VL[par][c_], VA[par][c_], deps=vdeps, sem="cc")
                fm_piece(ci, "ga", 40)
            for ci in range(14, 18):
                fm_piece(ci, "gb", 56)
            P.barrier()

        def aslot(s0, ns, dt=BF16):
            return carve(A0 + s0 * 2048, ns * 2048, dt)

        KTa = [aslot(0, 2), aslot(2, 2)]
        KTb = [aslot(4, 2), aslot(6, 2)]
        QTb = [aslot(8, 1), aslot(9, 1)]
        Vau = [aslot(10, 3)[:, 0:32 * 129].rearrange("p (k d) -> p k d", d=129),
               aslot(13, 3)[:, 0:32 * 129].rearrange("p (k d) -> p k d", d=129)]
        PT_rot = Rot([bslot(18, 1)[:, i * 512:(i + 1) * 512] for i in range(4)])
        onorm = [[bslot(19, 1, F32)[:, (m * 4 + j) * 128:(m * 4 + j + 1) * 128] for j in range(4)] for m in range(2)]
        onorm_all = [bslot(19, 1, F32)[:, m * 512:(m + 1) * 512] for m in range(2)]
        ocomb4 = bslot(20, 1, F32)[:, 0:512]
        junk4 = bslot(20, 1, F32)[:, 512:1024]
        obf4 = bslot(21, 1)[:, 0:512]
        rcp4 = small[:, 48:52]
        ssum4 = small[:, 52:56]
        rr4 = small[:, 56:60]
        brB_w = bigB[:, 8:16, :]
        prefetch2(spec_mg, l)
        for b in range(2):
            P.memset(KTa[b][64:128, :], 0.0)
            P.memset(KTb[b][0:64, :], 0.0)
            P.memset(Vau[b][:, :, 128:129], 1.0)
        P.barrier()
        S_rot = Rot([ps[0], ps[1], ps[2]])
        O_rot = Rot([psum[:, 1536:2560], psum[:, 2560:3584]])
        psT = ps[7].bitcast(BF16)
        psT_free = []
        hfree = [[], []]
        on_free = [[], []]
        ep = {"oc": [], "rr": [], "obf": [], "psT": [], "t_rr": None}
        pending = []

        def run_pending(kc):
            keep = []
            for trig, fn in pending:
                if trig <= kc:
                    fn()
                else:
                    keep.append((trig, fn))
            pending[:] = keep

        for h in range(8):
            buf = h % 2
            hc, hl = h // 2, h % 2
            tl = []
            for r_ in range(2):
                r0 = r_ * 256 + hl * 128
                tl.append(P.dma("sp", KTa[buf][0:64, r_ * SL:(r_ + 1) * SL], KA[par][hc][r0:r0 + 64, :],
                                deps=hfree[buf] if r_ == 0 else (), sem="hk%d" % buf))
                tl.append(P.dma("sp", KTb[buf][64:128, r_ * SL:(r_ + 1) * SL], KA[par][hc][r0 + 64:r0 + 128, :],
                                sem="hk%d" % buf))
            tl.append(P.dma("sp", QTb[buf], QTd[h], sem="hk%d" % buf))
            for r_ in range(2):
                tl.append(P.dma("sp", Vau[buf][:, r_ * 16:(r_ + 1) * 16, 0:128],
                                VA[par][hc][r_ * SL:(r_ + 1) * SL, hl * 128:(hl + 1) * 128].rearrange(
                                    "(k p) d -> p k d", p=128), sem="hk%d" % buf))
            t_ld = tl[-1]
            last_pe = None
            for qg in range(SL // 512):
                qsl = slice(qg * 512, (qg + 1) * 512)
                t_on_m = [None, None]
                for m in range(2):
                    KTm = KTa[buf] if m == 0 else KTb[buf]
                    io, Ob, ofree = O_rot.next()
                    Oj = [Ob[:, (j // 2) * 512 + (j % 2) * 129:(j // 2) * 512 + (j % 2) * 129 + 129] for j in range(4)]
                    sinfo = {}

                    def issue_S(kc):
                        ib, bank, bfree = S_rot.next()
                        t = P.mm(bank, KTm[:, kc * 128:(kc + 1) * 128], QTb[buf][:, qsl], start=True, stop=True,
                                 deps=[t_ld] + bfree, signal=True)
                        sinfo[kc] = (ib, bank, t)

                    issue_S(0)
                    issue_S(1)
                    for kc in range(32):
                        ib, bank, t_S = sinfo.pop(kc)
                        ipt, pt, frp = PT_rot.next()
                        t_e = P.act(pt, bank, AF.Exp, deps=[t_S] + frp, scale=0.125)
                        S_rot.free[ib] = [t_e]
                        if kc + 2 < 32:
                            issue_S(kc + 2)
                        run_pending(kc)
                        for j in range(4):
                            t_pv = P.mm(Oj[j], pt[:, j * 128:(j + 1) * 128], Vau[buf][:, kc, :],
                                        start=(kc == 0 and j % 2 == 0), stop=(kc == 31),
                                        deps=([t_e] + (ofree if kc == 0 else [])) if j == 0 else (), signal=(j == 3))
                        PT_rot.free[ipt] = [t_pv]
                    last_pe = t_pv
                    assert not pending
                    t_on = None
                    for j in range(4):
                        t_r = P.recip(rcp4[:, j:j + 1], Oj[j][:, 128:129], deps=[t_pv] + on_free[m])
                        t_on = P.ts("dve", onorm[m][j], Oj[j][:, 0:128], rcp4[:, j:j + 1], None, ALU.mult, deps=[t_r])
                    O_rot.free[io] = [t_on]
                    t_on_m[m] = t_on
                    if m == 1:
                        t_c = P.stt(ocomb4, onorm_all[1], neglam, onorm_all[0], ALU.mult, ALU.add,
                                    deps=[t_on_m[0], t_on_m[1]] + ep["oc"])
                        on_free[0] = [t_c]
                        on_free[1] = [t_c]
                        t_q = P.tt("dve", junk4, ocomb4, ocomb4, ALU.mult, deps=[t_c])
                        for j in range(4):
                            t_q = P.op("dve", lambda e, j=j: e.reduce_sum(out=ssum4[:, j:j + 1],
                                                                          in_=junk4[:, j * 128:(j + 1) * 128], axis=AX),
                                       deps=[t_q])
                        t_q = P.ts("dve", rr4, ssum4, 1.0 / 128, EPS, ALU.mult, ALU.add, deps=[t_q] + ep["rr"])

                        def epi_act(t_q=t_q):
                            t1 = P.act(rr4, rr4, AF.Ln, deps=[t_q])
                            ep["t_rr"] = P.act(rr4, rr4, AF.Exp, deps=[t1], scale=-0.5)

                        def epi_pe(h=h, qsl=qsl):
                            t_ob = None
                            for j in range(4):
                                t_ob = P.stt(obf4[:, j * 128:(j + 1) * 128], ocomb4[:, j * 128:(j + 1) * 128],
                                             rr4[:, j:j + 1], subg_t, ALU.mult, ALU.mult,
                                             deps=([ep["t_rr"]] + ep["obf"]) if j == 0 else ())
                            ep["rr"] = [t_ob]
                            ep["oc"] = [t_ob]
                            t_tr = None
                            for j in range(4):
                                t_tr = P.transpose(psT[:, j * 128:(j + 1) * 128], obf4[:, j * 128:(j + 1) * 128], ident,
                                                   deps=([t_ob] + ep["psT"]) if j == 0 else ())
                            ep["obf"] = [t_tr]
                            t_cp = P.copy("dve", brB_w[:, h, qsl], psT[:, 0:512], deps=[t_tr])
                            ep["psT"] = [t_cp]

                        pending.append((8, epi_act))
                        pending.append((12, epi_pe))
            hfree[buf] = [last_pe]
        run_pending(99)
        P.barrier()

        for half in range(1):
            hb = half * 2048
            brA = bigB[:, 0:8, :]
            brB = bigB[:, 8:16, :]
            merged = bigA
            sa_rot = Rot([bslot(16, 1), bslot(17, 1)])
            sb_rot = Rot([bslot(18, 1), bslot(19, 1)])
            t1_rot = Rot([bslot(20, 1, F32)[:, i * 512:(i + 1) * 512] for i in range(2)])
            t2_rot = Rot([bslot(21, 1, F32)[:, i * 512:(i + 1) * 512] for i in range(2)])
            bank_rot = Rot([ps[0], ps[1], ps[2], ps[3], ps[4], ps[5]])
            for ci in range(4):
                slot, t_w = W.take(*spec_mg(l, ci))
                wv = wbuf[slot].rearrange("p (a k n) -> p a k n", a=2, n=512)
                for sc in range(4):
                    c = ci * 4 + sc
                    isa, sat, frsa = sa_rot.next()
                    isb, sbt, frsb = sb_rot.next()
                    t_sa = P.dma("sp", sat, SAd[c][:, hb:hb + 2048], deps=frsa, sem="sa%d" % isa)
                    t_sb = P.dma("sp", sbt, SBd[c][:, hb:hb + 2048], deps=frsb, sem="sb%d" % isb)
                    for tg in range(4):
                        tsl = slice(tg * 512, (tg + 1) * 512)
                        ia_, bka, fa = bank_rot.next()
                        for k in range(8):
                            t_ma = P.mm(bka, wv[:, 0, k, sc * 128:(sc + 1) * 128], brA[:, k, tsl], start=(k == 0),
                                        stop=(k == 7), deps=([t_w] + fa) if k == 0 else (), signal=(k == 7))
                        ib_, bkb, fb = bank_rot.next()
                        for k in range(8):
                            t_mb = P.mm(bkb, wv[:, 1, k, sc * 128:(sc + 1) * 128], brB[:, k, tsl], start=(k == 0),
                                        stop=(k == 7), deps=fb if k == 0 else (), signal=(k == 7))
                        i1, t1b, f1 = t1_rot.next()
                        i2, t2b, f2 = t2_rot.next()
                        ta_ = P.tt("dve", t1b, bka, sat[:, tsl], ALU.mult, deps=[t_ma, t_sa] + f1)
                        tb2 = P.tt("dve", t2b, bkb, sbt[:, tsl], ALU.mult, deps=[t_mb, t_sb] + f2)
                        bank_rot.free[ia_] = [ta_]
                        bank_rot.free[ib_] = [tb2]
                        t_m = P.tt("pool", merged[:, c, tsl], t1b, t2b, ALU.add, deps=[ta_, tb2])
                        t1_rot.free[i1] = [t_m]
                        t2_rot.free[i2] = [t_m]
                    sa_rot.free[isa] = [ta_]
                    sb_rot.free[isb] = [tb2]
                W.release(slot, t_mb)
            prefetch2(spec_wo, l)
            P.barrier()
            xrot = Rot(XT + [bslot(i // 2, 1, F32)[:, (i % 2) * 512:(i % 2 + 1) * 512] for i in range(4)])
            for ci in range(4):
                slot, t_w = W.take(*spec_wo(l, ci))
                wv = wbuf[slot].rearrange("p (k n) -> p k n", n=512)
                for sc in range(4):
                    c = ci * 4 + sc
                    for tg in range(4):
                        tsl = slice(tg * 512, (tg + 1) * 512)
                        ib, bank, bfree = bank_rot.next()
                        for k in range(16):
                            t_mm = P.mm(bank, wv[:, k, sc * 128:(sc + 1) * 128], merged[:, k, tsl], start=(k == 0),
                                        stop=(k == 15), deps=([t_w] + bfree) if k == 0 else (), signal=(k == 15))

                        def cb(toks, ib=ib):
                            bank_rot.free[ib] = toks
                        resid_update(bank, t_mm, c, hb + tg * 512, xsrc, gt1, xrot, cb)
                W.release(slot, t_mm)
            prefetch2(spec_w13, l, 0)
            P.barrier()

        for half in range(1):
            hb = half * 2048
            norm_phase(XS, half, g2, sh2)
            hT = bigA
            hid = bigB
            bank_rot = Rot([ps[0], ps[1], ps[2], ps[3], ps[4], ps[5]])
            sil_rot = Rot(sil)
            xrot = Rot(XT)
            for hs in range(2):
                for jc in range(11):
                    slot, t_w = W.take(*spec_w13(l, hs, jc))
                    wv = wbuf[slot].rearrange("p (a k n) -> p a k n", a=2, n=256)
                    for sc in range(2):
                        jj = jc * 2 + sc
                        for tg in range(4):
                            tsl = slice(tg * 512, (tg + 1) * 512)
                            i1, b1, f1 = bank_rot.next()
                            for k in range(16):
                                t_m1 = P.mm(b1, wv[:, 0, k, sc * 128:(sc + 1) * 128], hT[:, k, tsl], start=(k == 0),
                                            stop=(k == 15), deps=([t_w] + f1) if k == 0 else (), signal=(k == 15))
                            i3, b3, f3 = bank_rot.next()
                            for k in range(16):
                                t_m3 = P.mm(b3, wv[:, 1, k, sc * 128:(sc + 1) * 128], hT[:, k, tsl], start=(k == 0),
                                            stop=(k == 15), deps=f3 if k == 0 else (), signal=(k == 15))
                            isl, sb_, fs = sil_rot.next()
                            t_s = P.act(sb_, b1, AF.Silu, deps=[t_m1] + fs)
                            bank_rot.free[i1] = [t_s]
                            t_h = P.tt("dve", hid[:, jj, tsl], b3, sb_, ALU.mult, deps=[t_m3, t_s])
                            bank_rot.free[i3] = [t_h]
                            sil_rot.free[isl] = [t_h]
                    W.release(slot, t_m3)
                prefetch2(spec_w2, l, hs)
                P.barrier()
                for ci in range(8):
                    slot, t_w = W.take(*spec_w2(l, hs, ci))
                    wv = wbuf[slot][:, 0:22 * 256].rearrange("p (k n) -> p k n", n=256)
                    for sc in range(2):
                        c = ci * 2 + sc
                        for tg in range(4):
                            tsl = slice(tg * 512, (tg + 1) * 512)
                            ib, bank, bfree = bank_rot.next()
                            for k in range(22):
                                t_mm = P.mm(bank, wv[:, k, sc * 128:(sc + 1) * 128], hid[:, k, tsl], start=(k == 0),
                                            stop=(k == 21), deps=([t_w] + bfree) if k == 0 else (), signal=(k == 21))

                            def cb(toks, ib=ib):
                                bank_rot.free[ib] = toks
                            resid_update(bank, t_mm, c, hb + tg * 512, XS, gt2, xrot, cb)
                    W.release(slot, t_mm)
                if hs == 0:
                    prefetch2(spec_w13, l, 1)
                elif l + 1 < depth:
                    prefetch2(spec_in, l + 1)
                P.barrier()

    for half in range(1):
        norm_phase(XS, half, fgt, None, final=True)
    P.barrier()

    with nc.Block() as block:
        @block.tensor
        def _(e):
            for f in P.streams["pe"]:
                f(e)

        @block.scalar
        def _(e):
            for f in P.streams["act"]:
                f(e)

        @block.vector
        def _(e):
            for f in P.streams["dve"]:
                f(e)

        @block.gpsimd
        def _(e):
            for f in P.streams["pool"]:
                f(e)

        @block.sync
        def _(e):
            for f in P.streams["sp"]:
                f(e)
    return nc


def rope_consts():
    j = np.arange(32, dtype=np.float32)
    inv_freq = (10000.0 ** (-(2.0 * j) / 64.0)).astype(np.float32)
    pos = np.arange(S, dtype=np.float32)
    ang = pos[None, :] * inv_freq[:, None]
    cos = np.cos(ang).astype(np.float32)
    sin = np.sin(ang).astype(np.float32)
    cosT = np.tile(cos, (4, 1))
    sinT = np.tile(sin, (4, 1))
    rt = np.zeros((128, 128), np.float32)
    for m in range(2):
        for jj in range(32):
            i0 = m * 64 + jj
            i1 = m * 64 + 32 + jj
            rt[i1, i0] = -1.0
            rt[i0, i1] = 1.0
    return cosT, sinT, rt, np.eye(128, dtype=np.float32)


def make_in_maps(x, c, ada_w, ada_b, norm1_g, norm2_g, w_in, gmlp_ln_g, gmlp_ln_b, w_s, b_s,
                 lambda_q1, lambda_k1, lambda_q2, lambda_k2, subln_g, w_up_gmlp, w_up_attn, w_o,
                 ffn_w1, ffn_w3, ffn_w2, final_g):
    f = lambda a: np.ascontiguousarray(np.asarray(a, dtype=np.float32))
    cosT, sinT, rt, ident = rope_consts()
    L = DEPTH
    shared = {
        "ada_w": f(ada_w),
        "ada_b": f(np.asarray(ada_b).reshape(L, 96, 128).transpose(2, 0, 1)),
        "n1g": f(np.asarray(norm1_g).reshape(L, 16, 128).transpose(2, 0, 1)),
        "n2g": f(np.asarray(norm2_g).reshape(L, 16, 128).transpose(2, 0, 1)),
        "fg": f(np.asarray(final_g).reshape(16, 128).T),
        "w_in": f(w_in),
        "lng": f(np.broadcast_to(np.asarray(gmlp_ln_g)[:, None, :], (L, 128, 1024))),
        "lnb": f(np.broadcast_to(np.asarray(gmlp_ln_b)[:, None, :], (L, 128, 1024))),
        "wsT": f(np.asarray(w_s).transpose(0, 3, 1, 2)),
        "bsb": f(np.broadcast_to(np.asarray(b_s).reshape(L, 1, 1024), (L, 128, 1024))),
        "lamv": f(np.broadcast_to(np.stack([np.asarray(lambda_q1), np.asarray(lambda_k1), np.asarray(lambda_q2),
                                            np.asarray(lambda_k2)], axis=1)[:, None], (L, 128, 4, 64))),
        "subg": f(np.broadcast_to(np.asarray(subln_g)[:, None, :], (L, 128, 128))),
        "wupg": f(w_up_gmlp), "wupa": f(w_up_attn), "w_o": f(w_o),
        "w1": f(ffn_w1), "w3": f(ffn_w3), "w2": f(ffn_w2),
        "cident": ident, "crt": rt,
    }
    x = np.asarray(x, dtype=np.float32)
    c = np.asarray(c, dtype=np.float32)
    in_maps = []
    for core in range(NCORES):
        b, r = core // 2, core % 2
        m = dict(shared)
        m["xT"] = np.ascontiguousarray(x[b, r * SL:(r + 1) * SL, :].T)
        m["ccol"] = np.ascontiguousarray(c[b].reshape(16, 128).T)
        m["ccos"] = np.ascontiguousarray(cosT[:, r * SL:(r + 1) * SL])
        m["csin"] = np.ascontiguousarray(sinT[:, r * SL:(r + 1) * SL])
        in_maps.append(m)
    return in_maps


def kernel(**inputs):
    in_maps = make_in_maps(**inputs)
    nc = build_program()
    res = run_bass_kernel_spmd(nc, in_maps, core_ids=list(range(NCORES)))
    out = np.empty((NCORES // 2, S, D), np.float32)
    for core in range(NCORES):
        b, r = core // 2, core % 2
        out[b, r * SL:(r + 1) * SL, :] = np.asarray(res.results[core]["yT"]).T
    return out
```

```python
import math
import numpy as np
import concourse.bass as bass
import concourse.mybir as mybir
from concourse.bass_utils import run_bass_kernel_spmd

F32 = mybir.dt.float32
BF16 = mybir.dt.bfloat16
AF = mybir.ActivationFunctionType
ALU = mybir.AluOpType
AX = mybir.AxisListType.X

D = 2048
S = 4096
SL = 2048
DEPTH = 4
DIN = 9216
FH = 5632
NCORES = 8
RG = [[0, 1], [2, 3], [4, 5], [6, 7]]
EPS = 1e-6
ENG = ["pe", "act", "dve", "pool", "sp"]


class Prog:
    def __init__(self, nc):
        self.nc = nc
        self.streams = {e: [] for e in ENG}
        self.psem = {e: nc.alloc_semaphore(name="prog_" + e) for e in ENG}
        self.cnt = {e: 0 for e in ENG}
        self.known = {e: {} for e in ENG}
        self.dsems = {}

    def _wait(self, eng, tok):
        if tok is None:
            return
        key, handle, val = tok
        if self.known[eng].get(key, 0) >= val:
            return
        self.known[eng][key] = val
        self.streams[eng].append(lambda e, h=handle, v=val: e.wait_ge(h, v))

    def op(self, eng, fn, deps=(), signal=True):
        for d in deps:
            self._wait(eng, d)
        if signal:
            self.cnt[eng] += 1
            h = self.psem[eng]
            self.streams[eng].append(lambda e, fn=fn, h=h: fn(e).then_inc(h, 1))
            return (eng, h, self.cnt[eng])
        self.streams[eng].append(lambda e, fn=fn: fn(e))
        return None

    def dma(self, queue, out, in_, deps=(), sem="d"):
        for d in deps:
            self._wait(queue, d)
        if sem not in self.dsems:
            self.dsems[sem] = [self.nc.alloc_semaphore(name="dma_" + sem), 0]
        s = self.dsems[sem]
        s[1] += 16
        self.streams[queue].append(lambda e, o=out, i=in_, h=s[0]: e.dma_start(out=o, in_=i).then_inc(h, 16))
        return ("dma_" + sem, s[0], s[1])

    def coll(self, in_t, out_t, deps=(), sem="cc"):
        for d in deps:
            self._wait("pool", d)
        if sem not in self.dsems:
            self.dsems[sem] = [self.nc.alloc_semaphore(name="dma_" + sem), 0]
        s = self.dsems[sem]
        s[1] += 1
        self.streams["pool"].append(lambda e, i=in_t, o=out_t, h=s[0]: e.collective_compute(
            "AllGather", ALU.bypass, replica_groups=RG, ins=[i.ap().opt()], outs=[o.ap().opt()]).then_inc(h))
        return ("dma_" + sem, s[0], s[1])

    def dtoks(self, names):
        return [("dma_" + k, self.dsems[k][0], self.dsems[k][1]) for k in names if k in self.dsems and self.dsems[k][1] > 0]

    def barrier(self):
        toks = [(e, self.psem[e], self.cnt[e]) for e in ENG if self.cnt[e] > 0]
        toks += [("dma_" + k, s[0], s[1]) for k, s in self.dsems.items() if s[1] > 0]
        for e in ENG:
            for t in toks:
                self._wait(e, t)

    def mm(self, out, lhsT, rhs, start, stop, deps=(), signal=False):
        return self.op("pe", lambda e: e.matmul(out, lhsT, rhs, start=start, stop=stop, skip_group_check=True),
                       deps, signal)

    def transpose(self, out, in_, ident, deps=(), signal=True):
        return self.op("pe", lambda e: e.transpose(out, in_, ident), deps, signal)

    def act(self, out, in_, func, deps=(), scale=None, bias=None, accum_out=None):
        kw = {}
        if scale is not None:
            kw["scale"] = scale
        if bias is not None:
            kw["bias"] = bias
        if accum_out is not None:
            kw["accum_out"] = accum_out
        return self.op("act", lambda e: e.activation(out=out, in_=in_, func=func, **kw), deps)

    def tt(self, eng, out, in0, in1, op, deps=()):
        return self.op(eng, lambda e: e.tensor_tensor(out=out, in0=in0, in1=in1, op=op), deps)

    def ts(self, eng, out, in0, s1, s2, op0, op1=None, deps=()):
        if op1 is None:
            return self.op(eng, lambda e: e.tensor_scalar(out=out, in0=in0, scalar1=s1, scalar2=None, op0=op0), deps)
        return self.op(eng, lambda e: e.tensor_scalar(out=out, in0=in0, scalar1=s1, scalar2=s2, op0=op0, op1=op1), deps)

    def stt(self, out, in0, scalar, in1, op0, op1, deps=()):
        return self.op("dve", lambda e: e.scalar_tensor_tensor(out=out, in0=in0, scalar=scalar, in1=in1, op0=op0, op1=op1), deps)

    def recip(self, out, in_, deps=()):
        return self.op("dve", lambda e: e.reciprocal(out=out, in_=in_), deps)

    def copy(self, eng, out, in_, deps=()):
        return self.op(eng, lambda e: e.tensor_copy(out=out, in_=in_), deps)

    def memset(self, ap, val, deps=()):
        return self.op("pool", lambda e: e.memset(ap, val), deps)


class Rot:
    def __init__(self, bufs):
        self.bufs = bufs
        self.free = [[] for _ in bufs]
        self.i = 0

    def next(self):
        i = self.i
        self.i = (i + 1) % len(self.bufs)
        return i, self.bufs[i], list(self.free[i])


def build_program(depth=DEPTH, debug=False):
    nc = bass.Bass("TRN2", target_bir_lowering=False)

    def din(name, shape, dt=F32):
        return nc.dram_tensor(name, list(shape), dt, kind="ExternalInput").ap()

    def dint(name, shape, dt):
        kind = "ExternalOutput" if debug else "Internal"
        return nc.dram_tensor(name, list(shape), dt, kind=kind).ap()

    xT = din("xT", [D, SL])
    ccol = din("ccol", [128, 16])
    ada_w = din("ada_w", [DEPTH, D, 6 * D])
    ada_b = din("ada_b", [128, DEPTH, 96])
    n1g = din("n1g", [128, DEPTH, 16])
    n2g = din("n2g", [128, DEPTH, 16])
    fg = din("fg", [128, 16])
    w_in = din("w_in", [DEPTH, D, DIN])
    lng = din("lng", [DEPTH, 128, 1024])
    lnb = din("lnb", [DEPTH, 128, 1024])
    wsT = din("wsT", [DEPTH, 128, 8, 128])
    bsb = din("bsb", [DEPTH, 128, 1024])
    lamv = din("lamv", [DEPTH, 128, 4, 64])
    subg = din("subg", [DEPTH, 128, 128])
    wupg = din("wupg", [DEPTH, 1024, D])
    wupa = din("wupa", [DEPTH, 1024, D])
    w_o = din("w_o", [DEPTH, D, D])
    w1 = din("w1", [DEPTH, D, FH])
    w3 = din("w3", [DEPTH, D, FH])
    w2 = din("w2", [DEPTH, FH, D])
    cident = din("cident", [128, 128])
    crt = din("crt", [128, 128])
    ccos = din("ccos", [128, SL])
    csin = din("csin", [128, SL])

    yT = nc.dram_tensor("yT", [D, SL], F32, kind="ExternalOutput").ap()
    XS = dint("XS", [D, SL], F32)
    QTd = dint("QTd", [8, 128, SL], BF16)
    KL = [[nc.dram_tensor("KL%d_%d" % (p_, c_), [256, SL], BF16) for c_ in range(4)] for p_ in range(2)]
    KA = [[nc.dram_tensor("KA%d_%d" % (p_, c_), [512, SL], BF16) for c_ in range(4)] for p_ in range(2)]
    VL = [[nc.dram_tensor("VL%d_%d" % (p_, c_), [SL, 256], BF16) for c_ in range(4)] for p_ in range(2)]
    VA = [[nc.dram_tensor("VA%d_%d" % (p_, c_), [2 * SL, 256], BF16) for c_ in range(4)] for p_ in range(2)]
    SAd = dint("SAd", [16, 128, SL], BF16)
    SBd = dint("SBd", [16, 128, SL], BF16)

    P = Prog(nc)

    ARENA = 106400
    arena = nc.alloc_sbuf_tensor("arena", [128, ARENA], BF16)
    psum = nc.alloc_psum_tensor("psum", [128, 4096], F32)

    def carve(off, nel, dt=BF16):
        a = arena[:, off:off + nel]
        if dt == F32:
            a = a.bitcast(F32)
        return a

    A0 = 0
    B0 = 32768
    W0 = B0 + 45056
    M0 = W0 + 16384

    bigA = carve(A0, 32768).rearrange("p (c t) -> p c t", t=2048)
    bigB = carve(B0, 45056).rearrange("p (c t) -> p c t", t=2048)

    def bslot(s0, ns, dt=BF16):
        return carve(B0 + s0 * 2048, ns * 2048, dt)

    wbuf = [carve(W0 + i * 8192, 8192) for i in range(2)]

    mo = [M0]

    def misc(nel_bf16, dt=BF16):
        a = carve(mo[0], nel_bf16, dt)
        mo[0] += nel_bf16
        assert mo[0] <= ARENA, mo[0]
        return a

    modAll = misc(4 * 96 * 2, F32).rearrange("p (l j) -> p l j", j=96)
    geff = misc(4 * 32 * 2, F32).rearrange("p (l j) -> p l j", j=32)
    ident = misc(128)
    RT = misc(128)
    ones = misc(128)
    cact = misc(16)
    small = misc(64 * 2, F32)
    lamt = misc(4 * 64 * 2, F32).rearrange("p (a b) -> p a b", b=64)
    lng_t = misc(1024)
    lnb_t = misc(1024)
    bs_t = misc(1024 * 2, F32)
    ws_t = misc(1024).rearrange("p (g q) -> p g q", q=128)
    subg_t = misc(128 * 2, F32)
    rstd_t = misc(512 * 2, F32)
    XT = [misc(512 * 2, F32) for _ in range(2)]
    sil = [misc(512) for _ in range(2)]
    par_f = misc(128 * 2, F32)

    ps = [psum[:, b * 512:(b + 1) * 512] for b in range(8)]

    neglam = small[:, 0:1]
    rcp = small[:, 1:2]
    ssum = small[:, 2:3]
    rr = small[:, 3:4]
    e1 = small[:, 4:5]
    e2 = small[:, 5:6]
    mv = small[:, 8:10]
    rs = small[:, 10:11]
    stats = small[:, 16:28].rearrange("p (a b) -> p a b", b=6)
    fgt = small[:, 32:48]

    P.dma("pool", ident, cident, sem="c0")
    P.dma("pool", RT, crt, sem="c0")
    P.dma("sp", par_f[:, 0:16], ccol, sem="c1")
    P.dma("sp", fgt, fg, sem="c1")
    P.dma("sp", modAll, ada_b, sem="c1")
    n1t = bslot(0, 1, F32)[:, 0:64].rearrange("p (l j) -> p l j", j=16)
    n2t = bslot(1, 1, F32)[:, 0:64].rearrange("p (l j) -> p l j", j=16)
    P.dma("sp", n1t, n1g, sem="c1")
    P.dma("sp", n2t, n2g, sem="c1")
    P.memset(ones, 1.0)
    P.barrier()
    P.act(cact, par_f[:, 0:16], AF.Silu)
    P.barrier()

    for l in range(depth):
        wfree = [[], []]
        for ci in range(24):
            slot = ci % 2
            wv = wbuf[slot].rearrange("p (k n) -> p k n", n=512)
            t_w = P.dma("pool", wv, ada_w[l][:, ci * 512:(ci + 1) * 512].rearrange("(k p) n -> p k n", p=128),
                        deps=wfree[slot], sem="w%d" % slot)
            for sc in range(4):
                j = ci * 4 + sc
                for k in range(16):
                    t = P.mm(ps[0][:, j:j + 1], wv[:, k, sc * 128:(sc + 1) * 128], cact[:, k:k + 1],
                             start=(j == 0 and k == 0), stop=(k == 15), deps=[t_w], signal=(k == 15))
            wfree[slot] = [t]
        t = P.tt("dve", modAll[:, l, :], ps[0][:, 0:96], modAll[:, l, :], ALU.add, deps=[t])
        P.stt(geff[:, l, 0:16], modAll[:, l, 16:32], 1.0, n1t[:, l, :], ALU.add, ALU.mult, deps=[t])
        P.stt(geff[:, l, 16:32], modAll[:, l, 64:80], 1.0, n2t[:, l, :], ALU.add, ALU.mult, deps=[t])
        P.barrier()

    class WStream:
        def __init__(self):
            self.free = [[], []]
            self.n = 0
            self.fifo = []

        def _issue(self, key, fn):
            slot = self.n % 2
            self.n += 1
            t = None
            for i, (dst, src) in enumerate(fn(wbuf[slot])):
                t = P.dma("pool", dst, src, deps=self.free[slot] if i == 0 else (), sem="w%d" % slot)
            return (key, slot, t)

        def prefetch(self, key, fn):
            self.fifo.append(self._issue(key, fn))

        def take(self, key, fn):
            if self.fifo:
                k, slot, t = self.fifo.pop(0)
                assert k == key, (k, key)
                return slot, t
            _, slot, t = self._issue(key, fn)
            return slot, t

        def release(self, slot, tok):
            self.free[slot] = [tok]

    W = WStream()

    def kv(src):
        return src.rearrange("(k p) n -> p k n", p=128)

    def spec_in(l, ci):
        return ("in", l, ci), lambda wb: [(wb.rearrange("p (k n) -> p k n", n=512), kv(w_in[l][:, ci * 512:(ci + 1) * 512]))]

    def spec_mg(l, ci):
        def fn(wb):
            wv = wb.rearrange("p (a k n) -> p a k n", a=2, n=512)
            return [(wv[:, 0], kv(wupg[l][:, ci * 512:(ci + 1) * 512])), (wv[:, 1], kv(wupa[l][:, ci * 512:(ci + 1) * 512]))]
        return ("mg", l, ci), fn

    def spec_wo(l, ci):
        return ("wo", l, ci), lambda wb: [(wb.rearrange("p (k n) -> p k n", n=512), kv(w_o[l][:, ci * 512:(ci + 1) * 512]))]

    def spec_w13(l, hs, jc):
        def fn(wb):
            col0 = (hs * 22 + jc * 2) * 128
            wv = wb.rearrange("p (a k n) -> p a k n", a=2, n=256)
            return [(wv[:, 0], kv(w1[l][:, col0:col0 + 256])), (wv[:, 1], kv(w3[l][:, col0:col0 + 256]))]
        return ("w13", l, hs, jc), fn

    def spec_w2(l, hs, ci):
        return ("w2", l, hs, ci), lambda wb: [(wb[:, 0:22 * 256].rearrange("p (k n) -> p k n", n=256),
                                               kv(w2[l][hs * 2816:(hs + 1) * 2816, ci * 256:(ci + 1) * 256]))]

    def prefetch2(spec_fn, *args):
        for i in range(2):
            W.prefetch(*spec_fn(*args, i))

    def norm_phase(xsrc, half, g_ap, sh_ap, final=False):
        xg_rot = Rot([bslot(12, 8, F32).rearrange("p (c t) -> p c t", t=512),
                      bslot(0, 8, F32).rearrange("p (c t) -> p c t", t=512)])
        sq_rot = Rot([bslot(20, 1)[:, i * 512:(i + 1) * 512] for i in range(4)])
        tmp_rot = Rot([bslot(21, 1, F32)[:, i * 512:(i + 1) * 512] for i in range(2)])
        out_rot = Rot(XT)
        bank_rot = Rot([ps[0], ps[1]])
        rstd_bufs = [rstd_t, bslot(8, 1, F32)[:, 0:512]]
        rstd_free = [[], []]
        loads = {}

        def load_x(tg_):
            ixg_, xg_, xfree_ = xg_rot.next()
            t0_ = half * 2048 + tg_ * 512
            loads[tg_] = (ixg_, xg_, P.dma("sp", xg_, xsrc[:, t0_:t0_ + 512].rearrange("(c p) t -> p c t", p=128),
                                           deps=xfree_, sem="xg%d" % ixg_))

        def part_A(tg):
            ixg, xg, t_ld = loads.pop(tg)
            ib, bank, bfree = bank_rot.next()
            rstd = rstd_bufs[tg % 2]
            t_mm = None
            for c in range(16):
                i, sqb, fr = sq_rot.next()
                t_sq = P.act(sqb, xg[:, c, :], AF.Square, deps=[t_ld] + fr)
                t_mm = P.mm(bank, ones, sqb, start=(c == 0), stop=(c == 15),
                            deps=[t_sq] + (bfree if c == 0 else []), signal=True)
                sq_rot.free[i] = [t_mm]
            t1 = P.ts("dve", rstd, bank, 1.0 / D, EPS, ALU.mult, ALU.add, deps=[t_mm] + rstd_free[tg % 2])
            bank_rot.free[ib] = [t1]
            t2 = P.act(rstd, rstd, AF.Sqrt, deps=[t1])
            t3 = P.recip(rstd, rstd, deps=[t2])
            return (tg, ixg, xg, rstd, t3)

        def part_B(tg, ixg, xg, rstd, t3):
            tok0 = half * 2048 + tg * 512
            last = []
            t_a = None
            for c in range(16):
                i, tb, fr = tmp_rot.next()
                t_a = P.tt("dve", tb, xg[:, c, :], rstd, ALU.mult, deps=[t3] + fr)
                if not final:
                    t_b = P.act(bigA[:, c, tg * 512:(tg + 1) * 512], tb, AF.Identity, deps=[t_a],
                                scale=g_ap[:, c:c + 1], bias=sh_ap[:, c:c + 1])
                    tmp_rot.free[i] = [t_b]
                else:
                    io, ob, fro = out_rot.next()
                    t_b = P.act(ob, tb, AF.Identity, deps=[t_a] + fro, scale=g_ap[:, c:c + 1])
                    tmp_rot.free[i] = [t_b]
                    t_s = P.dma("sp", yT[c * 128:(c + 1) * 128, tok0:tok0 + 512], ob, deps=[t_b], sem="xo%d" % io)
                    out_rot.free[io] = [t_s]
                last = [t_a, t_b]
            xg_rot.free[ixg] = last
            rstd_free[tg % 2] = [t_a]
            if tg + 2 < 4:
                load_x(tg + 2)

        load_x(0)
        load_x(1)
        pend = part_A(0)
        for tg in range(1, 4):
            nxt = part_A(tg)
            part_B(*pend)
            pend = nxt
        part_B(*pend)
        P.barrier()

    def resid_update(bank, t_mm, c, tok0, xsrc, gt_ap, xrot, bank_rot_free_cb):
        io, xt, fro = xrot.next()
        t_x = P.dma("sp", xt, xsrc[c * 128:(c + 1) * 128, tok0:tok0 + 512], deps=fro, sem="xi%d" % io)
        t_u = P.stt(xt, bank, gt_ap[:, c:c + 1], xt, ALU.mult, ALU.add, deps=[t_mm, t_x])
        bank_rot_free_cb([t_u])
        t_s = P.dma("sp", XS[c * 128:(c + 1) * 128, tok0:tok0 + 512], xt, deps=[t_u], sem="xi%d" % io)
        xrot.free[io] = [t_s]

    for l in range(depth):
        lambda_init = 0.8 - 0.6 * math.exp(-0.3 * l)
        xsrc = xT if l == 0 else XS
        sh1 = modAll[:, l, 0:16]
        gt1 = modAll[:, l, 32:48]
        sh2 = modAll[:, l, 48:64]
        gt2 = modAll[:, l, 80:96]
        g1 = geff[:, l, 0:16]
        g2 = geff[:, l, 16:32]

        par = l % 2
        if l == 0:
            prefetch2(spec_in, l)
        P.dma("pool", lng_t, lng[l], sem="c0")
        P.dma("pool", lnb_t, lnb[l], sem="c0")
        P.dma("pool", ws_t, wsT[l], sem="c0")
        P.dma("sp", bs_t, bsb[l], sem="c1")
        P.dma("sp", subg_t, subg[l], sem="c1")
        P.dma("sp", lamt, lamv[l], sem="c1")
        P.barrier()
        t = P.tt("dve", lamt[:, 0, :], lamt[:, 0, :], lamt[:, 1, :], ALU.mult)
        t = P.op("dve", lambda e: e.reduce_sum(out=e1, in_=lamt[:, 0, :], axis=AX), deps=[t])
        t = P.act(e1, e1, AF.Exp, deps=[t])
        t2 = P.tt("dve", lamt[:, 2, :], lamt[:, 2, :], lamt[:, 3, :], ALU.mult)
        t2 = P.op("dve", lambda e: e.reduce_sum(out=e2, in_=lamt[:, 2, :], axis=AX), deps=[t2])
        t2 = P.act(e2, e2, AF.Exp, deps=[t2])
        t = P.tt("dve", neglam, e2, e1, ALU.subtract, deps=[t, t2])
        t = P.ts("dve", neglam, neglam, -lambda_init, None, ALU.add, deps=[t])
        P.ts("dve", subg_t, subg_t, 1.0 - lambda_init, None, ALU.mult)
        P.barrier()

        for half in range(1):
            hb = half * 2048
            norm_phase(xsrc, half, g1, sh1)
            hT = bigA
            uT = bigB[:, 0:8, :]
            cosb = bslot(8, 1)
            sinb = bslot(10, 1)
            P.dma("pool", cosb[:, 0:SL], ccos, sem="c0")
            P.dma("pool", sinb[:, 0:SL], csin, sem="c0")
            vg = bslot(12, 1, F32)
            tmp2 = bslot(13, 1, F32)
            vn = bslot(14, 1)[:, 0:1024]
            vst_rot = Rot([bslot(14, 1)[:, 1024:2048], bslot(15, 1)[:, 0:1024]])
            stage_rot = Rot([bslot(16, 1)[:, i * 512:(i + 1) * 512] for i in range(4)])
            qb_rot = Rot([bslot(17, 1)[:, i * 512:(i + 1) * 512] for i in range(2)])
            f32_rot = Rot([bslot(18 + i // 2, 1, F32)[:, (i % 2) * 512:(i % 2 + 1) * 512] for i in range(4)])
            P.barrier()
            bank_rot = Rot([ps[0], ps[1], ps[2], ps[3]])
            pair_rot = Rot([psum[:, 2048:3072], psum[:, 3072:4096]])
            t_w = {}
            w_slot = {}

            def load_w(ci):
                w_slot[ci], t_w[ci] = W.take(*spec_in(l, ci))
                return wbuf[w_slot[ci]].rearrange("p (k n) -> p k n", n=512)

            def fm_piece(ci, kind, jbase):
                wv = load_w(ci)
                slot = w_slot[ci]
                rope_pend = []
                for sc in range(4):
                    j = ci * 4 + sc - jbase
                    for tg in range(4):
                        ib, bank, bfree = bank_rot.next()
                        tsl = slice(tg * 512, (tg + 1) * 512)
                        gsl = slice(hb + tg * 512, hb + (tg + 1) * 512)
                        for k in range(16):
                            t_mm = P.mm(bank, wv[:, k, sc * 128:(sc + 1) * 128], hT[:, k, tsl], start=(k == 0),
                                        stop=(k == 15), deps=([t_w[ci]] + bfree) if k == 0 else (), signal=(k == 15))
                        if kind == "u":
                            t = P.act(uT[:, j, tsl], bank, AF.Gelu_apprx_tanh, deps=[t_mm])
                            bank_rot.free[ib] = [t]
                        elif kind in ("ga", "gb"):
                            i4, st, frs = stage_rot.next()
                            t = P.act(st, bank, AF.Sigmoid, deps=[t_mm] + frs)
                            bank_rot.free[ib] = [t]
                            dst = SAd if kind == "ga" else SBd
                            t_s = P.dma("sp", dst[j][:, gsl], st, deps=[t], sem="st%d" % i4)
                            stage_rot.free[i4] = [t_s]
                        else:
                            iq, qb_, frq = qb_rot.next()
                            t_c = P.act(qb_, bank, AF.Copy, deps=[t_mm] + frq)
                            bank_rot.free[ib] = [t_c]

                            def rope_part(iq=iq, qb_=qb_, t_c=t_c, j=j, gsl=gsl):
                                ib2, bank2, bfree2 = bank_rot.next()
                                t_r = P.mm(bank2, RT, qb_, start=True, stop=True, deps=[t_c] + bfree2, signal=True)
                                ia, ta, fra = f32_rot.next()
                                t_a = P.tt("dve", ta, qb_, cosb[:, gsl], ALU.mult, deps=[t_c] + fra)
                                ibb, tb_, frb = f32_rot.next()
                                t_b = P.tt("dve", tb_, bank2, sinb[:, gsl], ALU.mult, deps=[t_r] + frb)
                                bank_rot.free[ib2] = [t_b]
                                qb_rot.free[iq] = [t_r, t_a]
                                i4, st, frs = stage_rot.next()
                                t_o = P.tt("dve", st, ta, tb_, ALU.add, deps=[t_a, t_b] + frs)
                                f32_rot.free[ia] = [t_o]
                                f32_rot.free[ibb] = [t_o]
                                if kind == "q":
                                    dst_ap = QTd[j][:, gsl]
                                else:
                                    dst_ap = KL[par][j // 2][(j % 2) * 128:(j % 2 + 1) * 128, gsl]
                                t_s = P.dma("sp", dst_ap, st, deps=[t_o], sem="st%d" % i4)
                                stage_rot.free[i4] = [t_s]

                            while rope_pend:
                                rope_pend.pop(0)()
                            rope_pend.append(rope_part)
                while rope_pend:
                    rope_pend.pop(0)()
                W.release(slot, t_mm)

            def tm_v(ci0):
                wvs = [load_w(ci0), load_w(ci0 + 1)]
                sets = [(bslot(12, 1, F32), bslot(13, 1, F32), bslot(14, 1)[:, 0:1024]),
                        (bslot(9, 1, F32), bslot(11, 1, F32), bslot(20, 1)[:, 0:1024])]
                set_free = [[], []]
                mainb, spb = pair_rot.bufs[0], pair_rot.bufs[1]
                fr = {"main": list(pair_rot.free[0]), "sp": list(pair_rot.free[1])}
                t_mm = None

                def sp_part(tt_, tmp2_, vn_, t_n_):
                    tsl_ = slice(tt_ * 128, (tt_ + 1) * 128)
                    for g in range(8):
                        t_sp = P.mm(spb[:, g * 128:(g + 1) * 128], vn_[:, g * 128:(g + 1) * 128], ws_t[:, g, :],
                                    start=True, stop=True, deps=([t_n_] + fr["sp"]) if g == 0 else (), signal=(g == 7))
                    t_m = P.tt("dve", tmp2_, spb, bs_t, ALU.add, deps=[t_sp])
                    fr["sp"] = [t_m]
                    t_u = P.tt("dve", uT[:, :, tsl_], tmp2_.rearrange("p (g q) -> p g q", q=128), uT[:, :, tsl_],
                               ALU.mult, deps=[t_m])
                    set_free[tt_ % 2] = [t_u]

                pend = None
                for tt_ in range(16):
                    vg_, tmp2_, vn_ = sets[tt_ % 2]
                    tsl = slice(tt_ * 128, (tt_ + 1) * 128)
                    for cc in range(2):
                        for k in range(16):
                            first = (cc == 0 and k == 0)
                            t_mm = P.mm(mainb[:, cc * 512:(cc + 1) * 512], hT[:, k, tsl], wvs[cc][:, k, :],
                                        start=(k == 0), stop=(k == 15),
                                        deps=([t_w[ci0], t_w[ci0 + 1]] + fr["main"]) if first else (),
                                        signal=(cc == 1 and k == 15))
                    t_g = P.act(vg_, mainb, AF.Gelu_apprx_tanh, deps=[t_mm] + set_free[tt_ % 2])
                    fr["main"] = [t_g]
                    ta_ = P.op("dve", lambda e, v=vg_: e.bn_stats(out=stats[:, 0, :], in_=v[:, 0:512]), deps=[t_g])
                    tb2 = P.op("dve", lambda e, v=vg_: e.bn_stats(out=stats[:, 1, :], in_=v[:, 512:1024]), deps=[t_g])
                    t_ag = P.op("dve", lambda e: e.bn_aggr(out=mv, in_=small[:, 16:28]), deps=[ta_, tb2])
                    t_r1 = P.ts("dve", rs, mv[:, 1:2], EPS, None, ALU.add, deps=[t_ag])
                    t_r2 = P.act(rs, rs, AF.Sqrt, deps=[t_r1])
                    t_r3 = P.recip(rs, rs, deps=[t_r2])
                    t_n = P.ts("dve", vg_, vg_, mv[:, 0:1], rs, ALU.subtract, ALU.mult, deps=[t_r3])
                    t_n = P.tt("dve", vg_, vg_, lng_t, ALU.mult, deps=[t_n])
                    t_n = P.tt("dve", vn_, vg_, lnb_t, ALU.add, deps=[t_n])
                    if pend is not None:
                        sp_part(*pend)
                    pend = (tt_, tmp2_, vn_, t_n)
                sp_part(*pend)
                pair_rot.free[0] = fr["main"]
                pair_rot.free[1] = fr["sp"]
                pair_rot.i = 0
                W.release(w_slot[ci0], t_mm)
                W.release(w_slot[ci0 + 1], t_mm)

            def tm_piece(ci0, kind):
                if kind == "v":
                    return tm_v(ci0)
                wv0 = load_w(ci0)
                wv1 = load_w(ci0 + 1)
                wvs = [wv0, wv1]
                t_u_last = None
                for tt_ in range(16):
                    ip, pair, pfree = pair_rot.next()
                    tsl = slice(tt_ * 128, (tt_ + 1) * 128)
                    for cc in range(2):
                        for k in range(16):
                            first = (cc == 0 and k == 0)
                            t_mm = P.mm(pair[:, cc * 512:(cc + 1) * 512], hT[:, k, tsl], wvs[cc][:, k, :],
                                        start=(k == 0), stop=(k == 15),
                                        deps=([t_w[ci0], t_w[ci0 + 1]] + pfree) if first else (),
                                        signal=(cc == 1 and k == 15))
                    if kind == "va":
                        iv, vs, frv = vst_rot.next()
                        t = P.act(vs, pair, AF.Copy, deps=[t_mm] + frv)
                        pair_rot.free[ip] = [t]
                        for c_ in range(4):
                            t_s = P.dma("sp", VL[par][c_][tt_ * 128:(tt_ + 1) * 128, :], vs[:, c_ * 256:(c_ + 1) * 256],
                                        deps=[t], sem="vs%d" % iv)
                        vst_rot.free[iv] = [t_s]
                    else:
                        t_g = P.act(vg, pair, AF.Gelu_apprx_tanh, deps=[t_mm] + ([t_u_last] if t_u_last else []))
                        pair_rot.free[ip] = [t_g]
                        ta_ = P.op("dve", lambda e: e.bn_stats(out=stats[:, 0, :], in_=vg[:, 0:512]), deps=[t_g])
                        tb2 = P.op("dve", lambda e: e.bn_stats(out=stats[:, 1, :], in_=vg[:, 512:1024]), deps=[t_g])
                        t_ag = P.op("dve", lambda e: e.bn_aggr(out=mv, in_=small[:, 16:28]), deps=[ta_, tb2])
                        t_r1 = P.ts("dve", rs, mv[:, 1:2], EPS, None, ALU.add, deps=[t_ag])
                        t_r2 = P.act(rs, rs, AF.Sqrt, deps=[t_r1])
                        t_r3 = P.recip(rs, rs, deps=[t_r2])
                        t_n = P.ts("dve", vg, vg, mv[:, 0:1], rs, ALU.subtract, ALU.mult, deps=[t_r3])
                        t_n = P.tt("dve", vg, vg, lng_t, ALU.mult, deps=[t_n])
                        t_n = P.tt("dve", vn, vg, lnb_t, ALU.add, deps=[t_n] + ([t_u_last] if t_u_last else []))
                        ip2, pair2, pfree2 = pair_rot.next()
                        for g in range(8):
                            t_sp = P.mm(pair2[:, g * 128:(g + 1) * 128], vn[:, g * 128:(g + 1) * 128], ws_t[:, g, :],
                                        start=True, stop=True, deps=([t_n] + pfree2) if g == 0 else (), signal=(g == 7))
                        t_m = P.tt("dve", tmp2, pair2, bs_t, ALU.add, deps=[t_sp])
                        pair_rot.free[ip2] = [t_m]
                        t_u_last = P.tt("dve", uT[:, :, tsl], tmp2.rearrange("p (g q) -> p g q", q=128), uT[:, :, tsl],
                                        ALU.mult, deps=[t_m])
                W.release(w_slot[ci0], t_mm)
                W.release(w_slot[ci0 + 1], t_mm)

            fm_piece(0, "u", 0)
            fm_piece(1, "u", 0)
            tm_piece(2, "v")
            fm_piece(4, "q", 16)
            fm_piece(5, "q", 16)
            fm_piece(6, "k", 24)
            fm_piece(7, "k", 24)
            tm_piece(8, "va")
            for ci in range(10, 14):
                if ci == 10:
                    kdeps = P.dtoks(["st0", "st1", "st2", "st3"])
                    for c_ in range(4):
                        P.coll(KL[par][c_], KA[par][c_], deps=kdeps, sem="cc")
                if ci == 12:
                    vdeps = P.dtoks(["vs0", "vs1"])
                    for c_ in range(4):
                        P.coll(VL[par][c_], VA[par][c_], deps=vdeps, sem="cc")
                fm_piece(ci, "ga", 40)
            for ci in range(14, 18):
                fm_piece(ci, "gb", 56)
            P.barrier()

        def aslot(s0, ns, dt=BF16):
            return carve(A0 + s0 * 2048, ns * 2048, dt)

        KTa = [aslot(0, 2), aslot(2, 2)]
        KTb = [aslot(4, 2), aslot(6, 2)]
        QTb = [aslot(8, 1), aslot(9, 1)]
        Vau = [aslot(10, 3)[:, 0:32 * 129].rearrange("p (k d) -> p k d", d=129),
               aslot(13, 3)[:, 0:32 * 129].rearrange("p (k d) -> p k d", d=129)]
        PT_rot = Rot([bslot(18, 1)[:, i * 512:(i + 1) * 512] for i in range(4)])
        onorm = [[bslot(19, 1, F32)[:, (m * 4 + j) * 128:(m * 4 + j + 1) * 128] for j in range(4)] for m in range(2)]
        onorm_all = [bslot(19, 1, F32)[:, m * 512:(m + 1) * 512] for m in range(2)]
        ocomb4 = bslot(20, 1, F32)[:, 0:512]
        junk4 = bslot(20, 1, F32)[:, 512:1024]
        obf4 = bslot(21, 1)[:, 0:512]
        rcp4 = small[:, 48:52]
        ssum4 = small[:, 52:56]
        rr4 = small[:, 56:60]
        brB_w = bigB[:, 8:16, :]
        prefetch2(spec_mg, l)
        for b in range(2):
            P.memset(KTa[b][64:128, :], 0.0)
            P.memset(KTb[b][0:64, :], 0.0)
            P.memset(Vau[b][:, :, 128:129], 1.0)
        P.barrier()
        S_rot = Rot([ps[0], ps[1], ps[2]])
        O_rot = Rot([psum[:, 1536:2560], psum[:, 2560:3584]])
        psT = ps[7].bitcast(BF16)
        psT_free = []
        hfree = [[], []]
        on_free = [[], []]
        ep = {"oc": [], "rr": [], "obf": [], "psT": [], "t_rr": None}
        pending = []

        def run_pending(kc):
            keep = []
            for trig, fn in pending:
                if trig <= kc:
                    fn()
                else:
                    keep.append((trig, fn))
            pending[:] = keep

        for h in range(8):
            buf = h % 2
            hc, hl = h // 2, h % 2
            tl = []
            for r_ in range(2):
                r0 = r_ * 256 + hl * 128
                tl.append(P.dma("sp", KTa[buf][0:64, r_ * SL:(r_ + 1) * SL], KA[par][hc][r0:r0 + 64, :],
                                deps=hfree[buf] if r_ == 0 else (), sem="hk%d" % buf))
                tl.append(P.dma("sp", KTb[buf][64:128, r_ * SL:(r_ + 1) * SL], KA[par][hc][r0 + 64:r0 + 128, :],
                                sem="hk%d" % buf))
            tl.append(P.dma("sp", QTb[buf], QTd[h], sem="hk%d" % buf))
            for r_ in range(2):
                tl.append(P.dma("sp", Vau[buf][:, r_ * 16:(r_ + 1) * 16, 0:128],
                                VA[par][hc][r_ * SL:(r_ + 1) * SL, hl * 128:(hl + 1) * 128].rearrange(
                                    "(k p) d -> p k d", p=128), sem="hk%d" % buf))
            t_ld = tl[-1]
            last_pe = None
            for qg in range(SL // 512):
                qsl = slice(qg * 512, (qg + 1) * 512)
                t_on_m = [None, None]
                for m in range(2):
                    KTm = KTa[buf] if m == 0 else KTb[buf]
                    io, Ob, ofree = O_rot.next()
                    Oj = [Ob[:, (j // 2) * 512 + (j % 2) * 129:(j // 2) * 512 + (j % 2) * 129 + 129] for j in range(4)]
                    sinfo = {}

                    def issue_S(kc):
                        ib, bank, bfree = S_rot.next()
                        t = P.mm(bank, KTm[:, kc * 128:(kc + 1) * 128], QTb[buf][:, qsl], start=True, stop=True,
                                 deps=[t_ld] + bfree, signal=True)
                        sinfo[kc] = (ib, bank, t)

                    issue_S(0)
                    issue_S(1)
                    for kc in range(32):
                        ib, bank, t_S = sinfo.pop(kc)
                        ipt, pt, frp = PT_rot.next()
                        t_e = P.act(pt, bank, AF.Exp, deps=[t_S] + frp, scale=0.125)
                        S_rot.free[ib] = [t_e]
                        if kc + 2 < 32:
                            issue_S(kc + 2)
                        run_pending(kc)
                        for j in range(4):
                            t_pv = P.mm(Oj[j], pt[:, j * 128:(j + 1) * 128], Vau[buf][:, kc, :],
                                        start=(kc == 0 and j % 2 == 0), stop=(kc == 31),
                                        deps=([t_e] + (ofree if kc == 0 else [])) if j == 0 else (), signal=(j == 3))
                        PT_rot.free[ipt] = [t_pv]
                    last_pe = t_pv
                    assert not pending
                    t_on = None
                    for j in range(4):
                        t_r = P.recip(rcp4[:, j:j + 1], Oj[j][:, 128:129], deps=[t_pv] + on_free[m])
                        t_on = P.ts("dve", onorm[m][j], Oj[j][:, 0:128], rcp4[:, j:j + 1], None, ALU.mult, deps=[t_r])
                    O_rot.free[io] = [t_on]
                    t_on_m[m] = t_on
                    if m == 1:
                        t_c = P.stt(ocomb4, onorm_all[1], neglam, onorm_all[0], ALU.mult, ALU.add,
                                    deps=[t_on_m[0], t_on_m[1]] + ep["oc"])
                        on_free[0] = [t_c]
                        on_free[1] = [t_c]
                        t_q = P.tt("dve", junk4, ocomb4, ocomb4, ALU.mult, deps=[t_c])
                        for j in range(4):
                            t_q = P.op("dve", lambda e, j=j: e.reduce_sum(out=ssum4[:, j:j + 1],
                                                                          in_=junk4[:, j * 128:(j + 1) * 128], axis=AX),
                                       deps=[t_q])
                        t_q = P.ts("dve", rr4, ssum4, 1.0 / 128, EPS, ALU.mult, ALU.add, deps=[t_q] + ep["rr"])

                        def epi_act(t_q=t_q):
                            t1 = P.act(rr4, rr4, AF.Ln, deps=[t_q])
                            ep["t_rr"] = P.act(rr4, rr4, AF.Exp, deps=[t1], scale=-0.5)

                        def epi_pe(h=h, qsl=qsl):
                            t_ob = None
                            for j in range(4):
                                t_ob = P.stt(obf4[:, j * 128:(j + 1) * 128], ocomb4[:, j * 128:(j + 1) * 128],
                                             rr4[:, j:j + 1], subg_t, ALU.mult, ALU.mult,
                                             deps=([ep["t_rr"]] + ep["obf"]) if j == 0 else ())
                            ep["rr"] = [t_ob]
                            ep["oc"] = [t_ob]
                            t_tr = None
                            for j in range(4):
                                t_tr = P.transpose(psT[:, j * 128:(j + 1) * 128], obf4[:, j * 128:(j + 1) * 128], ident,
                                                   deps=([t_ob] + ep["psT"]) if j == 0 else ())
                            ep["obf"] = [t_tr]
                            t_cp = P.copy("dve", brB_w[:, h, qsl], psT[:, 0:512], deps=[t_tr])
                            ep["psT"] = [t_cp]

                        pending.append((8, epi_act))
                        pending.append((12, epi_pe))
            hfree[buf] = [last_pe]
        run_pending(99)
        P.barrier()

        for half in range(1):
            hb = half * 2048
            brA = bigB[:, 0:8, :]
            brB = bigB[:, 8:16, :]
            merged = bigA
            sa_rot = Rot([bslot(16, 1), bslot(17, 1)])
            sb_rot = Rot([bslot(18, 1), bslot(19, 1)])
            t1_rot = Rot([bslot(20, 1, F32)[:, i * 512:(i + 1) * 512] for i in range(2)])
            t2_rot = Rot([bslot(21, 1, F32)[:, i * 512:(i + 1) * 512] for i in range(2)])
            bank_rot = Rot([ps[0], ps[1], ps[2], ps[3], ps[4], ps[5]])
            for ci in range(4):
                slot, t_w = W.take(*spec_mg(l, ci))
                wv = wbuf[slot].rearrange("p (a k n) -> p a k n", a=2, n=512)
                for sc in range(4):
                    c = ci * 4 + sc
                    isa, sat, frsa = sa_rot.next()
                    isb, sbt, frsb = sb_rot.next()
                    t_sa = P.dma("sp", sat, SAd[c][:, hb:hb + 2048], deps=frsa, sem="sa%d" % isa)
                    t_sb = P.dma("sp", sbt, SBd[c][:, hb:hb + 2048], deps=frsb, sem="sb%d" % isb)
                    for tg in range(4):
                        tsl = slice(tg * 512, (tg + 1) * 512)
                        ia_, bka, fa = bank_rot.next()
                        for k in range(8):
                            t_ma = P.mm(bka, wv[:, 0, k, sc * 128:(sc + 1) * 128], brA[:, k, tsl], start=(k == 0),
                                        stop=(k == 7), deps=([t_w] + fa) if k == 0 else (), signal=(k == 7))
                        ib_, bkb, fb = bank_rot.next()
                        for k in range(8):
                            t_mb = P.mm(bkb, wv[:, 1, k, sc * 128:(sc + 1) * 128], brB[:, k, tsl], start=(k == 0),
                                        stop=(k == 7), deps=fb if k == 0 else (), signal=(k == 7))
                        i1, t1b, f1 = t1_rot.next()
                        i2, t2b, f2 = t2_rot.next()
                        ta_ = P.tt("dve", t1b, bka, sat[:, tsl], ALU.mult, deps=[t_ma, t_sa] + f1)
                        tb2 = P.tt("dve", t2b, bkb, sbt[:, tsl], ALU.mult, deps=[t_mb, t_sb] + f2)
                        bank_rot.free[ia_] = [ta_]
                        bank_rot.free[ib_] = [tb2]
                        t_m = P.tt("pool", merged[:, c, tsl], t1b, t2b, ALU.add, deps=[ta_, tb2])
                        t1_rot.free[i1] = [t_m]
                        t2_rot.free[i2] = [t_m]
                    sa_rot.free[isa] = [ta_]
                    sb_rot.free[isb] = [tb2]
                W.release(slot, t_mb)
            prefetch2(spec_wo, l)
            P.barrier()
            xrot = Rot(XT + [bslot(i // 2, 1, F32)[:, (i % 2) * 512:(i % 2 + 1) * 512] for i in range(4)])
            for ci in range(4):
                slot, t_w = W.take(*spec_wo(l, ci))
                wv = wbuf[slot].rearrange("p (k n) -> p k n", n=512)
                for sc in range(4):
                    c = ci * 4 + sc
                    for tg in range(4):
                        tsl = slice(tg * 512, (tg + 1) * 512)
                        ib, bank, bfree = bank_rot.next()
                        for k in range(16):
                            t_mm = P.mm(bank, wv[:, k, sc * 128:(sc + 1) * 128], merged[:, k, tsl], start=(k == 0),
                                        stop=(k == 15), deps=([t_w] + bfree) if k == 0 else (), signal=(k == 15))

                        def cb(toks, ib=ib):
                            bank_rot.free[ib] = toks
                        resid_update(bank, t_mm, c, hb + tg * 512, xsrc, gt1, xrot, cb)
                W.release(slot, t_mm)
            prefetch2(spec_w13, l, 0)
            P.barrier()

        for half in range(1):
            hb = half * 2048
            norm_phase(XS, half, g2, sh2)
            hT = bigA
            hid = bigB
            bank_rot = Rot([ps[0], ps[1], ps[2], ps[3], ps[4], ps[5]])
            sil_rot = Rot(sil)
            xrot = Rot(XT)
            for hs in range(2):
                for jc in range(11):
                    slot, t_w = W.take(*spec_w13(l, hs, jc))
                    wv = wbuf[slot].rearrange("p (a k n) -> p a k n", a=2, n=256)
                    for sc in range(2):
                        jj = jc * 2 + sc
                        for tg in range(4):
                            tsl = slice(tg * 512, (tg + 1) * 512)
                            i1, b1, f1 = bank_rot.next()
                            for k in range(16):
                                t_m1 = P.mm(b1, wv[:, 0, k, sc * 128:(sc + 1) * 128], hT[:, k, tsl], start=(k == 0),
                                            stop=(k == 15), deps=([t_w] + f1) if k == 0 else (), signal=(k == 15))
                            i3, b3, f3 = bank_rot.next()
                            for k in range(16):
                                t_m3 = P.mm(b3, wv[:, 1, k, sc * 128:(sc + 1) * 128], hT[:, k, tsl], start=(k == 0),
                                            stop=(k == 15), deps=f3 if k == 0 else (), signal=(k == 15))
                            isl, sb_, fs = sil_rot.next()
                            t_s = P.act(sb_, b1, AF.Silu, deps=[t_m1] + fs)
                            bank_rot.free[i1] = [t_s]
                            t_h = P.tt("dve", hid[:, jj, tsl], b3, sb_, ALU.mult, deps=[t_m3, t_s])
                            bank_rot.free[i3] = [t_h]
                            sil_rot.free[isl] = [t_h]
                    W.release(slot, t_m3)
                prefetch2(spec_w2, l, hs)
                P.barrier()
                for ci in range(8):
                    slot, t_w = W.take(*spec_w2(l, hs, ci))
                    wv = wbuf[slot][:, 0:22 * 256].rearrange("p (k n) -> p k n", n=256)
                    for sc in range(2):
                        c = ci * 2 + sc
                        for tg in range(4):
                            tsl = slice(tg * 512, (tg + 1) * 512)
                            ib, bank, bfree = bank_rot.next()
                            for k in range(22):
                                t_mm = P.mm(bank, wv[:, k, sc * 128:(sc + 1) * 128], hid[:, k, tsl], start=(k == 0),
                                            stop=(k == 21), deps=([t_w] + bfree) if k == 0 else (), signal=(k == 21))

                            def cb(toks, ib=ib):
                                bank_rot.free[ib] = toks
                            resid_update(bank, t_mm, c, hb + tg * 512, XS, gt2, xrot, cb)
                    W.release(slot, t_mm)
                if hs == 0:
                    prefetch2(spec_w13, l, 1)
                elif l + 1 < depth:
                    prefetch2(spec_in, l + 1)
                P.barrier()

    for half in range(1):
        norm_phase(XS, half, fgt, None, final=True)
    P.barrier()

    with nc.Block() as block:
        @block.tensor
        def _(e):
            for f in P.streams["pe"]:
                f(e)

        @block.scalar
        def _(e):
            for f in P.streams["act"]:
                f(e)

        @block.vector
        def _(e):
            for f in P.streams["dve"]:
                f(e)

        @block.gpsimd
        def _(e):
            for f in P.streams["pool"]:
                f(e)

        @block.sync
        def _(e):
            for f in P.streams["sp"]:
                f(e)
    return nc


def rope_consts():
    j = np.arange(32, dtype=np.float32)
    inv_freq = (10000.0 ** (-(2.0 * j) / 64.0)).astype(np.float32)
    pos = np.arange(S, dtype=np.float32)
    ang = pos[None, :] * inv_freq[:, None]
    cos = np.cos(ang).astype(np.float32)
    sin = np.sin(ang).astype(np.float32)
    cosT = np.tile(cos, (4, 1))
    sinT = np.tile(sin, (4, 1))
    rt = np.zeros((128, 128), np.float32)
    for m in range(2):
        for jj in range(32):
            i0 = m * 64 + jj
            i1 = m * 64 + 32 + jj
            rt[i1, i0] = -1.0
            rt[i0, i1] = 1.0
    return cosT, sinT, rt, np.eye(128, dtype=np.float32)


def make_in_maps(x, c, ada_w, ada_b, norm1_g, norm2_g, w_in, gmlp_ln_g, gmlp_ln_b, w_s, b_s,
                 lambda_q1, lambda_k1, lambda_q2, lambda_k2, subln_g, w_up_gmlp, w_up_attn, w_o,
                 ffn_w1, ffn_w3, ffn_w2, final_g):
    f = lambda a: np.ascontiguousarray(np.asarray(a, dtype=np.float32))
    cosT, sinT, rt, ident = rope_consts()
    L = DEPTH
    shared = {
        "ada_w": f(ada_w),
        "ada_b": f(np.asarray(ada_b).reshape(L, 96, 128).transpose(2, 0, 1)),
        "n1g": f(np.asarray(norm1_g).reshape(L, 16, 128).transpose(2, 0, 1)),
        "n2g": f(np.asarray(norm2_g).reshape(L, 16, 128).transpose(2, 0, 1)),
        "fg": f(np.asarray(final_g).reshape(16, 128).T),
        "w_in": f(w_in),
        "lng": f(np.broadcast_to(np.asarray(gmlp_ln_g)[:, None, :], (L, 128, 1024))),
        "lnb": f(np.broadcast_to(np.asarray(gmlp_ln_b)[:, None, :], (L, 128, 1024))),
        "wsT": f(np.asarray(w_s).transpose(0, 3, 1, 2)),
        "bsb": f(np.broadcast_to(np.asarray(b_s).reshape(L, 1, 1024), (L, 128, 1024))),
        "lamv": f(np.broadcast_to(np.stack([np.asarray(lambda_q1), np.asarray(lambda_k1), np.asarray(lambda_q2),
                                            np.asarray(lambda_k2)], axis=1)[:, None], (L, 128, 4, 64))),
        "subg": f(np.broadcast_to(np.asarray(subln_g)[:, None, :], (L, 128, 128))),
        "wupg": f(w_up_gmlp), "wupa": f(w_up_attn), "w_o": f(w_o),
        "w1": f(ffn_w1), "w3": f(ffn_w3), "w2": f(ffn_w2),
        "cident": ident, "crt": rt,
    }
    x = np.asarray(x, dtype=np.float32)
    c = np.asarray(c, dtype=np.float32)
    in_maps = []
    for core in range(NCORES):
        b, r = core // 2, core % 2
        m = dict(shared)
        m["xT"] = np.ascontiguousarray(x[b, r * SL:(r + 1) * SL, :].T)
        m["ccol"] = np.ascontiguousarray(c[b].reshape(16, 128).T)
        m["ccos"] = np.ascontiguousarray(cosT[:, r * SL:(r + 1) * SL])
        m["csin"] = np.ascontiguousarray(sinT[:, r * SL:(r + 1) * SL])
        in_maps.append(m)
    return in_maps


def kernel(**inputs):
    in_maps = make_in_maps(**inputs)
    nc = build_program()
    res = run_bass_kernel_spmd(nc, in_maps, core_ids=list(range(NCORES)))
    out = np.empty((NCORES // 2, S, D), np.float32)
    for core in range(NCORES):
        b, r = core // 2, core % 2
        out[b, r * SL:(r + 1) * SL, :] = np.asarray(res.results[core]["yT"]).T
    return out
```

```python
import math
import numpy as np
import concourse.bass as bass
import concourse.mybir as mybir
from concourse.bass_utils import run_bass_kernel_spmd

F32 = mybir.dt.float32
BF16 = mybir.dt.bfloat16
AF = mybir.ActivationFunctionType
ALU = mybir.AluOpType
AX = mybir.AxisListType.X

D = 2048
S = 4096
SL = 2048
DEPTH = 4
DIN = 9216
FH = 5632
NCORES = 8
RG = [[0, 1], [2, 3], [4, 5], [6, 7]]
EPS = 1e-6
ENG = ["pe", "act", "dve", "pool", "sp"]


class Prog:
    def __init__(self, nc):
        self.nc = nc
        self.streams = {e: [] for e in ENG}
        self.psem = {e: nc.alloc_semaphore(name="prog_" + e) for e in ENG}
        self.cnt = {e: 0 for e in ENG}
        self.known = {e: {} for e in ENG}
        self.dsems = {}

    def _wait(self, eng, tok):
        if tok is None:
            return
        key, handle, val = tok
        if self.known[eng].get(key, 0) >= val:
            return
        self.known[eng][key] = val
        self.streams[eng].append(lambda e, h=handle, v=val: e.wait_ge(h, v))

    def op(self, eng, fn, deps=(), signal=True):
        for d in deps:
            self._wait(eng, d)
        if signal:
            self.cnt[eng] += 1
            h = self.psem[eng]
            self.streams[eng].append(lambda e, fn=fn, h=h: fn(e).then_inc(h, 1))
            return (eng, h, self.cnt[eng])
        self.streams[eng].append(lambda e, fn=fn: fn(e))
        return None

    def dma(self, queue, out, in_, deps=(), sem="d"):
        for d in deps:
            self._wait(queue, d)
        if sem not in self.dsems:
            self.dsems[sem] = [self.nc.alloc_semaphore(name="dma_" + sem), 0]
        s = self.dsems[sem]
        s[1] += 16
        self.streams[queue].append(lambda e, o=out, i=in_, h=s[0]: e.dma_start(out=o, in_=i).then_inc(h, 16))
        return ("dma_" + sem, s[0], s[1])

    def coll(self, in_t, out_t, deps=(), sem="cc"):
        for d in deps:
            self._wait("pool", d)
        if sem not in self.dsems:
            self.dsems[sem] = [self.nc.alloc_semaphore(name="dma_" + sem), 0]
        s = self.dsems[sem]
        s[1] += 1
        self.streams["pool"].append(lambda e, i=in_t, o=out_t, h=s[0]: e.collective_compute(
            "AllGather", ALU.bypass, replica_groups=RG, ins=[i.ap().opt()], outs=[o.ap().opt()]).then_inc(h))
        return ("dma_" + sem, s[0], s[1])

    def dtoks(self, names):
        return [("dma_" + k, self.dsems[k][0], self.dsems[k][1]) for k in names if k in self.dsems and self.dsems[k][1] > 0]

    def barrier(self):
        toks = [(e, self.psem[e], self.cnt[e]) for e in ENG if self.cnt[e] > 0]
        toks += [("dma_" + k, s[0], s[1]) for k, s in self.dsems.items() if s[1] > 0]
        for e in ENG:
            for t in toks:
                self._wait(e, t)

    def mm(self, out, lhsT, rhs, start, stop, deps=(), signal=False):
        return self.op("pe", lambda e: e.matmul(out, lhsT, rhs, start=start, stop=stop, skip_group_check=True),
                       deps, signal)

    def transpose(self, out, in_, ident, deps=(), signal=True):
        return self.op("pe", lambda e: e.transpose(out, in_, ident), deps, signal)

    def act(self, out, in_, func, deps=(), scale=None, bias=None, accum_out=None):
        kw = {}
        if scale is not None:
            kw["scale"] = scale
        if bias is not None:
            kw["bias"] = bias
        if accum_out is not None:
            kw["accum_out"] = accum_out
        return self.op("act", lambda e: e.activation(out=out, in_=in_, func=func, **kw), deps)

    def tt(self, eng, out, in0, in1, op, deps=()):
        return self.op(eng, lambda e: e.tensor_tensor(out=out, in0=in0, in1=in1, op=op), deps)

    def ts(self, eng, out, in0, s1, s2, op0, op1=None, deps=()):
        if op1 is None:
            return self.op(eng, lambda e: e.tensor_scalar(out=out, in0=in0, scalar1=s1, scalar2=None, op0=op0), deps)
        return self.op(eng, lambda e: e.tensor_scalar(out=out, in0=in0, scalar1=s1, scalar2=s2, op0=op0, op1=op1), deps)

    def stt(self, out, in0, scalar, in1, op0, op1, deps=()):
        return self.op("dve", lambda e: e.scalar_tensor_tensor(out=out, in0=in0, scalar=scalar, in1=in1, op0=op0, op1=op1), deps)

    def recip(self, out, in_, deps=()):
        return self.op("dve", lambda e: e.reciprocal(out=out, in_=in_), deps)

    def copy(self, eng, out, in_, deps=()):
        return self.op(eng, lambda e: e.tensor_copy(out=out, in_=in_), deps)

    def memset(self, ap, val, deps=()):
        return self.op("pool", lambda e: e.memset(ap, val), deps)


class Rot:
    def __init__(self, bufs):
        self.bufs = bufs
        self.free = [[] for _ in bufs]
        self.i = 0

    def next(self):
        i = self.i
        self.i = (i + 1) % len(self.bufs)
        return i, self.bufs[i], list(self.free[i])


def build_program(depth=DEPTH, debug=False):
    nc = bass.Bass("TRN2", target_bir_lowering=False)

    def din(name, shape, dt=F32):
        return nc.dram_tensor(name, list(shape), dt, kind="ExternalInput").ap()

    def dint(name, shape, dt):
        kind = "ExternalOutput" if debug else "Internal"
        return nc.dram_tensor(name, list(shape), dt, kind=kind).ap()

    xT = din("xT", [D, SL])
    ccol = din("ccol", [128, 16])
    ada_w = din("ada_w", [DEPTH, D, 6 * D])
    ada_b = din("ada_b", [128, DEPTH, 96])
    n1g = din("n1g", [128, DEPTH, 16])
    n2g = din("n2g", [128, DEPTH, 16])
    fg = din("fg", [128, 16])
    w_in = din("w_in", [DEPTH, D, DIN])
    lng = din("lng", [DEPTH, 128, 1024])
    lnb = din("lnb", [DEPTH, 128, 1024])
    wsT = din("wsT", [DEPTH, 128, 8, 128])
    bsb = din("bsb", [DEPTH, 128, 1024])
    lamv = din("lamv", [DEPTH, 128, 4, 64])
    subg = din("subg", [DEPTH, 128, 128])
    wupg = din("wupg", [DEPTH, 1024, D])
    wupa = din("wupa", [DEPTH, 1024, D])
    w_o = din("w_o", [DEPTH, D, D])
    w1 = din("w1", [DEPTH, D, FH])
    w3 = din("w3", [DEPTH, D, FH])
    w2 = din("w2", [DEPTH, FH, D])
    cident = din("cident", [128, 128])
    crt = din("crt", [128, 128])
    ccos = din("ccos", [128, SL])
    csin = din("csin", [128, SL])

    yT = nc.dram_tensor("yT", [D, SL], F32, kind="ExternalOutput").ap()
    XS = dint("XS", [D, SL], F32)
    QTd = dint("QTd", [8, 128, SL], BF16)
    KL = [[nc.dram_tensor("KL%d_%d" % (p_, c_), [256, SL], BF16) for c_ in range(4)] for p_ in range(2)]
    KA = [[nc.dram_tensor("KA%d_%d" % (p_, c_), [512, SL], BF16) for c_ in range(4)] for p_ in range(2)]
    VL = [[nc.dram_tensor("VL%d_%d" % (p_, c_), [SL, 256], BF16) for c_ in range(4)] for p_ in range(2)]
    VA = [[nc.dram_tensor("VA%d_%d" % (p_, c_), [2 * SL, 256], BF16) for c_ in range(4)] for p_ in range(2)]
    SAd = dint("SAd", [16, 128, SL], BF16)
    SBd = dint("SBd", [16, 128, SL], BF16)

    P = Prog(nc)

    ARENA = 106400
    arena = nc.alloc_sbuf_tensor("arena", [128, ARENA], BF16)
    psum = nc.alloc_psum_tensor("psum", [128, 4096], F32)

    def carve(off, nel, dt=BF16):
        a = arena[:, off:off + nel]
        if dt == F32:
            a = a.bitcast(F32)
        return a

    A0 = 0
    B0 = 32768
    W0 = B0 + 45056
    M0 = W0 + 16384

    bigA = carve(A0, 32768).rearrange("p (c t) -> p c t", t=2048)
    bigB = carve(B0, 45056).rearrange("p (c t) -> p c t", t=2048)

    def bslot(s0, ns, dt=BF16):
        return carve(B0 + s0 * 2048, ns * 2048, dt)

    wbuf = [carve(W0 + i * 8192, 8192) for i in range(2)]

    mo = [M0]

    def misc(nel_bf16, dt=BF16):
        a = carve(mo[0], nel_bf16, dt)
        mo[0] += nel_bf16
        assert mo[0] <= ARENA, mo[0]
        return a

    modAll = misc(4 * 96 * 2, F32).rearrange("p (l j) -> p l j", j=96)
    geff = misc(4 * 32 * 2, F32).rearrange("p (l j) -> p l j", j=32)
    ident = misc(128)
    RT = misc(128)
    ones = misc(128)
    cact = misc(16)
    small = misc(64 * 2, F32)
    lamt = misc(4 * 64 * 2, F32).rearrange("p (a b) -> p a b", b=64)
    lng_t = misc(1024)
    lnb_t = misc(1024)
    bs_t = misc(1024 * 2, F32)
    ws_t = misc(1024).rearrange("p (g q) -> p g q", q=128)
    subg_t = misc(128 * 2, F32)
    rstd_t = misc(512 * 2, F32)
    XT = [misc(512 * 2, F32) for _ in range(2)]
    sil = [misc(512) for _ in range(2)]
    par_f = misc(128 * 2, F32)

    ps = [psum[:, b * 512:(b + 1) * 512] for b in range(8)]

    neglam = small[:, 0:1]
    rcp = small[:, 1:2]
    ssum = small[:, 2:3]
    rr = small[:, 3:4]
    e1 = small[:, 4:5]
    e2 = small[:, 5:6]
    mv = small[:, 8:10]
    rs = small[:, 10:11]
    stats = small[:, 16:28].rearrange("p (a b) -> p a b", b=6)
    fgt = small[:, 32:48]

    P.dma("pool", ident, cident, sem="c0")
    P.dma("pool", RT, crt, sem="c0")
    P.dma("sp", par_f[:, 0:16], ccol, sem="c1")
    P.dma("sp", fgt, fg, sem="c1")
    P.dma("sp", modAll, ada_b, sem="c1")
    n1t = bslot(0, 1, F32)[:, 0:64].rearrange("p (l j) -> p l j", j=16)
    n2t = bslot(1, 1, F32)[:, 0:64].rearrange("p (l j) -> p l j", j=16)
    P.dma("sp", n1t, n1g, sem="c1")
    P.dma("sp", n2t, n2g, sem="c1")
    P.memset(ones, 1.0)
    P.barrier()
    P.act(cact, par_f[:, 0:16], AF.Silu)
    P.barrier()

    for l in range(depth):
        wfree = [[], []]
        for ci in range(24):
            slot = ci % 2
            wv = wbuf[slot].rearrange("p (k n) -> p k n", n=512)
            t_w = P.dma("pool", wv, ada_w[l][:, ci * 512:(ci + 1) * 512].rearrange("(k p) n -> p k n", p=128),
                        deps=wfree[slot], sem="w%d" % slot)
            for sc in range(4):
                j = ci * 4 + sc
                for k in range(16):
                    t = P.mm(ps[0][:, j:j + 1], wv[:, k, sc * 128:(sc + 1) * 128], cact[:, k:k + 1],
                             start=(j == 0 and k == 0), stop=(k == 15), deps=[t_w], signal=(k == 15))
            wfree[slot] = [t]
        t = P.tt("dve", modAll[:, l, :], ps[0][:, 0:96], modAll[:, l, :], ALU.add, deps=[t])
        P.stt(geff[:, l, 0:16], modAll[:, l, 16:32], 1.0, n1t[:, l, :], ALU.add, ALU.mult, deps=[t])
        P.stt(geff[:, l, 16:32], modAll[:, l, 64:80], 1.0, n2t[:, l, :], ALU.add, ALU.mult, deps=[t])
        P.barrier()

    class WStream:
        def __init__(self):
            self.free = [[], []]
            self.n = 0
            self.fifo = []

        def _issue(self, key, fn):
            slot = self.n % 2
            self.n += 1
            t = None
            for i, (dst, src) in enumerate(fn(wbuf[slot])):
                t = P.dma("pool", dst, src, deps=self.free[slot] if i == 0 else (), sem="w%d" % slot)
            return (key, slot, t)

        def prefetch(self, key, fn):
            self.fifo.append(self._issue(key, fn))

        def take(self, key, fn):
            if self.fifo:
                k, slot, t = self.fifo.pop(0)
                assert k == key, (k, key)
                return slot, t
            _, slot, t = self._issue(key, fn)
            return slot, t

        def release(self, slot, tok):
            self.free[slot] = [tok]

    W = WStream()

    def kv(src):
        return src.rearrange("(k p) n -> p k n", p=128)

    def spec_in(l, ci):
        return ("in", l, ci), lambda wb: [(wb.rearrange("p (k n) -> p k n", n=512), kv(w_in[l][:, ci * 512:(ci + 1) * 512]))]

    def spec_mg(l, ci):
        def fn(wb):
            wv = wb.rearrange("p (a k n) -> p a k n", a=2, n=512)
            return [(wv[:, 0], kv(wupg[l][:, ci * 512:(ci + 1) * 512])), (wv[:, 1], kv(wupa[l][:, ci * 512:(ci + 1) * 512]))]
        return ("mg", l, ci), fn

    def spec_wo(l, ci):
        return ("wo", l, ci), lambda wb: [(wb.rearrange("p (k n) -> p k n", n=512), kv(w_o[l][:, ci * 512:(ci + 1) * 512]))]

    def spec_w13(l, hs, jc):
        def fn(wb):
            col0 = (hs * 22 + jc * 2) * 128
            wv = wb.rearrange("p (a k n) -> p a k n", a=2, n=256)
            return [(wv[:, 0], kv(w1[l][:, col0:col0 + 256])), (wv[:, 1], kv(w3[l][:, col0:col0 + 256]))]
        return ("w13", l, hs, jc), fn

    def spec_w2(l, hs, ci):
        return ("w2", l, hs, ci), lambda wb: [(wb[:, 0:22 * 256].rearrange("p (k n) -> p k n", n=256),
                                               kv(w2[l][hs * 2816:(hs + 1) * 2816, ci * 256:(ci + 1) * 256]))]

    def prefetch2(spec_fn, *args):
        for i in range(2):
            W.prefetch(*spec_fn(*args, i))

    def norm_phase(xsrc, half, g_ap, sh_ap, final=False):
        xg_rot = Rot([bslot(12, 8, F32).rearrange("p (c t) -> p c t", t=512),
                      bslot(0, 8, F32).rearrange("p (c t) -> p c t", t=512)])
        sq_rot = Rot([bslot(20, 1)[:, i * 512:(i + 1) * 512] for i in range(4)])
        tmp_rot = Rot([bslot(21, 1, F32)[:, i * 512:(i + 1) * 512] for i in range(2)])
        out_rot = Rot(XT)
        bank_rot = Rot([ps[0], ps[1]])
        loads = {}

        def load_x(tg_):
            ixg_, xg_, xfree_ = xg_rot.next()
            t0_ = half * 2048 + tg_ * 512
            src_ = xsrc[:, t0_:t0_ + 512].rearrange("(c p) t -> p c t", p=128)
            ta_ = P.dma("sp", xg_[:, 0:8, :], src_[:, 0:8, :], deps=xfree_, sem="xg%d" % ixg_)
            tb_ = P.dma("act", xg_[:, 8:16, :], src_[:, 8:16, :], deps=xfree_, sem="xh%d" % ixg_)
            loads[tg_] = (ixg_, xg_, (ta_, tb_))

        load_x(0)
        for tg in range(4):
            tok0 = half * 2048 + tg * 512
            if tg + 1 < 4:
                load_x(tg + 1)
            ixg, xg, t_ld = loads.pop(tg)
            _, bank, bfree = bank_rot.next()
            for c in range(16):
                i, sqb, fr = sq_rot.next()
                t_sq = P.act(sqb, xg[:, c, :], AF.Square, deps=[t_ld[c // 8]] + fr)
                t_mm = P.mm(bank, ones, sqb, start=(c == 0), stop=(c == 15),
                            deps=[t_sq] + (bfree if c == 0 else []), signal=True)
                sq_rot.free[i] = [t_mm]
            t1 = P.ts("dve", rstd_t, bank, 1.0 / D, EPS, ALU.mult, ALU.add, deps=[t_mm])
            bank_rot.free[(bank_rot.i - 1) % 2] = [t1]
            t2 = P.act(rstd_t, rstd_t, AF.Sqrt, deps=[t1])
            t3 = P.recip(rstd_t, rstd_t, deps=[t2])
            last = []
            for c in range(16):
                i, tb, fr = tmp_rot.next()
                t_a = P.tt("dve", tb, xg[:, c, :], rstd_t, ALU.mult, deps=[t3] + fr)
                if not final:
                    t_b = P.act(bigA[:, c, tg * 512:(tg + 1) * 512], tb, AF.Identity, deps=[t_a],
                                scale=g_ap[:, c:c + 1], bias=sh_ap[:, c:c + 1])
                    tmp_rot.free[i] = [t_b]
                    last = [t_a, t_b]
                else:
                    io, ob, fro = out_rot.next()
                    t_b = P.act(ob, tb, AF.Identity, deps=[t_a] + fro, scale=g_ap[:, c:c + 1])
                    tmp_rot.free[i] = [t_b]
                    t_s = P.dma("sp", yT[c * 128:(c + 1) * 128, tok0:tok0 + 512], ob, deps=[t_b], sem="xo%d" % io)
                    out_rot.free[io] = [t_s]
                    last = [t_a, t_b]
            xg_rot.free[ixg] = last
        P.barrier()

    def resid_update(bank, t_mm, c, tok0, xsrc, gt_ap, xrot, bank_rot_free_cb):
        io, xt, fro = xrot.next()
        t_x = P.dma("sp", xt, xsrc[c * 128:(c + 1) * 128, tok0:tok0 + 512], deps=fro, sem="xi%d" % io)
        t_u = P.stt(xt, bank, gt_ap[:, c:c + 1], xt, ALU.mult, ALU.add, deps=[t_mm, t_x])
        bank_rot_free_cb([t_u])
        t_s = P.dma("sp", XS[c * 128:(c + 1) * 128, tok0:tok0 + 512], xt, deps=[t_u], sem="xi%d" % io)
        xrot.free[io] = [t_s]

    for l in range(depth):
        lambda_init = 0.8 - 0.6 * math.exp(-0.3 * l)
        xsrc = xT if l == 0 else XS
        sh1 = modAll[:, l, 0:16]
        gt1 = modAll[:, l, 32:48]
        sh2 = modAll[:, l, 48:64]
        gt2 = modAll[:, l, 80:96]
        g1 = geff[:, l, 0:16]
        g2 = geff[:, l, 16:32]

        par = l % 2
        if l == 0:
            prefetch2(spec_in, l)
        P.dma("pool", lng_t, lng[l], sem="c0")
        P.dma("pool", lnb_t, lnb[l], sem="c0")
        P.dma("pool", ws_t, wsT[l], sem="c0")
        P.dma("sp", bs_t, bsb[l], sem="c1")
        P.dma("sp", subg_t, subg[l], sem="c1")
        P.dma("sp", lamt, lamv[l], sem="c1")
        P.barrier()
        t = P.tt("dve", lamt[:, 0, :], lamt[:, 0, :], lamt[:, 1, :], ALU.mult)
        t = P.op("dve", lambda e: e.reduce_sum(out=e1, in_=lamt[:, 0, :], axis=AX), deps=[t])
        t = P.act(e1, e1, AF.Exp, deps=[t])
        t2 = P.tt("dve", lamt[:, 2, :], lamt[:, 2, :], lamt[:, 3, :], ALU.mult)
        t2 = P.op("dve", lambda e: e.reduce_sum(out=e2, in_=lamt[:, 2, :], axis=AX), deps=[t2])
        t2 = P.act(e2, e2, AF.Exp, deps=[t2])
        t = P.tt("dve", neglam, e2, e1, ALU.subtract, deps=[t, t2])
        t = P.ts("dve", neglam, neglam, -lambda_init, None, ALU.add, deps=[t])
        P.ts("dve", subg_t, subg_t, 1.0 - lambda_init, None, ALU.mult)
        P.barrier()

        for half in range(1):
            hb = half * 2048
            norm_phase(xsrc, half, g1, sh1)
            hT = bigA
            uT = bigB[:, 0:8, :]
            cosb = bslot(8, 1)
            sinb = bslot(10, 1)
            P.dma("pool", cosb[:, 0:SL], ccos, sem="c0")
            P.dma("pool", sinb[:, 0:SL], csin, sem="c0")
            vg = bslot(12, 1, F32)
            tmp2 = bslot(13, 1, F32)
            vn = bslot(14, 1)[:, 0:1024]
            vst_rot = Rot([bslot(14, 1)[:, 1024:2048], bslot(15, 1)[:, 0:1024]])
            stage_rot = Rot([bslot(16, 1)[:, i * 512:(i + 1) * 512] for i in range(4)])
            qb_rot = Rot([bslot(17, 1)[:, i * 512:(i + 1) * 512] for i in range(2)])
            f32_rot = Rot([bslot(18 + i // 2, 1, F32)[:, (i % 2) * 512:(i % 2 + 1) * 512] for i in range(4)])
            P.barrier()
            bank_rot = Rot([ps[0], ps[1], ps[2], ps[3]])
            pair_rot = Rot([psum[:, 2048:3072], psum[:, 3072:4096]])
            t_w = {}
            w_slot = {}

            def load_w(ci):
                w_slot[ci], t_w[ci] = W.take(*spec_in(l, ci))
                return wbuf[w_slot[ci]].rearrange("p (k n) -> p k n", n=512)

            def fm_piece(ci, kind, jbase):
                wv = load_w(ci)
                slot = w_slot[ci]
                rope_pend = []
                for sc in range(4):
                    j = ci * 4 + sc - jbase
                    for tg in range(4):
                        ib, bank, bfree = bank_rot.next()
                        tsl = slice(tg * 512, (tg + 1) * 512)
                        gsl = slice(hb + tg * 512, hb + (tg + 1) * 512)
                        for k in range(16):
                            t_mm = P.mm(bank, wv[:, k, sc * 128:(sc + 1) * 128], hT[:, k, tsl], start=(k == 0),
                                        stop=(k == 15), deps=([t_w[ci]] + bfree) if k == 0 else (), signal=(k == 15))
                        if kind == "u":
                            t = P.act(uT[:, j, tsl], bank, AF.Gelu_apprx_tanh, deps=[t_mm])
                            bank_rot.free[ib] = [t]
                        elif kind in ("ga", "gb"):
                            i4, st, frs = stage_rot.next()
                            t = P.act(st, bank, AF.Sigmoid, deps=[t_mm] + frs)
                            bank_rot.free[ib] = [t]
                            dst = SAd if kind == "ga" else SBd
                            t_s = P.dma("sp", dst[j][:, gsl], st, deps=[t], sem="st%d" % i4)
                            stage_rot.free[i4] = [t_s]
                        else:
                            iq, qb_, frq = qb_rot.next()
                            t_c = P.act(qb_, bank, AF.Copy, deps=[t_mm] + frq)
                            bank_rot.free[ib] = [t_c]

                            def rope_part(iq=iq, qb_=qb_, t_c=t_c, j=j, gsl=gsl):
                                ib2, bank2, bfree2 = bank_rot.next()
                                t_r = P.mm(bank2, RT, qb_, start=True, stop=True, deps=[t_c] + bfree2, signal=True)
                                ia, ta, fra = f32_rot.next()
                                t_a = P.tt("dve", ta, qb_, cosb[:, gsl], ALU.mult, deps=[t_c] + fra)
                                ibb, tb_, frb = f32_rot.next()
                                t_b = P.tt("dve", tb_, bank2, sinb[:, gsl], ALU.mult, deps=[t_r] + frb)
                                bank_rot.free[ib2] = [t_b]
                                qb_rot.free[iq] = [t_r, t_a]
                                i4, st, frs = stage_rot.next()
                                t_o = P.tt("dve", st, ta, tb_, ALU.add, deps=[t_a, t_b] + frs)
                                f32_rot.free[ia] = [t_o]
                                f32_rot.free[ibb] = [t_o]
                                if kind == "q":
                                    dst_ap = QTd[j][:, gsl]
                                else:
                                    dst_ap = KL[par][j // 2][(j % 2) * 128:(j % 2 + 1) * 128, gsl]
                                t_s = P.dma("sp", dst_ap, st, deps=[t_o], sem="st%d" % i4)
                                stage_rot.free[i4] = [t_s]

                            while rope_pend:
                                rope_pend.pop(0)()
                            rope_pend.append(rope_part)
                while rope_pend:
                    rope_pend.pop(0)()
                W.release(slot, t_mm)

            def tm_v(ci0):
                wvs = [load_w(ci0), load_w(ci0 + 1)]
                sets = [(bslot(12, 1, F32), bslot(13, 1, F32), bslot(14, 1)[:, 0:1024]),
                        (bslot(9, 1, F32), bslot(11, 1, F32), bslot(20, 1)[:, 0:1024])]
                set_free = [[], []]
                mainb, spb = pair_rot.bufs[0], pair_rot.bufs[1]
                fr = {"main": list(pair_rot.free[0]), "sp": list(pair_rot.free[1])}
                t_mm = None

                def sp_part(tt_, tmp2_, vn_, t_n_):
                    tsl_ = slice(tt_ * 128, (tt_ + 1) * 128)
                    for g in range(8):
                        t_sp = P.mm(spb[:, g * 128:(g + 1) * 128], vn_[:, g * 128:(g + 1) * 128], ws_t[:, g, :],
                                    start=True, stop=True, deps=([t_n_] + fr["sp"]) if g == 0 else (), signal=(g == 7))
                    t_m = P.tt("dve", tmp2_, spb, bs_t, ALU.add, deps=[t_sp])
                    fr["sp"] = [t_m]
                    t_u = P.tt("dve", uT[:, :, tsl_], tmp2_.rearrange("p (g q) -> p g q", q=128), uT[:, :, tsl_],
                               ALU.mult, deps=[t_m])
                    set_free[tt_ % 2] = [t_u]

                pend = None
                for tt_ in range(16):
                    vg_, tmp2_, vn_ = sets[tt_ % 2]
                    tsl = slice(tt_ * 128, (tt_ + 1) * 128)
                    for cc in range(2):
                        for k in range(16):
                            first = (cc == 0 and k == 0)
                            t_mm = P.mm(mainb[:, cc * 512:(cc + 1) * 512], hT[:, k, tsl], wvs[cc][:, k, :],
                                        start=(k == 0), stop=(k == 15),
                                        deps=([t_w[ci0], t_w[ci0 + 1]] + fr["main"]) if first else (),
                                        signal=(cc == 1 and k == 15))
                    t_g = P.act(vg_, mainb, AF.Gelu_apprx_tanh, deps=[t_mm] + set_free[tt_ % 2])
                    fr["main"] = [t_g]
                    ta_ = P.op("dve", lambda e, v=vg_: e.bn_stats(out=stats[:, 0, :], in_=v[:, 0:512]), deps=[t_g])
                    tb2 = P.op("dve", lambda e, v=vg_: e.bn_stats(out=stats[:, 1, :], in_=v[:, 512:1024]), deps=[t_g])
                    t_ag = P.op("dve", lambda e: e.bn_aggr(out=mv, in_=small[:, 16:28]), deps=[ta_, tb2])
                    t_r1 = P.ts("dve", rs, mv[:, 1:2], EPS, None, ALU.add, deps=[t_ag])
                    t_r2 = P.act(rs, rs, AF.Sqrt, deps=[t_r1])
                    t_r3 = P.recip(rs, rs, deps=[t_r2])
                    t_n = P.ts("dve", vg_, vg_, mv[:, 0:1], rs, ALU.subtract, ALU.mult, deps=[t_r3])
                    t_n = P.tt("dve", vg_, vg_, lng_t, ALU.mult, deps=[t_n])
                    t_n = P.tt("dve", vn_, vg_, lnb_t, ALU.add, deps=[t_n])
                    if pend is not None:
                        sp_part(*pend)
                    pend = (tt_, tmp2_, vn_, t_n)
                sp_part(*pend)
                pair_rot.free[0] = fr["main"]
                pair_rot.free[1] = fr["sp"]
                pair_rot.i = 0
                W.release(w_slot[ci0], t_mm)
                W.release(w_slot[ci0 + 1], t_mm)

            def tm_piece(ci0, kind):
                if kind == "v":
                    return tm_v(ci0)
                wv0 = load_w(ci0)
                wv1 = load_w(ci0 + 1)
                wvs = [wv0, wv1]
                t_u_last = None
                for tt_ in range(16):
                    ip, pair, pfree = pair_rot.next()
                    tsl = slice(tt_ * 128, (tt_ + 1) * 128)
                    for cc in range(2):
                        for k in range(16):
                            first = (cc == 0 and k == 0)
                            t_mm = P.mm(pair[:, cc * 512:(cc + 1) * 512], hT[:, k, tsl], wvs[cc][:, k, :],
                                        start=(k == 0), stop=(k == 15),
                                        deps=([t_w[ci0], t_w[ci0 + 1]] + pfree) if first else (),
                                        signal=(cc == 1 and k == 15))
                    if kind == "va":
                        iv, vs, frv = vst_rot.next()
                        t = P.act(vs, pair, AF.Copy, deps=[t_mm] + frv)
                        pair_rot.free[ip] = [t]
                        for c_ in range(4):
                            t_s = P.dma("sp", VL[par][c_][tt_ * 128:(tt_ + 1) * 128, :], vs[:, c_ * 256:(c_ + 1) * 256],
                                        deps=[t], sem="vs%d" % iv)
                        vst_rot.free[iv] = [t_s]
                    else:
                        t_g = P.act(vg, pair, AF.Gelu_apprx_tanh, deps=[t_mm] + ([t_u_last] if t_u_last else []))
                        pair_rot.free[ip] = [t_g]
                        ta_ = P.op("dve", lambda e: e.bn_stats(out=stats[:, 0, :], in_=vg[:, 0:512]), deps=[t_g])
                        tb2 = P.op("dve", lambda e: e.bn_stats(out=stats[:, 1, :], in_=vg[:, 512:1024]), deps=[t_g])
                        t_ag = P.op("dve", lambda e: e.bn_aggr(out=mv, in_=small[:, 16:28]), deps=[ta_, tb2])
                        t_r1 = P.ts("dve", rs, mv[:, 1:2], EPS, None, ALU.add, deps=[t_ag])
                        t_r2 = P.act(rs, rs, AF.Sqrt, deps=[t_r1])
                        t_r3 = P.recip(rs, rs, deps=[t_r2])
                        t_n = P.ts("dve", vg, vg, mv[:, 0:1], rs, ALU.subtract, ALU.mult, deps=[t_r3])
                        t_n = P.tt("dve", vg, vg, lng_t, ALU.mult, deps=[t_n])
                        t_n = P.tt("dve", vn, vg, lnb_t, ALU.add, deps=[t_n] + ([t_u_last] if t_u_last else []))
                        ip2, pair2, pfree2 = pair_rot.next()
                        for g in range(8):
                            t_sp = P.mm(pair2[:, g * 128:(g + 1) * 128], vn[:, g * 128:(g + 1) * 128], ws_t[:, g, :],
                                        start=True, stop=True, deps=([t_n] + pfree2) if g == 0 else (), signal=(g == 7))
                        t_m = P.tt("dve", tmp2, pair2, bs_t, ALU.add, deps=[t_sp])
                        pair_rot.free[ip2] = [t_m]
                        t_u_last = P.tt("dve", uT[:, :, tsl], tmp2.rearrange("p (g q) -> p g q", q=128), uT[:, :, tsl],
                                        ALU.mult, deps=[t_m])
                W.release(w_slot[ci0], t_mm)
                W.release(w_slot[ci0 + 1], t_mm)

            fm_piece(0, "u", 0)
            fm_piece(1, "u", 0)
            tm_piece(2, "v")
            fm_piece(4, "q", 16)
            fm_piece(5, "q", 16)
            fm_piece(6, "k", 24)
            fm_piece(7, "k", 24)
            tm_piece(8, "va")
            for ci in range(10, 14):
                if ci == 10:
                    kdeps = P.dtoks(["st0", "st1", "st2", "st3"])
                    for c_ in range(4):
                        P.coll(KL[par][c_], KA[par][c_], deps=kdeps, sem="cc")
                if ci == 12:
                    vdeps = P.dtoks(["vs0", "vs1"])
                    for c_ in range(4):
                        P.coll(VL[par][c_], VA[par][c_], deps=vdeps, sem="cc")
                fm_piece(ci, "ga", 40)
            for ci in range(14, 18):
                fm_piece(ci, "gb", 56)
            P.barrier()

        def aslot(s0, ns, dt=BF16):
            return carve(A0 + s0 * 2048, ns * 2048, dt)

        KTa = [aslot(0, 2), aslot(2, 2)]
        KTb = [aslot(4, 2), aslot(6, 2)]
        QTb = [aslot(8, 1), aslot(9, 1)]
        Vau = [aslot(10, 3)[:, 0:32 * 129].rearrange("p (k d) -> p k d", d=129),
               aslot(13, 3)[:, 0:32 * 129].rearrange("p (k d) -> p k d", d=129)]
        PT_rot = Rot([bslot(18, 1)[:, i * 512:(i + 1) * 512] for i in range(4)])
        onorm = [[bslot(19, 1, F32)[:, (m * 4 + j) * 128:(m * 4 + j + 1) * 128] for j in range(4)] for m in range(2)]
        onorm_all = [bslot(19, 1, F32)[:, m * 512:(m + 1) * 512] for m in range(2)]
        ocomb4 = bslot(20, 1, F32)[:, 0:512]
        junk4 = bslot(20, 1, F32)[:, 512:1024]
        obf4 = bslot(21, 1)[:, 0:512]
        rcp4 = small[:, 48:52]
        ssum4 = small[:, 52:56]
        rr4 = small[:, 56:60]
        brB_w = bigB[:, 8:16, :]
        prefetch2(spec_mg, l)
        for b in range(2):
            P.memset(KTa[b][64:128, :], 0.0)
            P.memset(KTb[b][0:64, :], 0.0)
            P.memset(Vau[b][:, :, 128:129], 1.0)
        P.barrier()
        S_rot = Rot([ps[0], ps[1], ps[2]])
        O_rot = Rot([psum[:, 1536:2560], psum[:, 2560:3584]])
        psT = ps[7].bitcast(BF16)
        psT_free = []
        hfree = [[], []]
        on_free = [[], []]
        ep = {"oc": [], "rr": [], "obf": [], "psT": [], "t_rr": None}
        pending = []

        def run_pending(kc):
            keep = []
            for trig, fn in pending:
                if trig <= kc:
                    fn()
                else:
                    keep.append((trig, fn))
            pending[:] = keep

        for h in range(8):
            buf = h % 2
            hc, hl = h // 2, h % 2
            tl = []
            for r_ in range(2):
                r0 = r_ * 256 + hl * 128
                tl.append(P.dma("sp", KTa[buf][0:64, r_ * SL:(r_ + 1) * SL], KA[par][hc][r0:r0 + 64, :],
                                deps=hfree[buf] if r_ == 0 else (), sem="hk%d" % buf))
                tl.append(P.dma("sp", KTb[buf][64:128, r_ * SL:(r_ + 1) * SL], KA[par][hc][r0 + 64:r0 + 128, :],
                                sem="hk%d" % buf))
            tl.append(P.dma("sp", QTb[buf], QTd[h], sem="hk%d" % buf))
            for r_ in range(2):
                tl.append(P.dma("sp", Vau[buf][:, r_ * 16:(r_ + 1) * 16, 0:128],
                                VA[par][hc][r_ * SL:(r_ + 1) * SL, hl * 128:(hl + 1) * 128].rearrange(
                                    "(k p) d -> p k d", p=128), sem="hk%d" % buf))
            t_ld = tl[-1]
            last_pe = None
            for qg in range(SL // 512):
                qsl = slice(qg * 512, (qg + 1) * 512)
                t_on_m = [None, None]
                for m in range(2):
                    KTm = KTa[buf] if m == 0 else KTb[buf]
                    io, Ob, ofree = O_rot.next()
                    Oj = [Ob[:, (j // 2) * 512 + (j % 2) * 129:(j // 2) * 512 + (j % 2) * 129 + 129] for j in range(4)]
                    sinfo = {}

                    def issue_S(kc):
                        ib, bank, bfree = S_rot.next()
                        t = P.mm(bank, KTm[:, kc * 128:(kc + 1) * 128], QTb[buf][:, qsl], start=True, stop=True,
                                 deps=[t_ld] + bfree, signal=True)
                        sinfo[kc] = (ib, bank, t)

                    issue_S(0)
                    issue_S(1)
                    for kc in range(32):
                        ib, bank, t_S = sinfo.pop(kc)
                        ipt, pt, frp = PT_rot.next()
                        t_e = P.act(pt, bank, AF.Exp, deps=[t_S] + frp, scale=0.125)
                        S_rot.free[ib] = [t_e]
                        if kc + 2 < 32:
                            issue_S(kc + 2)
                        run_pending(kc)
                        for j in range(4):
                            t_pv = P.mm(Oj[j], pt[:, j * 128:(j + 1) * 128], Vau[buf][:, kc, :],
                                        start=(kc == 0 and j % 2 == 0), stop=(kc == 31),
                                        deps=([t_e] + (ofree if kc == 0 else [])) if j == 0 else (), signal=(j == 3))
                        PT_rot.free[ipt] = [t_pv]
                    last_pe = t_pv
                    assert not pending
                    t_on = None
                    for j in range(4):
                        t_r = P.recip(rcp4[:, j:j + 1], Oj[j][:, 128:129], deps=[t_pv] + on_free[m])
                        t_on = P.ts("dve", onorm[m][j], Oj[j][:, 0:128], rcp4[:, j:j + 1], None, ALU.mult, deps=[t_r])
                    O_rot.free[io] = [t_on]
                    t_on_m[m] = t_on
                    if m == 1:
                        t_c = P.stt(ocomb4, onorm_all[1], neglam, onorm_all[0], ALU.mult, ALU.add,
                                    deps=[t_on_m[0], t_on_m[1]] + ep["oc"])
                        on_free[0] = [t_c]
                        on_free[1] = [t_c]
                        t_q = P.tt("dve", junk4, ocomb4, ocomb4, ALU.mult, deps=[t_c])
                        for j in range(4):
                            t_q = P.op("dve", lambda e, j=j: e.reduce_sum(out=ssum4[:, j:j + 1],
                                                                          in_=junk4[:, j * 128:(j + 1) * 128], axis=AX),
                                       deps=[t_q])
                        t_q = P.ts("dve", rr4, ssum4, 1.0 / 128, EPS, ALU.mult, ALU.add, deps=[t_q] + ep["rr"])

                        def epi_act(t_q=t_q):
                            t1 = P.act(rr4, rr4, AF.Ln, deps=[t_q])
                            ep["t_rr"] = P.act(rr4, rr4, AF.Exp, deps=[t1], scale=-0.5)

                        def epi_pe(h=h, qsl=qsl):
                            t_ob = None
                            for j in range(4):
                                t_ob = P.stt(obf4[:, j * 128:(j + 1) * 128], ocomb4[:, j * 128:(j + 1) * 128],
                                             rr4[:, j:j + 1], subg_t, ALU.mult, ALU.mult,
                                             deps=([ep["t_rr"]] + ep["obf"]) if j == 0 else ())
                            ep["rr"] = [t_ob]
                            ep["oc"] = [t_ob]
                            t_tr = None
                            for j in range(4):
                                t_tr = P.transpose(psT[:, j * 128:(j + 1) * 128], obf4[:, j * 128:(j + 1) * 128], ident,
                                                   deps=([t_ob] + ep["psT"]) if j == 0 else ())
                            ep["obf"] = [t_tr]
                            t_cp = P.copy("dve", brB_w[:, h, qsl], psT[:, 0:512], deps=[t_tr])
                            ep["psT"] = [t_cp]

                        pending.append((8, epi_act))
                        pending.append((12, epi_pe))
            hfree[buf] = [last_pe]
        run_pending(99)
        P.barrier()

        for half in range(1):
            hb = half * 2048
            brA = bigB[:, 0:8, :]
            brB = bigB[:, 8:16, :]
            merged = bigA
            sa_rot = Rot([bslot(16, 1), bslot(17, 1)])
            sb_rot = Rot([bslot(18, 1), bslot(19, 1)])
            t1_rot = Rot([bslot(20, 1, F32)[:, i * 512:(i + 1) * 512] for i in range(2)])
            t2_rot = Rot([bslot(21, 1, F32)[:, i * 512:(i + 1) * 512] for i in range(2)])
            bank_rot = Rot([ps[0], ps[1], ps[2], ps[3], ps[4], ps[5]])
            for ci in range(4):
                slot, t_w = W.take(*spec_mg(l, ci))
                wv = wbuf[slot].rearrange("p (a k n) -> p a k n", a=2, n=512)
                for sc in range(4):
                    c = ci * 4 + sc
                    isa, sat, frsa = sa_rot.next()
                    isb, sbt, frsb = sb_rot.next()
                    t_sa = P.dma("sp", sat, SAd[c][:, hb:hb + 2048], deps=frsa, sem="sa%d" % isa)
                    t_sb = P.dma("sp", sbt, SBd[c][:, hb:hb + 2048], deps=frsb, sem="sb%d" % isb)
                    for tg in range(4):
                        tsl = slice(tg * 512, (tg + 1) * 512)
                        ia_, bka, fa = bank_rot.next()
                        for k in range(8):
                            t_ma = P.mm(bka, wv[:, 0, k, sc * 128:(sc + 1) * 128], brA[:, k, tsl], start=(k == 0),
                                        stop=(k == 7), deps=([t_w] + fa) if k == 0 else (), signal=(k == 7))
                        ib_, bkb, fb = bank_rot.next()
                        for k in range(8):
                            t_mb = P.mm(bkb, wv[:, 1, k, sc * 128:(sc + 1) * 128], brB[:, k, tsl], start=(k == 0),
                                        stop=(k == 7), deps=fb if k == 0 else (), signal=(k == 7))
                        i1, t1b, f1 = t1_rot.next()
                        i2, t2b, f2 = t2_rot.next()
                        ta_ = P.tt("dve", t1b, bka, sat[:, tsl], ALU.mult, deps=[t_ma, t_sa] + f1)
                        tb2 = P.tt("dve", t2b, bkb, sbt[:, tsl], ALU.mult, deps=[t_mb, t_sb] + f2)
                        bank_rot.free[ia_] = [ta_]
                        bank_rot.free[ib_] = [tb2]
                        t_m = P.tt("pool", merged[:, c, tsl], t1b, t2b, ALU.add, deps=[ta_, tb2])
                        t1_rot.free[i1] = [t_m]
                        t2_rot.free[i2] = [t_m]
                    sa_rot.free[isa] = [ta_]
                    sb_rot.free[isb] = [tb2]
                W.release(slot, t_mb)
            prefetch2(spec_wo, l)
            P.barrier()
            xrot = Rot(XT + [bslot(i // 2, 1, F32)[:, (i % 2) * 512:(i % 2 + 1) * 512] for i in range(4)])
            for ci in range(4):
                slot, t_w = W.take(*spec_wo(l, ci))
                wv = wbuf[slot].rearrange("p (k n) -> p k n", n=512)
                for sc in range(4):
                    c = ci * 4 + sc
                    for tg in range(4):
                        tsl = slice(tg * 512, (tg + 1) * 512)
                        ib, bank, bfree = bank_rot.next()
                        for k in range(16):
                            t_mm = P.mm(bank, wv[:, k, sc * 128:(sc + 1) * 128], merged[:, k, tsl], start=(k == 0),
                                        stop=(k == 15), deps=([t_w] + bfree) if k == 0 else (), signal=(k == 15))

                        def cb(toks, ib=ib):
                            bank_rot.free[ib] = toks
                        resid_update(bank, t_mm, c, hb + tg * 512, xsrc, gt1, xrot, cb)
                W.release(slot, t_mm)
            prefetch2(spec_w13, l, 0)
            P.barrier()

        for half in range(1):
            hb = half * 2048
            norm_phase(XS, half, g2, sh2)
            hT = bigA
            hid = bigB
            bank_rot = Rot([ps[0], ps[1], ps[2], ps[3], ps[4], ps[5]])
            sil_rot = Rot(sil)
            xrot = Rot(XT)
            for hs in range(2):
                for jc in range(11):
                    slot, t_w = W.take(*spec_w13(l, hs, jc))
                    wv = wbuf[slot].rearrange("p (a k n) -> p a k n", a=2, n=256)
                    for sc in range(2):
                        jj = jc * 2 + sc
                        for tg in range(4):
                            tsl = slice(tg * 512, (tg + 1) * 512)
                            i1, b1, f1 = bank_rot.next()
                            for k in range(16):
                                t_m1 = P.mm(b1, wv[:, 0, k, sc * 128:(sc + 1) * 128], hT[:, k, tsl], start=(k == 0),
                                            stop=(k == 15), deps=([t_w] + f1) if k == 0 else (), signal=(k == 15))
                            i3, b3, f3 = bank_rot.next()
                            for k in range(16):
                                t_m3 = P.mm(b3, wv[:, 1, k, sc * 128:(sc + 1) * 128], hT[:, k, tsl], start=(k == 0),
                                            stop=(k == 15), deps=f3 if k == 0 else (), signal=(k == 15))
                            isl, sb_, fs = sil_rot.next()
                            t_s = P.act(sb_, b1, AF.Silu, deps=[t_m1] + fs)
                            bank_rot.free[i1] = [t_s]
                            t_h = P.tt("dve", hid[:, jj, tsl], b3, sb_, ALU.mult, deps=[t_m3, t_s])
                            bank_rot.free[i3] = [t_h]
                            sil_rot.free[isl] = [t_h]
                    W.release(slot, t_m3)
                prefetch2(spec_w2, l, hs)
                P.barrier()
                for ci in range(8):
                    slot, t_w = W.take(*spec_w2(l, hs, ci))
                    wv = wbuf[slot][:, 0:22 * 256].rearrange("p (k n) -> p k n", n=256)
                    for sc in range(2):
                        c = ci * 2 + sc
                        for tg in range(4):
                            tsl = slice(tg * 512, (tg + 1) * 512)
                            ib, bank, bfree = bank_rot.next()
                            for k in range(22):
                                t_mm = P.mm(bank, wv[:, k, sc * 128:(sc + 1) * 128], hid[:, k, tsl], start=(k == 0),
                                            stop=(k == 21), deps=([t_w] + bfree) if k == 0 else (), signal=(k == 21))

                            def cb(toks, ib=ib):
                                bank_rot.free[ib] = toks
                            resid_update(bank, t_mm, c, hb + tg * 512, XS, gt2, xrot, cb)
                    W.release(slot, t_mm)
                if hs == 0:
                    prefetch2(spec_w13, l, 1)
                elif l + 1 < depth:
                    prefetch2(spec_in, l + 1)
                P.barrier()

    for half in range(1):
        norm_phase(XS, half, fgt, None, final=True)
    P.barrier()

    with nc.Block() as block:
        @block.tensor
        def _(e):
            for f in P.streams["pe"]:
                f(e)

        @block.scalar
        def _(e):
            for f in P.streams["act"]:
                f(e)

        @block.vector
        def _(e):
            for f in P.streams["dve"]:
                f(e)

        @block.gpsimd
        def _(e):
            for f in P.streams["pool"]:
                f(e)

        @block.sync
        def _(e):
            for f in P.streams["sp"]:
                f(e)
    return nc


def rope_consts():
    j = np.arange(32, dtype=np.float32)
    inv_freq = (10000.0 ** (-(2.0 * j) / 64.0)).astype(np.float32)
    pos = np.arange(S, dtype=np.float32)
    ang = pos[None, :] * inv_freq[:, None]
    cos = np.cos(ang).astype(np.float32)
    sin = np.sin(ang).astype(np.float32)
    cosT = np.tile(cos, (4, 1))
    sinT = np.tile(sin, (4, 1))
    rt = np.zeros((128, 128), np.float32)
    for m in range(2):
        for jj in range(32):
            i0 = m * 64 + jj
            i1 = m * 64 + 32 + jj
            rt[i1, i0] = -1.0
            rt[i0, i1] = 1.0
    return cosT, sinT, rt, np.eye(128, dtype=np.float32)


def make_in_maps(x, c, ada_w, ada_b, norm1_g, norm2_g, w_in, gmlp_ln_g, gmlp_ln_b, w_s, b_s,
                 lambda_q1, lambda_k1, lambda_q2, lambda_k2, subln_g, w_up_gmlp, w_up_attn, w_o,
                 ffn_w1, ffn_w3, ffn_w2, final_g):
    f = lambda a: np.ascontiguousarray(np.asarray(a, dtype=np.float32))
    cosT, sinT, rt, ident = rope_consts()
    L = DEPTH
    shared = {
        "ada_w": f(ada_w),
        "ada_b": f(np.asarray(ada_b).reshape(L, 96, 128).transpose(2, 0, 1)),
        "n1g": f(np.asarray(norm1_g).reshape(L, 16, 128).transpose(2, 0, 1)),
        "n2g": f(np.asarray(norm2_g).reshape(L, 16, 128).transpose(2, 0, 1)),
        "fg": f(np.asarray(final_g).reshape(16, 128).T),
        "w_in": f(w_in),
        "lng": f(np.broadcast_to(np.asarray(gmlp_ln_g)[:, None, :], (L, 128, 1024))),
        "lnb": f(np.broadcast_to(np.asarray(gmlp_ln_b)[:, None, :], (L, 128, 1024))),
        "wsT": f(np.asarray(w_s).transpose(0, 3, 1, 2)),
        "bsb": f(np.broadcast_to(np.asarray(b_s).reshape(L, 1, 1024), (L, 128, 1024))),
        "lamv": f(np.broadcast_to(np.stack([np.asarray(lambda_q1), np.asarray(lambda_k1), np.asarray(lambda_q2),
                                            np.asarray(lambda_k2)], axis=1)[:, None], (L, 128, 4, 64))),
        "subg": f(np.broadcast_to(np.asarray(subln_g)[:, None, :], (L, 128, 128))),
        "wupg": f(w_up_gmlp), "wupa": f(w_up_attn), "w_o": f(w_o),
        "w1": f(ffn_w1), "w3": f(ffn_w3), "w2": f(ffn_w2),
        "cident": ident, "crt": rt,
    }
    x = np.asarray(x, dtype=np.float32)
    c = np.asarray(c, dtype=np.float32)
    in_maps = []
    for core in range(NCORES):
        b, r = core // 2, core % 2
        m = dict(shared)
        m["xT"] = np.ascontiguousarray(x[b, r * SL:(r + 1) * SL, :].T)
        m["ccol"] = np.ascontiguousarray(c[b].reshape(16, 128).T)
        m["ccos"] = np.ascontiguousarray(cosT[:, r * SL:(r + 1) * SL])
        m["csin"] = np.ascontiguousarray(sinT[:, r * SL:(r + 1) * SL])
        in_maps.append(m)
    return in_maps


def kernel(**inputs):
    in_maps = make_in_maps(**inputs)
    nc = build_program()
    res = run_bass_kernel_spmd(nc, in_maps, core_ids=list(range(NCORES)))
    out = np.empty((NCORES // 2, S, D), np.float32)
    for core in range(NCORES):
        b, r = core // 2, core % 2
        out[b, r * SL:(r + 1) * SL, :] = np.asarray(res.results[core]["yT"]).T
    return out
```
